# Optimizing a Trainium2 kernel written in Bass

```python
import math
import jax, jax.numpy as jnp
from jax import lax
import numpy as np

D_MODEL = 2048
BATCH = 2
SEQ = 4096
DEPTH = 1
DEC_BATCH = 8
DEC_SEQ = 16
PAST_LEN = 2048

CHUNK = 64
EPS = 1e-6
D_FF = 4 * D_MODEL
SSD_EXPAND = 2
D_INNER = SSD_EXPAND * D_MODEL
SSD_HEADDIM = 64
SSD_HEADS = D_INNER // SSD_HEADDIM
SSD_GROUPS = 8
SSD_STATE = 128
SSD_CONV = 4
XBC_DIM = D_INNER + 2 * SSD_GROUPS * SSD_STATE
D_CONV = D_MODEL
CM_WIDTH = 31
N_MEM = 256
XA_HEADS = 4
XA_HEAD_DIM = D_MODEL // XA_HEADS
N_BRANCH = 2
IN_COLS = D_INNER + XBC_DIM + SSD_HEADS + 2 * D_CONV + N_BRANCH * D_MODEL

kernel_name = "hybrid_ssd_conformer_stream_step"


def rmsnorm(x, g):
    xf = x.astype(jnp.float32)
    y = xf * lax.rsqrt(jnp.mean(xf * xf, axis=-1, keepdims=True) + EPS)
    return (y * g.astype(jnp.float32)).astype(x.dtype)


def group_rmsnorm(x, g, groups):
    shp = x.shape
    xf = x.astype(jnp.float32).reshape(shp[:-1] + (groups, shp[-1] // groups))
    y = xf * lax.rsqrt(jnp.mean(xf * xf, axis=-1, keepdims=True) + EPS)
    return (y.reshape(shp) * g.astype(jnp.float32)).astype(x.dtype)


def layernorm(x, g, b):
    xf = x.astype(jnp.float32)
    mu = jnp.mean(xf, axis=-1, keepdims=True)
    var = jnp.mean(jnp.square(xf - mu), axis=-1, keepdims=True)
    y = (xf - mu) * lax.rsqrt(var + EPS)
    return (y * g.astype(jnp.float32) + b.astype(jnp.float32)).astype(x.dtype)


def swiglu_ffn(h, wi, wo):
    a, b = jnp.split(h @ wi, 2, axis=-1)
    return (jax.nn.silu(a) * b) @ wo


def causal_dwconv(u, prev, w, bias):
    k = w.shape[0]
    full = jnp.concatenate([prev.astype(u.dtype), u], axis=1)
    y = lax.conv_general_dilated(full, w.astype(u.dtype)[:, None, :], window_strides=(1,), padding="VALID",
                                 dimension_numbers=("NWC", "WIO", "NWC"), feature_group_count=u.shape[-1])
    return y + bias, full[:, full.shape[1] - (k - 1):]


def ssd_scan(x, dA, Bm, Cm, h0):
    b, seq, nh, hp = x.shape
    ng, ns = Bm.shape[2], Bm.shape[3]
    nr = nh // ng
    t = min(CHUNK, seq)
    nc = seq // t
    f32 = jnp.float32
    xc = x.astype(f32).reshape(b, nc, t, ng, nr, hp)
    Bc = Bm.astype(f32).reshape(b, nc, t, ng, ns)
    Cc = Cm.astype(f32).reshape(b, nc, t, ng, ns)
    A = jnp.transpose(dA.astype(f32).reshape(b, nc, t, ng, nr), (0, 3, 4, 1, 2))
    a_cs = jnp.cumsum(A, axis=-1)
    mask = jnp.tril(jnp.ones((t, t), dtype=bool))
    seg = a_cs[..., :, None] - a_cs[..., None, :]
    lmat = jnp.exp(jnp.where(mask, seg, -jnp.inf))
    y_diag = jnp.einsum("bclgn,bcsgn,bgrcls,bcsgrp->bclgrp", Cc, Bc, lmat, xc)
    decay_states = jnp.exp(a_cs[..., -1:] - a_cs)
    states = jnp.einsum("bcsgn,bgrcs,bcsgrp->cbgrpn", Bc, decay_states, xc)
    chunk_decay = jnp.moveaxis(jnp.exp(a_cs[..., -1]), -1, 0)

    def step(h, inp):
        dec, st = inp
        return h * dec[..., None, None] + st, h

    h_init = h0.astype(f32).reshape(b, ng, nr, hp, ns)
    h_fin, h_prev = lax.scan(step, h_init, (chunk_decay, states))
    y_off = jnp.einsum("bclgn,cbgrpn,bgrcl->bclgrp", Cc, h_prev, jnp.exp(a_cs))
    y = (y_diag + y_off).reshape(b, seq, nh, hp).astype(x.dtype)
    return y, h_fin.reshape(b, nh, hp, ns).astype(h0.dtype)


def parallel_mixer(h, ssd_prev, ssm_h0, cm_prev, w_in, ssd_conv_w, ssd_conv_b, ssd_dt_bias, ssd_a_log,
                   ssd_d, ssd_norm, ssd_w_out, cm_dw_w, cm_dw_b, cm_ln_g, cm_ln_b, cm_w_out, w_mix_out):
    b, seq = h.shape[0], h.shape[1]
    proj = h @ w_in
    i1 = D_INNER
    i2 = i1 + XBC_DIM
    i3 = i2 + SSD_HEADS
    i4 = i3 + D_CONV
    i5 = i4 + D_CONV
    z, xbc, dt, cm_val, cm_gate, gates = jnp.split(proj, [i1, i2, i3, i4, i5], axis=-1)
    xbc, new_ssd_conv = causal_dwconv(xbc, ssd_prev, ssd_conv_w, ssd_conv_b)
    xbc = jax.nn.silu(xbc)
    xs, Bm, Cm = jnp.split(xbc, [D_INNER, D_INNER + SSD_GROUPS * SSD_STATE], axis=-1)
    xs = xs.reshape(b, seq, SSD_HEADS, SSD_HEADDIM)
    Bm = Bm.reshape(b, seq, SSD_GROUPS, SSD_STATE)
    Cm = Cm.reshape(b, seq, SSD_GROUPS, SSD_STATE)
    dtp = jax.nn.softplus(dt.astype(jnp.float32) + ssd_dt_bias.astype(jnp.float32))
    a = -jnp.exp(ssd_a_log.astype(jnp.float32))
    y, new_h = ssd_scan(xs * dtp[..., None].astype(xs.dtype), dtp * a, Bm, Cm, ssm_h0)
    y = (y + ssd_d[:, None] * xs).reshape(b, seq, D_INNER)
    y = group_rmsnorm(y * jax.nn.silu(z), ssd_norm, SSD_GROUPS)
    ssd_out = y @ ssd_w_out
    u = cm_val * jax.nn.sigmoid(cm_gate)
    u, new_cm_conv = causal_dwconv(u, cm_prev, cm_dw_w, cm_dw_b)
    u = jax.nn.silu(layernorm(u, cm_ln_g, cm_ln_b))
    cm_out = u @ cm_w_out
    g_ssd, g_cm = jnp.split(gates, 2, axis=-1)
    m = jax.nn.sigmoid(g_ssd) * ssd_out + jax.nn.sigmoid(g_cm) * cm_out
    return m @ w_mix_out, new_ssd_conv, new_h, new_cm_conv


def mem_kv(mem, g, w):
    m = rmsnorm(mem, g)
    return (m @ w).reshape(mem.shape[0], mem.shape[1], XA_HEADS, XA_HEAD_DIM)


def mem_attend(h, k, v, wq, wo):
    b, seq = h.shape[0], h.shape[1]
    q = (h @ wq).reshape(b, seq, XA_HEADS, XA_HEAD_DIM)
    s = jnp.einsum("blhd,bmhd->bhlm", q, k.astype(q.dtype)).astype(jnp.float32) * (XA_HEAD_DIM ** -0.5)
    pr = jax.nn.softmax(s, axis=-1).astype(h.dtype)
    o = jnp.einsum("bhlm,bmhd->blhd", pr, v.astype(h.dtype)).reshape(b, seq, D_MODEL)
    return o @ wo


def run_trunk(x, ssd_prev, ssm_prev, cm_prev, mem_k, mem_v, ffn1_norm, ffn1_wi, ffn1_wo, mix_norm, w_in,
              ssd_conv_w, ssd_conv_b, ssd_dt_bias, ssd_a_log, ssd_d, ssd_norm, ssd_w_out, cm_dw_w, cm_dw_b,
              cm_ln_g, cm_ln_b, cm_w_out, w_mix_out, xa_norm, xa_wq, xa_wo, ffn2_norm, ffn2_wi, ffn2_wo):
    new_ssd, new_ssm, new_cm = [], [], []
    for i in range(DEPTH):
        x = x + 0.5 * swiglu_ffn(rmsnorm(x, ffn1_norm[i]), ffn1_wi[i], ffn1_wo[i])
        mix, s_conv, s_h, s_cm = parallel_mixer(
            rmsnorm(x, mix_norm[i]), ssd_prev[i], ssm_prev[i], cm_prev[i], w_in[i], ssd_conv_w[i], ssd_conv_b[i],
            ssd_dt_bias[i], ssd_a_log[i], ssd_d[i], ssd_norm[i], ssd_w_out[i], cm_dw_w[i], cm_dw_b[i],
            cm_ln_g[i], cm_ln_b[i], cm_w_out[i], w_mix_out[i])
        x = x + mix
        x = x + mem_attend(rmsnorm(x, xa_norm[i]), mem_k[i], mem_v[i], xa_wq[i], xa_wo[i])
        x = x + 0.5 * swiglu_ffn(rmsnorm(x, ffn2_norm[i]), ffn2_wi[i], ffn2_wo[i])
        new_ssd.append(s_conv)
        new_ssm.append(s_h)
        new_cm.append(s_cm)
    return x, jnp.stack(new_ssd), jnp.stack(new_ssm), jnp.stack(new_cm)


def setup_inputs(seed: int = 0) -> dict:
    key = jax.random.key(seed)
    ks = iter(jax.random.split(key, 48))
    f32 = jnp.float32

    def normal(shape, scale):
        return scale * jax.random.normal(next(ks), shape, f32)

    def gain(shape):
        return 1.0 + normal(shape, 0.05)

    dt0 = jnp.exp(jax.random.uniform(next(ks), (DEPTH, SSD_HEADS), f32, math.log(1e-3), math.log(1e-1)))
    return {
        "x_prompt": normal((BATCH, SEQ, D_MODEL), 1.0),
        "x_sample": normal((DEC_BATCH, DEC_SEQ, D_MODEL), 1.0),
        "mem_prompt": normal((BATCH, N_MEM, D_MODEL), 1.0),
        "cache_ssd_conv": normal((DEPTH, DEC_BATCH, SSD_CONV - 1, XBC_DIM), 1.0),
        "cache_ssm_state": normal((DEPTH, DEC_BATCH, SSD_HEADS, SSD_HEADDIM, SSD_STATE), 0.3),
        "cache_cm_conv": normal((DEPTH, DEC_BATCH, CM_WIDTH - 1, D_CONV), 1.0),
        "cache_mem_k": normal((DEPTH, DEC_BATCH, N_MEM, XA_HEADS, XA_HEAD_DIM), 1.0),
        "cache_mem_v": normal((DEPTH, DEC_BATCH, N_MEM, XA_HEADS, XA_HEAD_DIM), 1.0),
        "ffn1_norm": gain((DEPTH, D_MODEL)),
        "ffn1_wi": normal((DEPTH, D_MODEL, 2 * D_FF), D_MODEL ** -0.5),
        "ffn1_wo": normal((DEPTH, D_FF, D_MODEL), D_FF ** -0.5),
        "mix_norm": gain((DEPTH, D_MODEL)),
        "w_in": normal((DEPTH, D_MODEL, IN_COLS), D_MODEL ** -0.5),
        "ssd_conv_w": normal((DEPTH, SSD_CONV, XBC_DIM), SSD_CONV ** -0.5),
        "ssd_conv_b": normal((DEPTH, XBC_DIM), 0.02),
        "ssd_dt_bias": dt0 + jnp.log(-jnp.expm1(-dt0)),
        "ssd_a_log": jnp.log(jax.random.uniform(next(ks), (DEPTH, SSD_HEADS), f32, 1.0, 16.0)),
        "ssd_d": gain((DEPTH, SSD_HEADS)),
        "ssd_norm": gain((DEPTH, D_INNER)),
        "ssd_w_out": normal((DEPTH, D_INNER, D_MODEL), D_INNER ** -0.5),
        "cm_dw_w": normal((DEPTH, CM_WIDTH, D_CONV), CM_WIDTH ** -0.5),
        "cm_dw_b": normal((DEPTH, D_CONV), 0.02),
        "cm_ln_g": gain((DEPTH, D_CONV)),
        "cm_ln_b": normal((DEPTH, D_CONV), 0.02),
        "cm_w_out": normal((DEPTH, D_CONV, D_MODEL), D_CONV ** -0.5),
        "w_mix_out": normal((DEPTH, D_MODEL, D_MODEL), D_MODEL ** -0.5),
        "xa_norm": gain((DEPTH, D_MODEL)),
        "mem_norm": gain((DEPTH, D_MODEL)),
        "xa_wq": normal((DEPTH, D_MODEL, D_MODEL), D_MODEL ** -0.5),
        "xa_wk": normal((DEPTH, D_MODEL, D_MODEL), D_MODEL ** -0.5),
        "xa_wv": normal((DEPTH, D_MODEL, D_MODEL), D_MODEL ** -0.5),
        "xa_wo": normal((DEPTH, D_MODEL, D_MODEL), D_MODEL ** -0.5),
        "ffn2_norm": gain((DEPTH, D_MODEL)),
        "ffn2_wi": normal((DEPTH, D_MODEL, 2 * D_FF), D_MODEL ** -0.5),
        "ffn2_wo": normal((DEPTH, D_FF, D_MODEL), D_FF ** -0.5),
        "final_norm": gain((D_MODEL,)),
    }


def reference(x_prompt, x_sample, mem_prompt, cache_ssd_conv, cache_ssm_state, cache_cm_conv, cache_mem_k,
              cache_mem_v, ffn1_norm, ffn1_wi, ffn1_wo, mix_norm, w_in, ssd_conv_w, ssd_conv_b, ssd_dt_bias,
              ssd_a_log, ssd_d, ssd_norm, ssd_w_out, cm_dw_w, cm_dw_b, cm_ln_g, cm_ln_b, cm_w_out, w_mix_out,
              xa_norm, mem_norm, xa_wq, xa_wk, xa_wv, xa_wo, ffn2_norm, ffn2_wi, ffn2_wo, final_norm):
    p_mem_k = jnp.stack([mem_kv(mem_prompt, mem_norm[i], xa_wk[i]) for i in range(DEPTH)])
    p_mem_v = jnp.stack([mem_kv(mem_prompt, mem_norm[i], xa_wv[i]) for i in range(DEPTH)])
    bp, dtp = x_prompt.shape[0], x_prompt.dtype
    zero_ssd = jnp.zeros((DEPTH, bp, SSD_CONV - 1, XBC_DIM), dtp)
    zero_ssm = jnp.zeros((DEPTH, bp, SSD_HEADS, SSD_HEADDIM, SSD_STATE), dtp)
    zero_cm = jnp.zeros((DEPTH, bp, CM_WIDTH - 1, D_CONV), dtp)
    h_p, p_ssd_conv, p_ssm_state, p_cm_conv = run_trunk(
        x_prompt, zero_ssd, zero_ssm, zero_cm, p_mem_k, p_mem_v, ffn1_norm, ffn1_wi, ffn1_wo, mix_norm, w_in,
        ssd_conv_w, ssd_conv_b, ssd_dt_bias, ssd_a_log, ssd_d, ssd_norm, ssd_w_out, cm_dw_w, cm_dw_b,
        cm_ln_g, cm_ln_b, cm_w_out, w_mix_out, xa_norm, xa_wq, xa_wo, ffn2_norm, ffn2_wi, ffn2_wo)
    y_prompt = rmsnorm(h_p, final_norm)
    h_s, s_ssd_conv, s_ssm_state, s_cm_conv = run_trunk(
        x_sample, cache_ssd_conv, cache_ssm_state, cache_cm_conv, cache_mem_k, cache_mem_v, ffn1_norm, ffn1_wi,
        ffn1_wo, mix_norm, w_in, ssd_conv_w, ssd_conv_b, ssd_dt_bias, ssd_a_log, ssd_d, ssd_norm, ssd_w_out,
        cm_dw_w, cm_dw_b, cm_ln_g, cm_ln_b, cm_w_out, w_mix_out, xa_norm, xa_wq, xa_wo, ffn2_norm, ffn2_wi,
        ffn2_wo)
    y_sample = rmsnorm(h_s, final_norm)
    return (y_prompt, y_sample, p_ssd_conv, p_ssm_state, p_cm_conv, p_mem_k, p_mem_v, s_ssd_conv, s_ssm_state, s_cm_conv)
```

```python
import contextlib
import numpy as np
import concourse.bass as bass
import concourse.mybir as mybir
from concourse.bass_utils import run_bass_kernel_spmd

F32 = mybir.dt.float32
BF16 = mybir.dt.bfloat16
AF = mybir.ActivationFunctionType
ALU = mybir.AluOpType

D = 2048
DC = 16
SEQ = 4096
SL = 1024
NSL = 4
DEC = 16
DFF = 8192
DIN = 4096
XBC = 6144
NH = 64
HP = 64
NG = 8
NS = 128
NMEM = 256
CMW = 31
EPS = 1e-6
INC = 18496
O_Z, O_XBC, O_DT, O_CV, O_CG, O_GS, O_GC = 0, 4096, 10240, 10304, 12352, 14400, 16448

ENGS = ("tensor", "vector", "scalar", "gpsimd", "sync")


class Op:
    __slots__ = ("eng", "fn", "waits", "signal", "idx", "dkey", "dval")

    def __init__(self, eng, fn, dkey=None):
        self.eng, self.fn, self.dkey = eng, fn, dkey
        self.waits = []
        self.signal = False
        self.idx = None
        self.dval = None


import os
DBG = bool(os.environ.get('SCHED_DBG'))


class Sched:
    PID = 0

    def __init__(self, nc):
        self.nc = nc
        self.ops = {e: [] for e in ENGS}
        self.last_w = {}
        self.readers = {}
        self.dcount = {}
        self.out_dmas = []

    def add(self, eng, fn, reads=(), writes=(), dkey=None):
        op = Op(eng, fn, dkey)
        deps = []
        for b in reads:
            w = self.last_w.get(b)
            if w is not None:
                deps.append(w)
        for b in writes:
            w = self.last_w.get(b)
            if w is not None:
                deps.append(w)
            deps.extend(self.readers.get(b, ()))
        seen = set()
        for d in deps:
            if id(d) in seen:
                continue
            seen.add(id(d))
            if d.dkey is None and d.eng == "tensor" and eng == "tensor" and dkey is None:
                continue
            op.waits.append(d)
            d.signal = True
        for b in reads:
            self.readers.setdefault(b, []).append(op)
        for b in writes:
            self.last_w[b] = op
            self.readers[b] = []
        if dkey is not None:
            self.dcount[dkey] = self.dcount.get(dkey, 0) + 1
            op.dval = 16 * self.dcount[dkey]
        self.ops[eng].append(op)
        return op

    def emit(self):
        nc = self.nc
        Sched.PID += 1
        pid = Sched.PID
        esem = {e: nc.alloc_semaphore("se%d_%s" % (pid, e)) for e in ENGS}
        dsem = {k: nc.alloc_semaphore("sd%d_%d" % (pid, i)) for i, k in enumerate(self.dcount)}
        for e in ENGS:
            c = 0
            for op in self.ops[e]:
                if op.dkey is None and op.signal:
                    c += 1
                    op.idx = c
        out_dmas = self.out_dmas

        def run(e, eng):
            known = {}
            for op in self.ops[e]:
                need = {}
                for d in op.waits:
                    if d.dkey is not None:
                        key, val = ("d", d.dkey), d.dval
                    else:
                        key, val = ("e", d.eng), d.idx
                    if val > need.get(key, 0):
                        need[key] = val
                for key, val in need.items():
                    if known.get(key, 0) >= val:
                        continue
                    eng.wait_ge(dsem[key[1]] if key[0] == "d" else esem[key[1]], val)
                    known[key] = val
                    if DBG:
                        print("WAIT", e, key, val)
                ins = op.fn(eng)
                if DBG:
                    print("OP", e, "dkey", op.dkey, "sig", op.idx if op.signal else None)
                if op.dkey is not None:
                    ins.then_inc(dsem[op.dkey], 16)
                elif op.signal:
                    ins.then_inc(esem[e], 1)
            fin = {}
            for d in self.ops[e]:
                if d.dkey is not None:
                    fin[d.dkey] = max(fin.get(d.dkey, 0), d.dval)
            for kk, v in fin.items():
                if known.get(("d", kk), 0) < v:
                    eng.wait_ge(dsem[kk], v)

        with nc.Block() as block:
            for e in ENGS:
                if self.ops[e] or e == "sync":
                    getattr(block, e)(lambda eng, e=e: run(e, eng))


class K:
    pass


@contextlib.contextmanager
def phase(k):
    with k.nc.cleanup_on_exit():
        S = Sched(k.nc)
        k.S = S
        k.rr = {}
        k.cache = {}
        yield S
        S.emit()


def rot(k, name, n):
    i = k.rr.get(name, 0)
    k.rr[name] = i + 1
    return i % n


def sb(k, name, shape, dt):
    return k.nc.alloc_sbuf_tensor(name, list(shape), dt, allow_name_mangling=True)


def bank(k, i, n=512, p=128, dt=None):
    ap = k.ps[0:p, i * 512:i * 512 + 512]
    if dt is not None:
        return ap.bitcast(dt)
    return ap[:, 0:n]


def tiles_of(TT):
    t = []
    o = 0
    while o < TT:
        n = min(512, TT - o)
        t.append((o, n))
        o += n
    return t


def load_xT(k, S, xT, srcs):
    nc = k.nc
    if "xin" not in k.cache:
        k.cache["xin"] = [sb(k, "xin%d" % i, [128, D], F32) for i in range(1)]
    xin = k.cache["xin"]
    for (src, col0, rows) in srcs:
        for t0 in range(0, rows, 128):
            r = min(128, rows - t0)
            sl = rot(k, "xin", 1)
            xi = xin[sl]
            S.add("sync", lambda e, xi=xi, src=src, t0=t0, r=r: e.dma_start(out=xi[0:r, :], in_=src[t0:t0 + r, :]),
                  writes=[("xin", sl)], dkey=("xin", sl))
            for c0 in range(0, DC, 4):
                b = 4 + rot(k, "tpb", 4)
                for j in range(4):
                    c = c0 + j
                    S.add("tensor", lambda e, b=b, j=j, xi=xi, c=c, r=r: e.transpose(
                        bank(k, b)[:, j * 128:j * 128 + r], xi[0:r, c * 128:(c + 1) * 128], k.ident[0:r, 0:r]),
                        reads=[("xin", sl)], writes=[("ps", b)])
                eng = "scalar" if (c0 // 4) % 2 == 0 else "vector"
                col = col0 + t0

                def cp(e, b=b, c0=c0, col=col, r=r, eng=eng):
                    src_ = bank(k, b).rearrange("p (j t) -> p j t", j=4)[:, :, 0:r]
                    dst_ = xT[:, c0:c0 + 4, col:col + r]
                    if eng == "scalar":
                        return e.copy(dst_, src_)
                    return e.tensor_copy(dst_, src_)
                S.add(eng, cp, reads=[("ps", b)], writes=[("xT", c, col // 512) for c in range(c0, c0 + 4)] +
                      [("xT", c, (col + r - 1) // 512) for c in range(c0, c0 + 4)])


def rmsnorm(k, S, xT, hT, gcol, TT, xname="xT", hname="hT"):
    if "nsq" not in k.cache:
        k.cache["nsq"] = [sb(k, "nsq%d" % i, [128, 512], F32) for i in range(2)]
        k.cache["nrs"] = [sb(k, "nrs%d" % i, [128, 512], F32) for i in range(2)]
    sq, rstd = k.cache["nsq"], k.cache["nrs"]
    for ti, (o, n) in enumerate(tiles_of(TT)):
        b = 7
        for c in range(DC):
            s = rot(k, "nsq", 2)
            S.add("scalar", lambda e, s=s, c=c, o=o, n=n: e.activation(out=sq[s][:, 0:n], in_=xT[:, c, o:o + n], func=AF.Square),
                  reads=[(xname, c, ti)], writes=[("nsq", s)])
            S.add("tensor", lambda e, s=s, c=c, n=n, b=b: e.matmul(bank(k, b, n), k.onesD[:, :], sq[s][:, 0:n],
                                                                   start=(c == 0), stop=(c == DC - 1)),
                  reads=[("nsq", s)], writes=[("ps", b)])
        r = rot(k, "nrs", 2)
        S.add("scalar", lambda e, r=r, n=n, b=b: e.activation(out=rstd[r][:, 0:n], in_=bank(k, b, n), func=AF.Sqrt, bias=k.eps[:, 0:1]),
              reads=[("ps", b)], writes=[("nrs", r)])
        S.add("vector", lambda e, r=r, n=n: e.reciprocal(rstd[r][:, 0:n], rstd[r][:, 0:n]),
              reads=[("nrs", r)], writes=[("nrs", r)])
        for c in range(DC):
            S.add("vector", lambda e, r=r, c=c, o=o, n=n: e.scalar_tensor_tensor(
                hT[:, c, o:o + n], xT[:, c, o:o + n], gcol[:, c:c + 1], rstd[r][:, 0:n], ALU.mult, ALU.mult),
                reads=[(xname, c, ti), ("nrs", r)], writes=[(hname, c, ti)])


def ffn(k, S, xT, hT, wi, wo, TT):
    tl = tiles_of(TT)
    FG = 256
    NB = 4
    wa = [sb(k, "wa%d" % i, [128, DC, FG], BF16) for i in range(2)]
    wb = [sb(k, "wb%d" % i, [128, DC, FG], BF16) for i in range(2)]
    wos = [sb(k, "wo%d" % i, [128, NB, D], BF16) for i in range(2)]
    g = [sb(k, "g%d" % i, [128, NB, TT], BF16) for i in range(1)]
    sa = [sb(k, "sa%d" % i, [128, 512], F32) for i in range(2)]
    wiv = wi.rearrange("(kc p) n -> p kc n", p=128)
    wov = wo.rearrange("(fc p) n -> p fc n", p=128)
    nblk = DFF // (128 * NB)
    for blk in range(nblk):
        gs = rot(k, "g", 1)
        ws = rot(k, "wo", 2)
        S.add("gpsimd", lambda e, ws=ws, blk=blk: e.dma_start(out=wos[ws][:], in_=wov[:, blk * NB:(blk + 1) * NB, :]),
              writes=[("wo", ws)], dkey=("wo", ws))
        for half in range(NB * 128 // FG):
            s = rot(k, "wi", 2)
            f0 = blk * NB * 128 + half * FG
            S.add("gpsimd", lambda e, s=s, f0=f0: e.dma_start(out=wa[s][:], in_=wiv[:, :, f0:f0 + FG]),
                  writes=[("wa", s)], dkey=("wa", s))
            S.add("gpsimd", lambda e, s=s, f0=f0: e.dma_start(out=wb[s][:], in_=wiv[:, :, DFF + f0:DFF + f0 + FG]),
                  writes=[("wb", s)], dkey=("wb", s))
            for fcl in range(FG // 128):
                fc = half * (FG // 128) + fcl
                for ti, (o, n) in enumerate(tl):
                    ba = rot(k, "psA", 2)
                    bb = 2 + rot(k, "psB", 2)
                    for kc in range(DC):
                        S.add("tensor", lambda e, ba=ba, s=s, kc=kc, fcl=fcl, o=o, n=n: e.matmul(
                            bank(k, ba, n), wa[s][:, kc, fcl * 128:(fcl + 1) * 128], hT[:, kc, o:o + n],
                            start=(kc == 0), stop=(kc == DC - 1)),
                            reads=[("wa", s), ("hT", kc, ti)], writes=[("ps", ba)])
                    for kc in range(DC):
                        S.add("tensor", lambda e, bb=bb, s=s, kc=kc, fcl=fcl, o=o, n=n: e.matmul(
                            bank(k, bb, n), wb[s][:, kc, fcl * 128:(fcl + 1) * 128], hT[:, kc, o:o + n],
                            start=(kc == 0), stop=(kc == DC - 1)),
                            reads=[("wb", s), ("hT", kc, ti)], writes=[("ps", bb)])
                    ss = rot(k, "sa", 2)
                    S.add("scalar", lambda e, ss=ss, ba=ba, n=n: e.activation(out=sa[ss][:, 0:n], in_=bank(k, ba, n), func=AF.Silu),
                          reads=[("ps", ba)], writes=[("sa", ss)])
                    S.add("vector", lambda e, ss=ss, bb=bb, gs=gs, fc=fc, o=o, n=n: e.tensor_tensor(
                        g[gs][:, fc, o:o + n], sa[ss][:, 0:n], bank(k, bb, n), ALU.mult),
                        reads=[("sa", ss), ("ps", bb)], writes=[("g", gs, fc, ti)])
        for dc in range(DC):
            for ti, (o, n) in enumerate(tl):
                bo = 4 + rot(k, "psO", 3)
                for fc in range(NB):
                    S.add("tensor", lambda e, bo=bo, ws=ws, fc=fc, dc=dc, gs=gs, o=o, n=n: e.matmul(
                        bank(k, bo, n), wos[ws][:, fc, dc * 128:(dc + 1) * 128], g[gs][:, fc, o:o + n],
                        start=(fc == 0), stop=(fc == NB - 1)),
                        reads=[("wo", ws), ("g", gs, fc, ti)], writes=[("ps", bo)])
                S.add("vector", lambda e, bo=bo, dc=dc, o=o, n=n: e.scalar_tensor_tensor(
                    xT[:, dc, o:o + n], bank(k, bo, n), 0.5, xT[:, dc, o:o + n], ALU.mult, ALU.add),
                    reads=[("ps", bo), ("xT", dc, ti)], writes=[("xT", dc, ti)])


def allt(TT):
    return range(len(tiles_of(TT)))


def multilinear(k, S, jobs, nchunks, TT, epi, width=128, tag="ml", banks=(0, 1, 2, 3)):
    SW = 2 * width
    st_bufs = []
    for j, (inT, inname, KC, W, col0) in enumerate(jobs):
        ck = ("stage", tag, j, KC, SW)
        if ck not in k.cache:
            k.cache[ck] = [sb(k, "%s_w%d_%d" % (tag, j, i), [128, KC, SW], BF16) for i in range(2)]
        st_bufs.append(k.cache[ck])
    tl = tiles_of(TT)
    for st in range((nchunks + 1) // 2):
        nch = min(2, nchunks - 2 * st)
        slots = []
        for j, (inT, inname, KC, W, col0) in enumerate(jobs):
            sl = rot(k, "%s_s%d" % (tag, j), 2)
            slots.append(sl)
            Wv = W.rearrange("(kc p) n -> p kc n", p=128)
            c0 = col0 + st * SW
            S.add("gpsimd", lambda e, j=j, sl=sl, Wv=Wv, c0=c0, nch=nch: e.dma_start(
                out=st_bufs[j][sl][:, :, 0:nch * width], in_=Wv[:, :, c0:c0 + nch * width]),
                writes=[(tag, j, sl)], dkey=(tag, j, sl))
        for cl in range(nch):
            i = 2 * st + cl
            for ti, (o, n) in enumerate(tl):
                bks = []
                for j, (inT, inname, KC, W, col0) in enumerate(jobs):
                    b = banks[rot(k, tag + "_b", len(banks))]
                    sl = slots[j]
                    for kc in range(KC):
                        S.add("tensor", lambda e, b=b, j=j, sl=sl, kc=kc, cl=cl, inT=inT, o=o, n=n, KC=KC: e.matmul(
                            bank(k, b, n, p=width), st_bufs[j][sl][:, kc, cl * width:(cl + 1) * width], inT[:, kc, o:o + n],
                            start=(kc == 0), stop=(kc == KC - 1)),
                            reads=[(tag, j, sl), (inname, kc, ti)], writes=[("ps", b)])
                    bks.append(b)
                epi(i, ti, o, n, bks)


def transpose_to(k, S, dst_fn, src, rows, cols, dt, reads, writes, b, eng="scalar", col_off=0):
    def mm(e):
        if dt == BF16:
            out = bank(k, b, dt=BF16)[0:cols, col_off:col_off + rows]
            idn = k.identb
        else:
            out = bank(k, b)[0:cols, col_off:col_off + rows]
            idn = k.ident
        return e.transpose(out, src, idn[0:rows, 0:rows])
    S.add("tensor", mm, reads=reads, writes=[("ps", b)])


def evac(S, eng, dst, src, reads, writes):
    if eng == "scalar":
        S.add("scalar", lambda e: e.copy(dst, src), reads=reads, writes=writes)
    else:
        S.add("vector", lambda e: e.tensor_copy(dst, src), reads=reads, writes=writes)


def cm_branch(k, S, TT, last):
    nT = len(tiles_of(TT))
    UB = 30 + SL + (30 + DEC if last else 0)
    hT = k.hT
    v = sb(k, "cm_v", [128, DC, TT], F32)
    lnst = [sb(k, "cm_lnst%d" % i, [128, 512], BF16) for i in range(2)]
    ubuf = [sb(k, "cm_ub%d" % i, [128, UB], F32) for i in range(2)]
    sgt = [sb(k, "cm_sg%d" % i, [128, 512], F32) for i in range(2)]
    NPE = 26
    ubb = [sb(k, "cm_ubb%d" % i, [128, 30 + SL], BF16) for i in range(2)]
    dg = [sb(k, "cm_dg%d" % i, [128, NPE, 128], BF16) for i in range(2)]
    if last:
        cct = sb(k, "cm_cct", [32, D], F32)
        cmc = sb(k, "cm_cmc", [128, DC, 30], F32)
        stail = sb(k, "cm_stail", [128, DC, 30], F32)
        S.add("sync", lambda e: e.dma_start(out=cct[0:30, :], in_=k.din["c_cmconv"][:, :]), writes=["cct"], dkey="cct")
        for c0 in range(0, DC, 4):
            b = 4 + rot(k, "tpb", 4)
            for j in range(4):
                c = c0 + j
                transpose_to(k, S, None, cct[0:30, c * 128:(c + 1) * 128], 30, 128, F32, ["cct"], None, b, col_off=j * 128)
            S.add("scalar", lambda e, b=b, c0=c0: e.copy(cmc[:, c0:c0 + 4, :], bank(k, b).rearrange("p (j t) -> p j t", j=4)[:, :, 0:30]),
                  reads=[("ps", b)], writes=[("cmc", c) for c in range(c0, c0 + 4)])

    def off(o):
        return 30 + o if o < SL else 30 + SL + 30 + (o - SL)

    state = {}

    def epi(i, ti, o, n, bks):
        bv, bg = bks
        ub = i % 2
        if ti == 0:
            S.add("vector", lambda e, ub=ub, i=i: e.tensor_copy(ubuf[ub][:, 0:30], k.cm_tail[:, i, :]),
                  reads=[("cm_tail", i)], writes=[("ub", ub)])
            if last:
                S.add("vector", lambda e, ub=ub, i=i: e.tensor_copy(ubuf[ub][:, 30 + SL:60 + SL], cmc[:, i, :]),
                      reads=[("cmc", i)], writes=[("ub", ub)])
        ss = rot(k, "cm_sg", 2)
        S.add("scalar", lambda e, ss=ss, bg=bg, n=n: e.activation(out=sgt[ss][:, 0:n], in_=bank(k, bg, n), func=AF.Sigmoid),
              reads=[("ps", bg)], writes=[("cm_sg", ss)])
        S.add("vector", lambda e, ss=ss, bv=bv, ub=ub, o=o, n=n: e.tensor_tensor(
            ubuf[ub][:, off(o):off(o) + n], bank(k, bv, n), sgt[ss][:, 0:n], ALU.mult),
            reads=[("ps", bv), ("cm_sg", ss)], writes=[("ub", ub)])
        if ti == nT - 1:
            S.add("scalar", lambda e, ub=ub: e.copy(ubb[ub][:, 0:30 + SL], ubuf[ub][:, 0:30 + SL]),
                  reads=[("ub", ub)], writes=[("ubb", ub)])
            ds = rot(k, "cm_dg", 2)
            S.add("vector", lambda e, ds=ds, i=i: e.tensor_tensor(
                dg[ds][:, :, :], k.ident[:, :].unsqueeze(1).to_broadcast([128, NPE, 128]),
                k.cm_w[:, i, 0:NPE].unsqueeze(2).to_broadcast([128, NPE, 128]), ALU.mult),
                writes=[("cm_dg", ds)])
            for (o2, n2) in ((0, 512), (512, 512)):
                bcv = 6 + rot(k, "cm_cvb", 2)
                for kk in range(NPE):
                    S.add("tensor", lambda e, bcv=bcv, ds=ds, ub=ub, kk=kk, o2=o2, n2=n2: e.matmul(
                        bank(k, bcv, n2), dg[ds][:, kk, :], ubb[ub][:, o2 + kk:o2 + kk + n2], start=(kk == 0), stop=(kk == NPE - 1)),
                        reads=[("cm_dg", ds), ("ubb", ub)], writes=[("ps", bcv)])
                S.add("vector", lambda e, bcv=bcv, i=i, o2=o2, n2=n2: e.tensor_single_scalar(
                    v[:, i, o2:o2 + n2], bank(k, bcv, n2), k.cm_b[:, i:i + 1], ALU.add),
                    reads=[("ps", bcv)], writes=[("cm_v", i)])
            for kk in range(NPE, CMW):
                S.add("vector", lambda e, ub=ub, i=i, kk=kk: e.scalar_tensor_tensor(
                    v[:, i, 0:SL], ubuf[ub][:, kk:kk + SL], k.cm_w[:, i, kk:kk + 1], v[:, i, 0:SL], ALU.mult, ALU.add),
                    reads=[("ub", ub), ("cm_v", i)], writes=[("cm_v", i)])
            parts = []
            if last:
                parts.append((SL, 30 + SL, DEC))
            for (vo, uo, ln) in parts:
                S.add("vector", lambda e, ub=ub, i=i, vo=vo, uo=uo, ln=ln: e.tensor_scalar(
                    v[:, i, vo:vo + ln], ubuf[ub][:, uo:uo + ln], k.cm_w[:, i, 0:1], k.cm_b[:, i:i + 1], ALU.mult, ALU.add),
                    reads=[("ub", ub)], writes=[("cm_v", i)])
                for kk in range(1, CMW):
                    S.add("vector", lambda e, ub=ub, i=i, vo=vo, uo=uo, ln=ln, kk=kk: e.scalar_tensor_tensor(
                        v[:, i, vo:vo + ln], ubuf[ub][:, uo + kk:uo + kk + ln], k.cm_w[:, i, kk:kk + 1], v[:, i, vo:vo + ln],
                        ALU.mult, ALU.add),
                        reads=[("ub", ub), ("cm_v", i)], writes=[("cm_v", i)])
            S.add("scalar", lambda e, ub=ub, i=i: e.copy(k.cm_tail[:, i, :], ubuf[ub][:, SL:SL + 30]),
                  reads=[("ub", ub)], writes=[("cm_tail", i)])
            if last:
                S.add("scalar", lambda e, ub=ub, i=i: e.copy(stail[:, i, :], ubuf[ub][:, UB - 30:UB]),
                      reads=[("ub", ub)], writes=[("cm_stail", i)])

    multilinear(k, S, [(hT, "hT", DC, k.din["w_in"], O_CV), (hT, "hT", DC, k.din["w_in"], O_CG)], DC, TT, epi, tag="cmA")

    sq = [sb(k, "cm_sq%d" % i, [128, 512], F32) for i in range(2)]
    mean = sb(k, "cm_mean", [128, 512], F32)
    rstd = sb(k, "cm_rstd", [128, 512], F32)
    tmp = [sb(k, "cm_tmp%d" % i, [128, 512], F32) for i in range(2)]
    for ti, (o, n) in enumerate(tiles_of(TT)):
        b1, b2 = 4, 5
        for c in range(DC):
            S.add("tensor", lambda e, c=c, o=o, n=n: e.matmul(bank(k, b1, n), k.onesD[:, :], v[:, c, o:o + n], start=(c == 0), stop=(c == DC - 1)),
                  reads=[("cm_v", c)], writes=[("ps", b1)])
        for c in range(DC):
            s = rot(k, "cm_sq", 2)
            S.add("scalar", lambda e, s=s, c=c, o=o, n=n: e.activation(out=sq[s][:, 0:n], in_=v[:, c, o:o + n], func=AF.Square),
                  reads=[("cm_v", c)], writes=[("cm_sq", s)])
            S.add("tensor", lambda e, s=s, c=c, n=n: e.matmul(bank(k, b2, n), k.onesD[:, :], sq[s][:, 0:n], start=(c == 0), stop=(c == DC - 1)),
                  reads=[("cm_sq", s)], writes=[("ps", b2)])
        S.add("scalar", lambda e, n=n: e.copy(mean[:, 0:n], bank(k, b1, n)), reads=[("ps", b1)], writes=["cm_mean"])
        S.add("vector", lambda e, n=n: e.tensor_tensor(rstd[:, 0:n], mean[:, 0:n], mean[:, 0:n], ALU.mult), reads=["cm_mean"], writes=["cm_rstd"])
        S.add("vector", lambda e, n=n: e.tensor_tensor(rstd[:, 0:n], bank(k, b2, n), rstd[:, 0:n], ALU.subtract),
              reads=[("ps", b2), "cm_rstd"], writes=["cm_rstd"])
        S.add("scalar", lambda e, n=n: e.activation(out=rstd[:, 0:n], in_=rstd[:, 0:n], func=AF.Sqrt, bias=k.eps[:, 0:1]),
              reads=["cm_rstd"], writes=["cm_rstd"])
        S.add("vector", lambda e, n=n: e.reciprocal(rstd[:, 0:n], rstd[:, 0:n]), reads=["cm_rstd"], writes=["cm_rstd"])
        for c in range(DC):
            t = rot(k, "cm_tmp", 2)
            S.add("vector", lambda e, t=t, c=c, o=o, n=n: e.tensor_tensor(tmp[t][:, 0:n], v[:, c, o:o + n], mean[:, 0:n], ALU.subtract),
                  reads=[("cm_v", c), "cm_mean"], writes=[("cm_tmp", t)])
            S.add("vector", lambda e, t=t, c=c, n=n: e.scalar_tensor_tensor(tmp[t][:, 0:n], tmp[t][:, 0:n], k.cm_g[:, c:c + 1], rstd[:, 0:n], ALU.mult, ALU.mult),
                  reads=[("cm_tmp", t), "cm_rstd"], writes=[("cm_tmp", t)])
            ls = rot(k, "cm_lnst", 2)
            S.add("scalar", lambda e, t=t, c=c, ls=ls, n=n: e.activation(out=lnst[ls][:, 0:n], in_=tmp[t][:, 0:n], func=AF.Silu, bias=k.cm_lb[:, c:c + 1]),
                  reads=[("cm_tmp", t)], writes=[("cm_lnst", ls)])
            S.add("sync", lambda e, ls=ls, c=c, o=o, n=n: e.dma_start(out=k.lns[:, c, o:o + n], in_=lnst[ls][:, 0:n]),
                  reads=[("cm_lnst", ls)], dkey=("cm_lnst", ls))

    if last:
        cm_tails_out(k, S, stail)


def cm_tail_only(k, S):
    hv = k.hT[:, :, SL - 30:SL]
    sg = [sb(k, "ct_sg%d" % i, [128, 32], F32) for i in range(2)]

    def epi(i, ti, o, n, bks):
        bv, bg = bks
        ss = rot(k, "ct_sg", 2)
        S.add("scalar", lambda e, ss=ss, bg=bg: e.activation(out=sg[ss][:, 0:30], in_=bank(k, bg, 30), func=AF.Sigmoid),
              reads=[("ps", bg)], writes=[("ct_sg", ss)])
        S.add("vector", lambda e, ss=ss, bv=bv, i=i: e.tensor_tensor(k.cm_tail[:, i, :], bank(k, bv, 30), sg[ss][:, 0:30], ALU.mult),
              reads=[("ps", bv), ("ct_sg", ss)], writes=[("cm_tail", i)])
    multilinear(k, S, [(hv, "hT", DC, k.din["w_in"], O_CV), (hv, "hT", DC, k.din["w_in"], O_CG)], DC, 30, epi, tag="cmA")


def cm_branch_b(k, S, TT):
    hT = k.hT
    lnT = sb(k, "cm_ln", [128, DC, TT], BF16)
    sgt = [sb(k, "cm_sg%d" % i, [128, 512], F32) for i in range(2)]
    S.add("sync", lambda e: e.dma_start(out=lnT[:], in_=k.lns[:, :, 0:TT]), writes=[("lnT", c, ti) for c in range(DC) for ti in allt(TT)], dkey="lnl")
    bst = [sb(k, "cm_bst%d" % i, [128, 512], F32) for i in range(2)]

    def epi2(i, ti, o, n, bks):
        bc, bg = bks
        ss = rot(k, "cm_sg", 2)
        S.add("scalar", lambda e, ss=ss, bg=bg, n=n: e.activation(out=sgt[ss][:, 0:n], in_=bank(k, bg, n), func=AF.Sigmoid),
              reads=[("ps", bg)], writes=[("cm_sg", ss)])
        bs = rot(k, "cm_bst", 2)
        S.add("vector", lambda e, ss=ss, bc=bc, bs=bs, n=n: e.tensor_tensor(bst[bs][:, 0:n], bank(k, bc, n), sgt[ss][:, 0:n], ALU.mult),
              reads=[("ps", bc), ("cm_sg", ss)], writes=[("cm_bst", bs)])
        S.add("sync", lambda e, bs=bs, i=i, o=o, n=n: e.dma_start(out=k.Bs[:, i, o:o + n], in_=bst[bs][:, 0:n]),
              reads=[("cm_bst", bs)], dkey=("cm_bst", bs))

    multilinear(k, S, [(lnT, "lnT", DC, k.din["cm_w_out"], 0), (hT, "hT", DC, k.din["w_in"], O_GC)], DC, TT, epi2, tag="cmA")


def cm_tails_out(k, S, stail):
    if True:
        for (src, sname, dst) in ((k.cm_tail, "cm_tail", k.dout["o_p_cm"]), (stail, "cm_stail", k.dout["o_s_cm"])):
            if "cm_ot" not in k.cache:
                k.cache["cm_ot"] = sb(k, "cm_ot", [32, D], F32)
            ot = k.cache["cm_ot"]
            for c0 in range(0, DC, 4):
                b = 4 + rot(k, "tpb", 4)
                for j in range(4):
                    c = c0 + j
                    S.add("tensor", lambda e, b=b, j=j, c=c, src=src: e.transpose(
                        bank(k, b)[0:30, j * 128:(j + 1) * 128], src[:, c, :], k.ident[:, :]),
                        reads=[(sname, c)], writes=[("ps", b)])
                S.add("scalar", lambda e, b=b, c0=c0, ot=ot: e.copy(ot[0:30, c0 * 128:(c0 + 4) * 128], bank(k, b)[0:30, :]),
                      reads=[("ps", b)], writes=["cm_ot"])
            S.add("sync", lambda e, ot=ot, dst=dst: e.dma_start(out=dst[:, :], in_=ot[0:30, :]), reads=["cm_ot"], dkey=("cm_ot", sname))


def ssd_branch(k, S, T0, NM, last, first, mode="full", maskcol=None):
    full = (mode == "full")
    TT = NM + (DEC if last else 0)
    hT = k.hT[:, :, T0:T0 + TT]
    tl = tiles_of(TT)
    nT = len(tl)
    chunks = [(c * 128, 128) for c in range(NM // 128)] + ([(NM, DEC)] if last else [])
    NCH = len(chunks)
    XB = 3 + NM + (3 + DEC if last else 0)
    win = k.din["w_in"]
    MISC = [2, 3, 6, 7] if full else [2, 3, 4, 5, 6, 7]

    def mb():
        return MISC[rot(k, "misc", len(MISC))]

    def xoff(o):
        return 3 + o if o < NM else 3 + NM + 3 + (o - NM)

    ST = sb(k, "ssd_ST", [128, NH, HP], F32)
    STb = sb(k, "ssd_STb", [128, NH, HP], BF16)
    if first:
        S.add("vector", lambda e: e.memset(ST[:], 0.0), writes=[("ST", u) for u in range(16)])
    else:
        S.add("sync", lambda e: e.dma_start(out=ST[:], in_=k.sts[:, :].rearrange("p (h q) -> p h q", h=NH)),
              writes=[("ST", u) for u in range(16)], dkey="stload")
    if full:
        S.add("scalar", lambda e: e.copy(STb[:], ST[:]), reads=[("ST", u) for u in range(16)], writes=[("STb", u) for u in range(16)])

    dtT = sb(k, "ssd_dtT", [64, TT], F32)
    acsT = sb(k, "ssd_acsT", [64, TT], F32)
    dteT = sb(k, "ssd_dteT", [64, TT], F32)
    dA = sb(k, "ssd_dA", [64, TT], F32)
    tk = sb(k, "ssd_tk", [128, NCH, 3, 64], F32)
    Edec = sb(k, "ssd_Edec", [128, NCH, 64], F32)
    diagm = sb(k, "ssd_diag", [64, NCH, 64], F32)
    et64 = sb(k, "ssd_et64", [64, 512], F32)

    def epi_dt(i, ti, o, n, bks):
        b = bks[0]
        S.add("scalar", lambda e, b=b, n=n: e.activation(out=et64[:, 0:n], in_=bank(k, b, n, p=64), func=AF.Exp, bias=k.dtb[:, 0:1]),
              reads=[("ps", b)], writes=["et64"])
        S.add("scalar", lambda e, o=o, n=n: e.activation(out=dtT[:, o:o + n], in_=et64[:, 0:n], func=AF.Ln, bias=1.0),
              reads=["et64"], writes=[("dtT", ti)])
        S.add("vector", lambda e, o=o, n=n: e.tensor_single_scalar(dA[:, o:o + n], dtT[:, o:o + n], k.acol[:, 0:1], ALU.mult),
              reads=[("dtT", ti)], writes=[("dA", ti)])

    multilinear(k, S, [(hT, "hT", DC, win, O_DT)], 1, TT, epi_dt, width=64, tag="dt", banks=(0, 1))
    for ch, (t0, ln) in enumerate(chunks):
        ti = t0 // 512
        S.add("vector", lambda e, t0=t0, ln=ln: e.tensor_tensor_scan(acsT[:, t0:t0 + ln], k.ones64[:, 0:ln], dA[:, t0:t0 + ln], 0.0, ALU.mult, ALU.add),
              reads=[("dA", ti)], writes=[("acsT", ch)])
        S.add("scalar", lambda e, t0=t0, ln=ln: e.activation(out=dteT[:, t0:t0 + ln], in_=acsT[:, t0:t0 + ln], func=AF.Exp,
                                                            bias=acsT[:, t0 + ln - 1:t0 + ln], scale=-1.0),
              reads=[("acsT", ch)], writes=[("dteT", ch)])
        S.add("vector", lambda e, t0=t0, ln=ln: e.tensor_tensor(dteT[:, t0:t0 + ln], dteT[:, t0:t0 + ln], dtT[:, t0:t0 + ln], ALU.mult),
              reads=[("dteT", ch), ("dtT", ti)], writes=[("dteT", ch)])
        b = mb()
        for j, (src, nm) in enumerate(((dtT, ("dtT", ti)), (dteT, ("dteT", ch)), (acsT, ("acsT", ch)))):
            S.add("tensor", lambda e, b=b, j=j, src=src, t0=t0, ln=ln: e.transpose(
                bank(k, b)[0:ln, j * 64:(j + 1) * 64], src[:, t0:t0 + ln], k.ident[0:64, 0:64]),
                reads=[nm], writes=[("ps", b)])
        S.add("scalar", lambda e, b=b, ch=ch, ln=ln: e.copy(tk[0:ln, ch, :, :], bank(k, b)[0:ln, 0:192].rearrange("p (j h) -> p j h", j=3)),
              reads=[("ps", b)], writes=[("tk", ch)])
        S.add("vector", lambda e, ch=ch, t0=t0, ln=ln: e.tensor_single_scalar(diagm[:, ch, :], k.ident[0:64, 0:64], acsT[:, t0 + ln - 1:t0 + ln], ALU.mult),
              reads=[("acsT", ch)], writes=[("diagm", ch)])
    if full:
        S.add("sync", lambda e: e.dma_start(out=k.acsd[:, 0:TT], in_=acsT[:, :]), reads=[("acsT", ch) for ch in range(NCH)], writes=["acsd"], dkey="acsd")
    dflat = diagm[:, :, :].rearrange("p c h -> p (c h)")
    eflat = Edec[:, :, :].rearrange("p c h -> p (c h)")
    for o in range(0, NCH * 64, 512):
        n = min(512, NCH * 64 - o)
        b = mb()
        S.add("tensor", lambda e, b=b, o=o, n=n: e.matmul(bank(k, b, n), k.ones64[:, :], dflat[:, o:o + n], start=True, stop=True),
              reads=[("diagm", ch) for ch in range(NCH)], writes=[("ps", b)])
        S.add("scalar", lambda e, b=b, o=o, n=n: e.activation(out=eflat[:, o:o + n], in_=bank(k, b, n), func=AF.Exp),
              reads=[("ps", b)], writes=["Edec"])

    xbuf = [sb(k, "ssd_xb%d" % i, [128, XB], F32) for i in range(2)]
    cacc = [sb(k, "ssd_ca%d" % i, [128, XB], F32) for i in range(2)]
    two = range(2)
    BT2 = [sb(k, "ssd_BT%d" % i, [128, TT], BF16) for i in two]
    CT2 = [sb(k, "ssd_CT%d" % i, [128, TT], BF16) for i in (two if full else range(1))] * (1 if full else 2)
    Btok2 = [sb(k, "ssd_Btok%d" % i, [128, NCH, 128], BF16) for i in two]
    GmT2 = [sb(k, "ssd_GmT%d" % i, [128, NCH, 128], BF16) for i in (two if full else range(1))] * (1 if full else 2)
    ygs = [sb(k, "ssd_yg%d" % i, [128, 4, TT], BF16) for i in (two if full else range(1))] * (1 if full else 2)
    xs2 = [sb(k, "ssd_xs%d" % i, [128, 2, TT], F32) for i in two]
    zs2 = [sb(k, "ssd_zs%d" % i, [128, 2, TT], BF16) for i in (two if full else range(1))] * (1 if full else 2)
    xdt2 = [sb(k, "ssd_xdt%d" % i, [128, NCH, 256], BF16) for i in (two if full else range(1))] * (1 if full else 2)
    xw2 = [sb(k, "ssd_xw%d" % i, [128, NCH, 256], BF16) for i in two]
    MT2 = [sb(k, "ssd_MT%d" % i, [128, 4, NCH, 128], BF16) for i in (two if full else range(1))] * (1 if full else 2)
    CdT2 = [sb(k, "ssd_CdT%d" % i, [128, 4, TT], BF16) for i in (two if full else range(1))] * (1 if full else 2)
    ar = [sb(k, "ssd_ar%d" % i, [128, TT], F32) for i in range(3 if full else 0)]
    er = [sb(k, "ssd_er%d" % i, [128, TT], F32) for i in range(2 if full else 0)]
    sg = [sb(k, "ssd_sg%d" % i, [128, 128], F32) for i in range(2)]
    et = [sb(k, "ssd_et%d" % i, [128, 128], BF16) for i in range(2)]
    sqn = [sb(k, "ssd_sq%d" % i, [128, 512], F32) for i in range(2)]
    rsn = sb(k, "ssd_rs", [128, 512], F32)
    if last:
        sct = sb(k, "ssd_sct", [8, 1536], F32)
        scc = sb(k, "ssd_scc", [128, 48, 3], F32)
        sst = sb(k, "ssd_sst", [128, 48, 3], F32)
        stst = [sb(k, "ssd_stst%d" % i, [128, 2, 128], F32) for i in range(2)]
        cst = [sb(k, "ssd_cst%d" % i, [128, 2, 128], F32) for i in range(2)]
        for c0 in range(0, 48, 4):
            if c0 % 12 == 0:
                S.add("sync", lambda e, c0=c0: e.dma_start(out=sct[0:3, :], in_=k.din["c_ssdconv"][:, c0 * 128:c0 * 128 + 1536]), writes=["sct"], dkey="sct")
            b = mb()
            for j in range(4):
                c = c0 + j
                S.add("tensor", lambda e, b=b, j=j, c=c: e.transpose(bank(k, b)[:, j * 128:j * 128 + 3], sct[0:3, (c % 12) * 128:(c % 12 + 1) * 128], k.ident[0:3, 0:3]),
                      reads=["sct"], writes=[("ps", b)])
            S.add("scalar", lambda e, b=b, c0=c0: e.copy(scc[:, c0:c0 + 4, :], bank(k, b).rearrange("p (j t) -> p j t", j=4)[:, :, 0:3]),
                  reads=[("ps", b)], writes=[("scc", c) for c in range(c0, c0 + 4)])

    def conv_chunk(xb, cidx, outs, tail_only=False):
        if tail_only:
            S.add("scalar", lambda e, xb=xb, cidx=cidx: e.copy(k.ssd_tail[:, cidx, :], xbuf[xb][:, NM:NM + 3]),
                  reads=[("xb", xb)], writes=[("ssd_tail", cidx)])
            return
        S.add("vector", lambda e, xb=xb, cidx=cidx: e.tensor_copy(xbuf[xb][:, 0:3], k.ssd_tail[:, cidx, :]),
              reads=[("ssd_tail", cidx)], writes=[("xb", xb)])
        if last:
            S.add("vector", lambda e, xb=xb, cidx=cidx: e.tensor_copy(xbuf[xb][:, 3 + NM:6 + NM], scc[:, cidx, :]),
                  reads=[("scc", cidx)], writes=[("xb", xb)])
        W = XB - 3
        S.add("vector", lambda e, xb=xb, cidx=cidx: e.tensor_scalar(
            cacc[xb][:, 0:W], xbuf[xb][:, 0:W], k.sc_w[:, cidx, 0:1], k.sc_b[:, cidx:cidx + 1], ALU.mult, ALU.add),
            reads=[("xb", xb)], writes=[("ca", xb)])
        for kk in range(1, 4):
            S.add("vector", lambda e, xb=xb, cidx=cidx, kk=kk: e.scalar_tensor_tensor(
                cacc[xb][:, 0:W], xbuf[xb][:, kk:kk + W], k.sc_w[:, cidx, kk:kk + 1], cacc[xb][:, 0:W], ALU.mult, ALU.add),
                reads=[("xb", xb), ("ca", xb)], writes=[("ca", xb)])
        S.add("scalar", lambda e, xb=xb, cidx=cidx: e.copy(k.ssd_tail[:, cidx, :], xbuf[xb][:, NM:NM + 3]),
              reads=[("xb", xb)], writes=[("ssd_tail", cidx)])
        if last:
            S.add("scalar", lambda e, xb=xb, cidx=cidx: e.copy(sst[:, cidx, :], xbuf[xb][:, XB - 3:XB]),
                  reads=[("xb", xb)], writes=[("sst", cidx)])

    def conv_silu(xb, dst_fn, wkeys):
        parts = [(0, 0, NM)] + ([(NM, NM + 3, DEC)] if last else [])
        for (vo, ao, ln) in parts:
            S.add("scalar", lambda e, xb=xb, vo=vo, ao=ao, ln=ln: e.activation(out=dst_fn(vo, ln), in_=cacc[xb][:, ao:ao + ln], func=AF.Silu),
                  reads=[("ca", xb)], writes=wkeys)

    def GA(g):
        gp = g % 2
        for which in range(2):
            if which == 1 and mode == "state":
                continue
            dstT = (BT2, CT2)[which][gp]
            nm = (("BT", gp), ("CT", gp))[which]
            cidx = 32 + 8 * which + g
            xb = rot(k, "xb", 2)

            def epi_bc(i, ti, o, n, bks, xb=xb):
                b = bks[0]
                S.add("scalar", lambda e, b=b, xb=xb, o=o, n=n: e.copy(xbuf[xb][:, xoff(o):xoff(o) + n], bank(k, b, n)),
                      reads=[("ps", b)], writes=[("xb", xb)])
            multilinear(k, S, [(hT, "hT", DC, win, O_XBC + cidx * 128)], 1, TT, epi_bc, tag="sp", banks=(0, 1))
            if which == 1 and not full:
                conv_chunk(xb, cidx, None, tail_only=True)
                continue
            conv_chunk(xb, cidx, None)
            conv_silu(xb, lambda vo, ln, dstT=dstT: dstT[:, vo:vo + ln], [nm])
        BT, CT = BT2[gp], CT2[gp]
        for ch, (t0, ln) in enumerate(chunks):
            b = mb()
            S.add("tensor", lambda e, b=b, t0=t0, ln=ln: e.transpose(bank(k, b, dt=BF16)[0:ln, 0:128], BT[:, t0:t0 + ln], k.identb[:, :]),
                  reads=[("BT", gp)], writes=[("ps", b)])
            S.add("scalar", lambda e, b=b, ch=ch, ln=ln: e.copy(Btok2[gp][0:ln, ch, :], bank(k, b, dt=BF16)[0:ln, 0:128]),
                  reads=[("ps", b)], writes=[("Btok", gp, ch)])
            if not full:
                continue
            b2 = mb()
            S.add("tensor", lambda e, b2=b2, t0=t0, ln=ln: e.matmul(bank(k, b2)[0:ln, 0:ln], BT[:, t0:t0 + ln], CT[:, t0:t0 + ln], start=True, stop=True),
                  reads=[("BT", gp), ("CT", gp)], writes=[("ps", b2)])
            S.add("vector", lambda e, b2=b2, ch=ch, ln=ln: e.tensor_tensor(GmT2[gp][0:ln, ch, 0:ln], bank(k, b2)[0:ln, 0:ln], k.tri[0:ln, 0:ln], ALU.mult),
                  reads=[("ps", b2)], writes=[("GmT", gp, ch)])

    def UA(u):
        g, half = u // 2, u % 2
        up = u % 2
        cc0 = 4 * g + 2 * half
        h0 = 8 * g + 4 * half
        xs, zs, xdt, xw = xs2[up], zs2[up], xdt2[up], xw2[up]

        def epi_z(i, ti, o, n, bks):
            b = bks[0]
            S.add("scalar", lambda e, b=b, i=i, o=o, n=n: e.activation(out=zs[:, i, o:o + n], in_=bank(k, b, n), func=AF.Silu),
                  reads=[("ps", b)], writes=[("zs", up, i, ti)])
        if full:
            multilinear(k, S, [(hT, "hT", DC, win, O_Z + cc0 * 128)], 2, TT, epi_z, tag="sp", banks=(0, 1))
        xbs = {}

        def epi_x(i, ti, o, n, bks):
            b = bks[0]
            if ti == 0:
                xbs[i] = rot(k, "xb", 2)
            xb = xbs[i]
            S.add("scalar", lambda e, b=b, xb=xb, o=o, n=n: e.copy(xbuf[xb][:, xoff(o):xoff(o) + n], bank(k, b, n)),
                  reads=[("ps", b)], writes=[("xb", xb)])
            if ti == nT - 1:
                conv_chunk(xb, cc0 + i, None)
                conv_silu(xb, lambda vo, ln, i=i: xs[:, i, vo:vo + ln], [("xs", up, i)])
        multilinear(k, S, [(hT, "hT", DC, win, O_XBC + cc0 * 128)], 2, TT, epi_x, tag="sp", banks=(0, 1))

    def UA2(u):
        g, half = u // 2, u % 2
        up = u % 2
        h0 = 8 * g + 4 * half
        xs, xdt, xw = xs2[up], xdt2[up], xw2[up]
        for ch, (t0, ln) in enumerate(chunks):
            b = mb()
            for cl in range(2):
                S.add("tensor", lambda e, b=b, cl=cl, t0=t0, ln=ln: e.transpose(
                    bank(k, b)[0:ln, cl * 128:(cl + 1) * 128], xs[:, cl, t0:t0 + ln], k.ident[:, :]),
                    reads=[("xs", up, cl)], writes=[("ps", b)])
            for (dst, nm, j) in (((xdt, "xdt", 0), (xw, "xw", 1)) if full else ((xw, "xw", 1),)):
                S.add("vector", lambda e, b=b, dst=dst, j=j, ch=ch, ln=ln: e.tensor_tensor(
                    dst[0:ln, ch, :].rearrange("p (h q) -> p h q", h=4),
                    bank(k, b)[0:ln, 0:256].rearrange("p (h q) -> p h q", h=4),
                    tk[0:ln, ch, j, h0:h0 + 4].unsqueeze(2).to_broadcast([ln, 4, 64]), ALU.mult),
                    reads=[("ps", b), ("tk", ch)], writes=[(nm, up, ch)])

    def UB(u):
        if not full:
            return
        g, half = u // 2, u % 2
        up, gp = u % 2, g % 2
        h0 = 8 * g + 4 * half
        MT, CdT, CT, GmT = MT2[up], CdT2[up], CT2[gp], GmT2[gp]
        for hl in range(4):
            h = h0 + hl
            a = rot(k, "ar", 3)
            S.add("sync", lambda e, a=a, h=h: e.dma_start(out=ar[a][:, 0:TT], in_=k.acsd[h:h + 1, 0:TT].to_broadcast([128, TT])),
                  reads=["acsd"], writes=[("ar", a)], dkey=("ar", a))
            r = rot(k, "er", 2)
            S.add("scalar", lambda e, r=r, a=a: e.activation(out=er[r][:, 0:TT], in_=ar[a][:, 0:TT], func=AF.Exp),
                  reads=[("ar", a)], writes=[("er", r)])
            S.add("vector", lambda e, r=r, hl=hl: e.tensor_tensor(CdT[:, hl, 0:TT], CT[:, 0:TT], er[r][:, 0:TT], ALU.mult),
                  reads=[("er", r), ("CT", gp)], writes=[("CdT", up, hl, ti) for ti in range(nT)])
            for ch, (t0, ln) in enumerate(chunks):
                s_ = rot(k, "sg", 2)
                S.add("vector", lambda e, s_=s_, a=a, t0=t0, ln=ln, ch=ch, h=h: e.tensor_scalar(
                    sg[s_][0:ln, 0:ln], ar[a][0:ln, t0:t0 + ln], tk[0:ln, ch, 2, h:h + 1], 0.0, ALU.subtract, ALU.min),
                    reads=[("ar", a), ("tk", ch)], writes=[("sg", s_)])
                S.add("scalar", lambda e, s_=s_, ln=ln: e.activation(out=et[s_][0:ln, 0:ln], in_=sg[s_][0:ln, 0:ln], func=AF.Exp),
                      reads=[("sg", s_)], writes=[("et", s_)])
                S.add("vector", lambda e, s_=s_, hl=hl, ch=ch, ln=ln: e.tensor_tensor(MT[0:ln, hl, ch, 0:ln], et[s_][0:ln, 0:ln], GmT[0:ln, ch, 0:ln], ALU.mult),
                      reads=[("et", s_), ("GmT", gp, ch)], writes=[("MT", up, hl, ch)])

    def UC(u):
        g, half = u // 2, u % 2
        up, gp = u % 2, g % 2
        cc0 = 4 * g + 2 * half
        h0 = 8 * g + 4 * half
        xs, zs, xdt, xw, MT, CdT = xs2[up], zs2[up], xdt2[up], xw2[up], MT2[up], CdT2[up]
        Btok = Btok2[gp]
        yg = ygs[gp]

        def save_state(dst):
            s_ = rot(k, "stst", 2)
            b = mb()
            for pr in range(2):
                S.add("tensor", lambda e, b=b, pr=pr: e.transpose(
                    bank(k, b)[:, pr * 128:(pr + 1) * 128], ST[:, h0 + 2 * pr:h0 + 2 * pr + 2, :].rearrange("p h q -> p (h q)"), k.ident[:, :]),
                    reads=[("ST", u)], writes=[("ps", b)])
            S.add("scalar", lambda e, b=b, s_=s_: e.copy(stst[s_][:, :, :], bank(k, b)[:, 0:256].rearrange("p (j n) -> p j n", j=2)),
                  reads=[("ps", b)], writes=[("stst", s_)])
            S.add("sync", lambda e, s_=s_, dst=dst: e.dma_start(
                out=dst.rearrange("(pr q) n -> q pr n", q=128)[:, h0 // 2:h0 // 2 + 2, :], in_=stst[s_][:, :, :]),
                reads=[("stst", s_)], dkey=("stst", s_))

        for ch, (t0, ln) in enumerate(chunks):
            if last and ch == NCH - 1:
                save_state(k.dout["o_p_ssm"])
                s_ = rot(k, "cst", 2)
                S.add("sync", lambda e, s_=s_: e.dma_start(
                    out=cst[s_][:, :, :], in_=k.din["c_ssm"].rearrange("(pr q) n -> q pr n", q=128)[:, h0 // 2:h0 // 2 + 2, :]),
                    writes=[("cst", s_)], dkey=("cst", s_))
                b = mb()
                for pr in range(2):
                    S.add("tensor", lambda e, b=b, pr=pr, s_=s_: e.transpose(bank(k, b)[:, pr * 128:(pr + 1) * 128], cst[s_][:, pr, :], k.ident[:, :]),
                          reads=[("cst", s_)], writes=[("ps", b)])
                S.add("vector", lambda e, b=b: e.tensor_copy(ST[:, h0:h0 + 4, :].rearrange("p h q -> p (h q)"), bank(k, b)[:, 0:256]),
                      reads=[("ps", b)], writes=[("ST", u)])
                S.add("scalar", lambda e: e.copy(STb[:, h0:h0 + 4, :], ST[:, h0:h0 + 4, :]), reads=[("ST", u)], writes=[("STb", u)])
            bS = mb()
            S.add("tensor", lambda e, bS=bS, ch=ch, ln=ln: e.matmul(bank(k, bS, 256), Btok[0:ln, ch, :], xw[0:ln, ch, :], start=True, stop=True),
                  reads=[("Btok", gp, ch), ("xw", up, ch)], writes=[("ps", bS)])
            bY = 4 + rot(k, "psY", 2)
            for hl in (range(4) if full else ()):
                def outap(bY=bY, hl=hl, ln=ln):
                    return bank(k, bY)[(hl % 2) * 64:(hl % 2) * 64 + 64, (hl // 2) * 128:(hl // 2) * 128 + ln]
                S.add("tensor", lambda e, outap=outap, hl=hl, ch=ch, ln=ln: e.matmul(
                    outap(), xdt[0:ln, ch, hl * 64:(hl + 1) * 64], MT[0:ln, hl, ch, 0:ln], start=True, stop=False),
                    reads=[("xdt", up, ch), ("MT", up, hl, ch)], writes=[("ps", bY)])
                S.add("tensor", lambda e, outap=outap, hl=hl, t0=t0, ln=ln: e.matmul(
                    outap(), STb[:, h0 + hl, :], CdT[:, hl, t0:t0 + ln], start=False, stop=True),
                    reads=[("STb", u), ("CdT", up, hl, t0 // 512)], writes=[("ps", bY)])
            for cl in (range(2) if full else ()):
                S.add("vector", lambda e, bY=bY, cl=cl, t0=t0, ln=ln: e.scalar_tensor_tensor(
                    yg[:, 2 * half + cl, t0:t0 + ln], xs[:, cl, t0:t0 + ln], k.dexp[:, cc0 + cl:cc0 + cl + 1],
                    bank(k, bY)[:, cl * 128:cl * 128 + ln], ALU.mult, ALU.add),
                    reads=[("ps", bY), ("xs", up, cl)], writes=[("yg", gp, 2 * half + cl, ch)])
                S.add("vector", lambda e, cl=cl, t0=t0, ln=ln: e.tensor_tensor(
                    yg[:, 2 * half + cl, t0:t0 + ln], yg[:, 2 * half + cl, t0:t0 + ln], zs[:, cl, t0:t0 + ln], ALU.mult),
                    reads=[("yg", gp, 2 * half + cl, ch), ("zs", up, cl, t0 // 512)], writes=[("yg", gp, 2 * half + cl, ch)])
            stv = ST[:, h0:h0 + 4, :]
            S.add("vector", lambda e, stv=stv, ch=ch: e.tensor_tensor(stv, stv, Edec[:, ch, h0:h0 + 4].unsqueeze(2).to_broadcast([128, 4, 64]), ALU.mult),
                  reads=[("ST", u), "Edec"], writes=[("ST", u)])
            S.add("vector", lambda e, stv=stv, bS=bS: e.tensor_tensor(stv, stv, bank(k, bS, 256).rearrange("p (h q) -> p h q", h=4), ALU.add),
                  reads=[("ST", u), ("ps", bS)], writes=[("ST", u)])
            if full:
                S.add("scalar", lambda e, stv=stv: e.copy(STb[:, h0:h0 + 4, :], stv), reads=[("ST", u)], writes=[("STb", u)])
        if last:
            save_state(k.dout["o_s_ssm"])

    def GN(g):
        if not full:
            return
        gp = g % 2
        yg = ygs[gp]
        rk = [("yg", gp, c4, ch) for c4 in range(4) for ch in range(NCH)]
        for ti, (o, n) in enumerate(tl):
            b = mb()
            for c4 in range(4):
                s_ = rot(k, "sqn", 2)
                S.add("scalar", lambda e, s_=s_, c4=c4, o=o, n=n: e.activation(out=sqn[s_][:, 0:n], in_=yg[:, c4, o:o + n], func=AF.Square),
                      reads=rk, writes=[("sqn", s_)])
                S.add("tensor", lambda e, b=b, s_=s_, c4=c4, n=n: e.matmul(bank(k, b, n), k.ones512[:, :], sqn[s_][:, 0:n], start=(c4 == 0), stop=(c4 == 3)),
                      reads=[("sqn", s_)], writes=[("ps", b)])
            S.add("scalar", lambda e, b=b, n=n: e.activation(out=rsn[:, 0:n], in_=bank(k, b, n), func=AF.Sqrt, bias=k.eps[:, 0:1]),
                  reads=[("ps", b)], writes=["rsn"])
            S.add("vector", lambda e, n=n: e.reciprocal(rsn[:, 0:n], rsn[:, 0:n]), reads=["rsn"], writes=["rsn"])
            for c4 in range(4):
                S.add("vector", lambda e, c4=c4, o=o, n=n: e.scalar_tensor_tensor(
                    yg[:, c4, o:o + n], yg[:, c4, o:o + n], k.ssdn[:, 4 * g + c4:4 * g + c4 + 1], rsn[:, 0:n], ALU.mult, ALU.mult),
                    reads=rk + ["rsn"], writes=[("ygn", gp, c4, ti)])
        S.add("sync", lambda e: e.dma_start(out=k.ynd[:, 4 * g:4 * g + 4, T0:T0 + TT], in_=yg[:, :, :]),
              reads=[("ygn", gp, c4, ti) for c4 in range(4) for ti in range(nT)] + rk, dkey=("ynb", gp))

    NU = 2 * NG
    GA(0)
    UA(0)
    UA2(0)
    UB(0)
    for u in range(NU):
        if u + 1 < NU:
            if (u + 1) % 2 == 0:
                GA((u + 1) // 2)
            UA(u + 1)
        UC(u)
        if u + 1 < NU:
            UA2(u + 1)
            UB(u + 1)
        if u % 2 == 0 and u >= 2:
            GN((u - 2) // 2)
    GN(NG - 1)

    if maskcol is not None:
        S.add("vector", lambda e: e.tensor_single_scalar(ST[:], ST[:], k.pm[:, maskcol:maskcol + 1], ALU.mult),
              reads=[("ST", u) for u in range(16)], writes=[("ST", u) for u in range(16)])
    if not last:
        S.add("sync", lambda e: e.dma_start(out=k.sts[:, :].rearrange("p (h q) -> p h q", h=NH), in_=ST[:]),
              reads=[("ST", u) for u in range(16)], dkey="stsave")
    else:
        for (src, sname, dst) in ((k.ssd_tail, "ssd_tail", k.dout["o_p_ssdconv"]), (sst, "sst", k.dout["o_s_ssdconv"])):
            if "ssd_ot" not in k.cache:
                k.cache["ssd_ot"] = sb(k, "ssd_ot", [8, 1536], F32)
            ot = k.cache["ssd_ot"]
            for c0 in range(0, 48, 4):
                b = mb()
                for j in range(4):
                    c = c0 + j
                    S.add("tensor", lambda e, b=b, j=j, c=c, src=src: e.transpose(bank(k, b)[0:3, j * 128:(j + 1) * 128], src[:, c, :], k.ident[:, :]),
                          reads=[(sname, c)], writes=[("ps", b)])
                S.add("scalar", lambda e, b=b, c0=c0, ot=ot: e.copy(ot[0:3, (c0 % 12) * 128:(c0 % 12 + 4) * 128], bank(k, b)[0:3, :]),
                      reads=[("ps", b)], writes=["ssd_ot"])
                if c0 % 12 == 8:
                    S.add("sync", lambda e, ot=ot, dst=dst, c0=c0: e.dma_start(out=dst[:, (c0 - 8) * 128:(c0 - 8) * 128 + 1536], in_=ot[0:3, :]),
                          reads=["ssd_ot"], dkey="ssd_ot")


def merge_phase(k, S, TT):
    hT = k.hT
    yn = sb(k, "mg_yn", [128, 32, TT], BF16)
    for q in range(4):
        S.add("sync", lambda e, q=q: e.dma_start(out=yn[:, 8 * q:8 * q + 8, :], in_=k.ynd[:, 8 * q:8 * q + 8, 0:TT]),
              writes=[("yn", kc, ti) for kc in range(8 * q, 8 * q + 8) for ti in allt(TT)], dkey=("ynl", q))
    sgt = [sb(k, "mg_sg%d" % i, [128, 512], F32) for i in range(2)]
    bt = [sb(k, "mg_bt%d" % i, [128, 512], F32) for i in range(2)]
    mt = [sb(k, "mg_mt%d" % i, [128, 512], BF16) for i in range(2)]

    def epi(i, ti, o, n, bks):
        bo, bg = bks
        ss = rot(k, "mg_sg", 2)
        S.add("sync", lambda e, ss=ss, i=i, o=o, n=n: e.dma_start(out=bt[ss][:, 0:n], in_=k.Bs[:, i, o:o + n]),
              writes=[("mg_bt", ss)], dkey=("mg_bt", ss))
        S.add("scalar", lambda e, ss=ss, bg=bg, n=n: e.activation(out=sgt[ss][:, 0:n], in_=bank(k, bg, n), func=AF.Sigmoid),
              reads=[("ps", bg)], writes=[("mg_sg", ss)])
        S.add("vector", lambda e, ss=ss, bo=bo, n=n: e.tensor_tensor(sgt[ss][:, 0:n], bank(k, bo, n), sgt[ss][:, 0:n], ALU.mult),
              reads=[("ps", bo), ("mg_sg", ss)], writes=[("mg_sg", ss)])
        ms = rot(k, "mg_mt", 2)
        S.add("vector", lambda e, ss=ss, ms=ms, n=n: e.tensor_tensor(mt[ms][:, 0:n], sgt[ss][:, 0:n], bt[ss][:, 0:n], ALU.add),
              reads=[("mg_sg", ss), ("mg_bt", ss)], writes=[("mg_mt", ms)])
        S.add("sync", lambda e, ms=ms, i=i, o=o, n=n: e.dma_start(out=k.mts[:, i, o:o + n], in_=mt[ms][:, 0:n]),
              reads=[("mg_mt", ms)], dkey=("mg_mt", ms))

    multilinear(k, S, [(yn, "yn", 32, k.din["ssd_w_out"], 0), (hT, "hT", DC, k.din["w_in"], O_GS)], DC, TT, epi, tag="mg")


def attn_phase(k, S, TT, last, xT):
    hT = k.hT
    tl = tiles_of(TT)
    aT = sb(k, "at_aT", [128, DC, TT], BF16)
    S.add("sync", lambda e: e.dma_start(out=xT[:], in_=k.x1s[:, :, 0:TT]), writes=[("xT", c, ti) for c in range(DC) for ti in allt(TT)], dkey="x1l")
    S.add("sync", lambda e: e.dma_start(out=aT[:], in_=k.mts[:, :, 0:TT]), writes=[("aT", c, ti) for c in range(DC) for ti in allt(TT)], dkey="mtl")

    def epi_add(i, ti, o, n, bks):
        b = bks[0]
        S.add("vector", lambda e, b=b, i=i, o=o, n=n: e.tensor_tensor(xT[:, i, o:o + n], xT[:, i, o:o + n], bank(k, b, n), ALU.add),
              reads=[("ps", b), ("xT", i, ti)], writes=[("xT", i, ti)])
    multilinear(k, S, [(aT, "aT", DC, k.din["w_mix_out"], 0)], DC, TT, epi_add, tag="lin")
    rmsnorm(k, S, xT, hT, k.gv["xa_norm"], TT)

    def epi_q(i, ti, o, n, bks):
        b = bks[0]
        S.add("scalar", lambda e, b=b, i=i, o=o, n=n: e.copy(aT[:, i, o:o + n], bank(k, b, n)),
              reads=[("ps", b)], writes=[("aT", i, ti)])
    multilinear(k, S, [(hT, "hT", DC, k.din["xa_wq"], 0)], DC, TT, epi_q, tag="lin")

    kT1 = sb(k, "at_kT", [128, DC, NMEM], BF16)
    V1 = sb(k, "at_V", [128, 2, D], BF16)
    kT = [kT1, kT1]
    V = [V1, V1]
    S.add("sync", lambda e: e.dma_start(out=kT1[:], in_=k.kTs[0][:, :, :]), writes=["kT"], dkey="kTl")
    S.add("sync", lambda e: e.dma_start(out=V1[:], in_=k.Vs[0][:, :, :]), writes=["V"], dkey="Vl")
    PT = sb(k, "at_PT", [128, 2, 4, 512], BF16)
    pe = [sb(k, "at_pe%d" % i, [128, 2, NMEM], BF16) for i in range(2)]
    mx = [sb(k, "at_mx%d" % i, [128, 4], F32) for i in range(2)]
    rs = [sb(k, "at_rs%d" % i, [128, 4], F32) for i in range(2)]
    scale = float(512 ** -0.5)
    for ti, (o, n) in enumerate(tl):
        kv = 1 if (last and o >= SL) else 0
        if kv == 1:
            S.add("sync", lambda e: e.dma_start(out=kT1[:], in_=k.kTs[1][:, :, :]), writes=["kT"], dkey="kTl")
            S.add("sync", lambda e: e.dma_start(out=V1[:], in_=k.Vs[1][:, :, :]), writes=["V"], dkey="Vl")
        for s0 in range(0, n, 128):
            sn = min(128, n - s0)
            st_ = rot(k, "at_st", 2)
            for hp in range(2):
                b = 4 + rot(k, "at_sb", 2)
                for h2 in range(2):
                    h = 2 * hp + h2
                    for dc in range(4):
                        c = 4 * h + dc
                        S.add("tensor", lambda e, b=b, h2=h2, c=c, o=o, s0=s0, sn=sn, kv=kv, dc=dc: e.matmul(
                            bank(k, b)[0:sn, h2 * 256:(h2 + 1) * 256], aT[:, c, o + s0:o + s0 + sn], kT[kv][:, c, :],
                            start=(dc == 0), stop=(dc == 3)),
                            reads=[("aT", c, ti), "kT"], writes=[("ps", b)])
                S.add("vector", lambda e, b=b, st_=st_, hp=hp, sn=sn: e.tensor_reduce(
                    mx[st_][0:sn, 2 * hp:2 * hp + 2], bank(k, b)[0:sn, :].rearrange("p (h m) -> p h m", h=2), mybir.AxisListType.X, ALU.max),
                    reads=[("ps", b)], writes=[("at_mx", st_, hp)])
                S.add("vector", lambda e, st_=st_, hp=hp, sn=sn: e.tensor_single_scalar(
                    mx[st_][0:sn, 2 * hp:2 * hp + 2], mx[st_][0:sn, 2 * hp:2 * hp + 2], -scale, ALU.mult),
                    reads=[("at_mx", st_, hp)], writes=[("at_mx", st_, hp)])
                ps_ = rot(k, "at_pe", 2)
                for h2 in range(2):
                    S.add("scalar", lambda e, b=b, ps_=ps_, h2=h2, st_=st_, hp=hp, sn=sn: e.activation(
                        out=pe[ps_][0:sn, h2, :], in_=bank(k, b)[0:sn, h2 * 256:(h2 + 1) * 256], func=AF.Exp,
                        bias=mx[st_][0:sn, 2 * hp + h2:2 * hp + h2 + 1], scale=scale,
                        accum_out=rs[st_][0:sn, 2 * hp + h2:2 * hp + h2 + 1]),
                        reads=[("ps", b), ("at_mx", st_, hp)], writes=[("at_pe", ps_), ("at_rs", st_, hp)])
                S.add("vector", lambda e, st_=st_, hp=hp, sn=sn: e.reciprocal(rs[st_][0:sn, 2 * hp:2 * hp + 2], rs[st_][0:sn, 2 * hp:2 * hp + 2]),
                      reads=[("at_rs", st_, hp)], writes=[("at_rs", st_, hp)])
                for h2 in range(2):
                    S.add("vector", lambda e, ps_=ps_, h2=h2, st_=st_, hp=hp, sn=sn: e.tensor_single_scalar(
                        pe[ps_][0:sn, h2, :], pe[ps_][0:sn, h2, :], rs[st_][0:sn, 2 * hp + h2:2 * hp + h2 + 1], ALU.mult),
                        reads=[("at_pe", ps_), ("at_rs", st_, hp)], writes=[("at_pe", ps_)])
                bt_ = 6 + rot(k, "misc", 2)
                for h2 in range(2):
                    for mc in range(2):
                        S.add("tensor", lambda e, bt_=bt_, ps_=ps_, h2=h2, mc=mc, sn=sn: e.transpose(
                            bank(k, bt_, dt=BF16)[:, (h2 * 2 + mc) * 128:(h2 * 2 + mc) * 128 + sn], pe[ps_][0:sn, h2, mc * 128:(mc + 1) * 128],
                            k.identb[0:sn, 0:sn]),
                            reads=[("at_pe", ps_)], writes=[("ps", bt_)])
                for h2 in range(2):
                    S.add("scalar", lambda e, bt_=bt_, h2=h2, hp=hp, s0=s0, sn=sn: e.copy(
                        PT[:, :, 2 * hp + h2, s0:s0 + sn], bank(k, bt_, dt=BF16)[:, h2 * 256:(h2 + 1) * 256].rearrange("p (mc t) -> p mc t", mc=2)[:, :, 0:sn]),
                        reads=[("ps", bt_)], writes=[("PT", 2 * hp + h2)])
        for c in range(DC):
            h = c // 4
            b = rot(k, "at_ob", 4)
            for mc in range(2):
                S.add("tensor", lambda e, b=b, c=c, mc=mc, h=h, n=n, kv=kv: e.matmul(
                    bank(k, b, n), V[kv][:, mc, c * 128:(c + 1) * 128], PT[:, mc, h, 0:n], start=(mc == 0), stop=(mc == 1)),
                    reads=["V", ("PT", h)], writes=[("ps", b)])
            S.add("scalar", lambda e, b=b, c=c, o=o, n=n: e.copy(hT[:, c, o:o + n], bank(k, b, n)),
                  reads=[("ps", b)], writes=[("hT", c, ti)])
    multilinear(k, S, [(hT, "hT", DC, k.din["xa_wo"], 0)], DC, TT, epi_add, tag="lin")


def final_out(k, S, TT, last, xT, sl):
    sq = k.cache["nsq"]
    rstd = k.cache["nrs"][0]
    yt = [sb(k, "fn_yt%d" % i, [128, 128], F32) for i in range(2)]
    yo = [sb(k, "fn_yo%d" % i, [128, D], F32) for i in range(1)]
    for ti, (o, n) in enumerate(tiles_of(TT)):
        b = 7
        for c in range(DC):
            s = rot(k, "nsq", 2)
            S.add("scalar", lambda e, s=s, c=c, o=o, n=n: e.activation(out=sq[s][:, 0:n], in_=xT[:, c, o:o + n], func=AF.Square),
                  reads=[("xT", c, ti)], writes=[("nsq", s)])
            S.add("tensor", lambda e, s=s, c=c, n=n: e.matmul(bank(k, b, n), k.onesD[:, :], sq[s][:, 0:n], start=(c == 0), stop=(c == DC - 1)),
                  reads=[("nsq", s)], writes=[("ps", b)])
        S.add("scalar", lambda e, n=n: e.activation(out=rstd[:, 0:n], in_=bank(k, b, n), func=AF.Sqrt, bias=k.eps[:, 0:1]),
              reads=[("ps", b)], writes=[("nrs", 0)])
        S.add("vector", lambda e, n=n: e.reciprocal(rstd[:, 0:n], rstd[:, 0:n]), reads=[("nrs", 0)], writes=[("nrs", 0)])
        for s0 in range(0, n, 128):
            sn = min(128, n - s0)
            ys = rot(k, "fn_yo", 1)
            for c0 in range(0, DC, 4):
                bt_ = 4 + rot(k, "fn_tb", 2)
                for j in range(4):
                    c = c0 + j
                    t = rot(k, "fn_yt", 2)
                    S.add("vector", lambda e, t=t, c=c, o=o, s0=s0, sn=sn: e.scalar_tensor_tensor(
                        yt[t][:, 0:sn], xT[:, c, o + s0:o + s0 + sn], k.gv["final_norm"][:, c:c + 1], rstd[:, s0:s0 + sn], ALU.mult, ALU.mult),
                        reads=[("xT", c, ti), ("nrs", 0)], writes=[("fn_yt", t)])
                    S.add("tensor", lambda e, bt_=bt_, j=j, t=t, sn=sn: e.transpose(bank(k, bt_)[0:sn, j * 128:(j + 1) * 128], yt[t][:, 0:sn], k.ident[:, :]),
                          reads=[("fn_yt", t)], writes=[("ps", bt_)])
                S.add("scalar", lambda e, bt_=bt_, ys=ys, c0=c0, sn=sn: e.copy(yo[ys][0:sn, c0 * 128:(c0 + 4) * 128], bank(k, bt_)[0:sn, :]),
                      reads=[("ps", bt_)], writes=[("fn_yo", ys)])
            tok = o + s0
            if tok < SL:
                dst = k.dout["y_p"][sl * SL + tok:sl * SL + tok + sn, :]
            else:
                dst = k.dout["y_s"][tok - SL:tok - SL + sn, :]
            S.add("sync", lambda e, ys=ys, dst=dst, sn=sn: e.dma_start(out=dst, in_=yo[ys][0:sn, :]), reads=[("fn_yo", ys)], dkey=("fn_yo", ys))


def kv_phase(k, S):
    mT = sb(k, "kv_mT", [128, DC, NMEM], F32)
    mh = sb(k, "kv_mh", [128, DC, NMEM], BF16)
    load_xT(k, S, mT, [(k.din["mem"], 0, NMEM)])
    rmsnorm(k, S, mT, mh, k.gv["mem_norm"], NMEM, xname="xT", hname="mh")
    import os
    KVS = int(os.environ.get("KV_STOP", "9"))
    if KVS <= 1:
        return
    wst = [sb(k, "kv_w%d" % i, [128, DC, 512], BF16) for i in range(2)]
    of = [sb(k, "kv_of%d" % i, [128, 512], F32) for i in range(2)]
    ktok = [sb(k, "kv_kt%d" % i, [128, 2, D], BF16) for i in range(2)]
    kTb = sb(k, "kv_kT", [128, DC, NMEM], BF16)
    cin = [sb(k, "kv_cin%d" % i, [128, D], F32) for i in range(2)]
    for which, (wname, oname) in enumerate((("xa_wk", "o_p_k"), ("xa_wv", "o_p_v"))):
        Wv = k.din[wname].rearrange("(kc p) n -> p kc n", p=128)
        for nt in range(4):
            ws = rot(k, "kv_w", 2)
            for hf in range(2):
                S.add("gpsimd", lambda e, ws=ws, Wv=Wv, nt=nt, hf=hf: e.dma_start(
                    out=wst[ws][:, :, hf * 256:(hf + 1) * 256], in_=Wv[:, :, nt * 512 + hf * 256:nt * 512 + (hf + 1) * 256]),
                    writes=[("kv_w", ws)], dkey=("kv_w", ws, hf))
            for mc in range(2):
                b = rot(k, "kv_b", 4)
                for kc in range(DC):
                    S.add("tensor", lambda e, b=b, ws=ws, kc=kc, mc=mc: e.matmul(bank(k, b), mh[:, kc, mc * 128:(mc + 1) * 128], wst[ws][:, kc, :],
                                                                          start=(kc == 0), stop=(kc == DC - 1)),
                          reads=[("kv_w", ws), ("mh", kc, 0)], writes=[("ps", b)])
                os_ = rot(k, "kv_of", 2)
                S.add("vector", lambda e, b=b, os_=os_: e.tensor_copy(of[os_][:, :], bank(k, b)), reads=[("ps", b)], writes=[("kv_of", os_)])
                S.add("vector", lambda e, b=b, which=which, mc=mc, nt=nt: e.tensor_copy(ktok[which][:, mc, nt * 512:(nt + 1) * 512], bank(k, b)),
                      reads=[("ps", b)], writes=[("kv_kt", which)])
                S.add("sync", lambda e, os_=os_, oname=oname, mc=mc, nt=nt: e.dma_start(
                    out=k.dout[oname][mc * 128:(mc + 1) * 128, nt * 512:(nt + 1) * 512], in_=of[os_][:, :]),
                    reads=[("kv_of", os_)], dkey=("kv_of", os_))

    if KVS <= 2:
        return

    def make_kT(src, sname, dst_kT, dst_V, vsrc, vname):
        for mc in range(2):
            for c0 in range(0, DC, 8):
                b = 4 + rot(k, "tpb", 4)
                for j in range(8):
                    c = c0 + j
                    S.add("tensor", lambda e, b=b, j=j, c=c, mc=mc, src=src: e.transpose(
                        bank(k, b, dt=BF16)[:, j * 128:(j + 1) * 128], src[:, mc, c * 128:(c + 1) * 128], k.identb[:, :]),
                        reads=[sname], writes=[("ps", b)])
                S.add("scalar", lambda e, b=b, c0=c0, mc=mc: e.copy(
                    kTb[:, c0:c0 + 8, mc * 128:(mc + 1) * 128], bank(k, b, dt=BF16).rearrange("p (j t) -> p j t", j=8)),
                    reads=[("ps", b)], writes=["kTb"])
        S.add("sync", lambda e: e.dma_start(out=dst_kT[:, :, :], in_=kTb[:]), reads=["kTb"], dkey=("kTst", sname if isinstance(sname, str) else str(sname)))
        S.add("sync", lambda e: e.dma_start(out=dst_V[:, :, :], in_=vsrc[:]), reads=[vname], dkey=("Vst", str(vname)))

    make_kT(ktok[0], ("kv_kt", 0), k.kTs[0], k.Vs[0], ktok[1], ("kv_kt", 1))
    if KVS <= 3:
        return
    ck = [sb(k, "kv_ck%d" % i, [128, 2, D], BF16) for i in range(2)]
    for which, nm in enumerate(("c_k", "c_v")):
        for mc in range(2):
            cs = rot(k, "kv_cin", 2)
            S.add("sync", lambda e, cs=cs, nm=nm, mc=mc: e.dma_start(out=cin[cs][:, :], in_=k.din[nm][mc * 128:(mc + 1) * 128, :]),
                  writes=[("kv_cin", cs)], dkey=("kv_cin", cs))
            S.add("vector", lambda e, cs=cs, which=which, mc=mc: e.tensor_copy(ck[which][:, mc, :], cin[cs][:, :]),
                  reads=[("kv_cin", cs)], writes=[("kv_ck", which)])
    make_kT(ck[0], ("kv_ck", 0), k.kTs[1], k.Vs[1], ck[1], ("kv_ck", 1))


def dump(k, name, src):
    d = k.nc.dram_tensor("dbg_" + name, list(src.shape), src.dtype, kind="ExternalOutput").ap()
    with phase(k) as S:
        S.add("sync", lambda e: e.dma_start(out=d, in_=src), dkey="dump")


def build(nsl=NSL, dbg=None, stop=None):
    nc = bass.Bass("TRN2", target_bir_lowering=False)
    k = K()
    k.nc = nc
    k.din, k.dout = {}, {}

    def din(name, shape):
        k.din[name] = nc.dram_tensor(name, list(shape), F32, kind="ExternalInput").ap()

    def dout(name, shape):
        k.dout[name] = nc.dram_tensor(name, list(shape), F32, kind="ExternalOutput").ap()

    for nm, shp in IN_SHAPES.items():
        din(nm, shp)
    for nm, shp in OUT_SHAPES.items():
        dout(nm, shp)
    TM = SL + DEC
    k.x1s = nc.dram_tensor("x1s", [128, DC, TM], F32).ap()
    k.Bs = nc.dram_tensor("Bs", [128, DC, TM], F32).ap()
    k.mts = nc.dram_tensor("mts", [128, DC, TM], BF16).ap()
    k.ynd = nc.dram_tensor("ynd", [128, 32, TM], BF16).ap()
    k.lns = nc.dram_tensor("lns", [128, DC, TM], BF16).ap()
    k.acsd = nc.dram_tensor("acsd", [64, TM], F32).ap()
    k.sts = nc.dram_tensor("sts", [128, NH * HP], F32).ap()
    k.kTs = [nc.dram_tensor("kTs%d" % i, [128, DC, NMEM], BF16).ap() for i in range(2)]
    k.Vs = [nc.dram_tensor("Vs%d" % i, [128, 2, D], BF16).ap() for i in range(2)]

    k.ps = nc.alloc_psum_tensor("ps", [128, 8 * 512], F32).ap()
    cshapes = {"ident": [128, 128], "onesD": [128, 128], "ones512": [128, 128], "tri": [128, 128], "ones64": [64, 128]}
    for nm, shp in cshapes.items():
        setattr(k, nm, sb(k, "c_" + nm, shp, F32))
    k.identb = sb(k, "c_identb", [128, 128], BF16)
    k.eps = sb(k, "c_eps", [128, 1], F32)
    k.gv = {nm: sb(k, "g_" + nm, [128, DC], F32) for nm in NORMS}
    k.cm_w = sb(k, "c_cm_w", [128, DC, CMW], F32)
    k.cm_b = sb(k, "c_cm_b", [128, DC], F32)
    k.cm_g = sb(k, "c_cm_g", [128, DC], F32)
    k.cm_lb = sb(k, "c_cm_lb", [128, DC], F32)
    k.sc_w = sb(k, "c_sc_w", [128, 48, 4], F32)
    k.sc_b = sb(k, "c_sc_b", [128, 48], F32)
    k.ssdn = sb(k, "c_ssdn", [128, 32], F32)
    k.dexp = sb(k, "c_dexp", [128, 32], F32)
    k.dtb = sb(k, "c_dtb", [64, 1], F32)
    k.acol = sb(k, "c_acol", [64, 8], F32)
    k.pm = sb(k, "c_pm", [128, 4], F32)
    k.cm_tail = sb(k, "p_cm_tail", [128, DC, 30], F32)
    k.ssd_tail = sb(k, "p_ssd_tail", [128, 48, 3], F32)
    k.hT = sb(k, "p_hT", [128, DC, TM], BF16)

    with phase(k) as S:
        i = 0
        for nm in list(cshapes) + ["cm_w", "cm_b", "cm_g", "cm_lb", "sc_w", "sc_b", "ssdn", "dexp", "dtb", "acol", "pm"]:
            t = getattr(k, nm)
            src = k.din["v_" + nm] if nm not in cshapes else k.din["k_" + nm]
            S.add("sync", lambda e, t=t, src=src: e.dma_start(out=t[:], in_=src), writes=[nm], dkey="c%d" % i)
            i += 1
        for nm in NORMS:
            S.add("sync", lambda e, nm=nm: e.dma_start(out=k.gv[nm][:], in_=k.din["v_" + nm][:, :]), writes=["g_" + nm], dkey="c%d" % i)
            i += 1
        S.add("vector", lambda e: e.memset(k.eps[:], EPS), writes=["eps"])
        S.add("vector", lambda e: e.tensor_copy(k.identb[:], k.ident[:]), reads=["ident"], writes=["identb"])
        S.add("vector", lambda e: e.memset(k.cm_tail[:], 0.0), writes=["cmt"])
        S.add("vector", lambda e: e.memset(k.ssd_tail[:], 0.0), writes=["sst"])
        S.add("scalar", lambda e: e.activation(out=k.acol[:], in_=k.acol[:], func=AF.Exp), reads=["acol"], writes=["acol"])
        S.add("vector", lambda e: e.tensor_single_scalar(k.acol[:], k.acol[:], -1.0, ALU.mult), reads=["acol"], writes=["acol"])
    if stop == "consts":
        return nc
    with phase(k) as S:
        kv_phase(k, S)
    if stop == "kv":
        return nc

    NPRE = nsl - 1
    for j in range(NPRE):
        with phase(k) as S:
            xT = sb(k, "xT", [128, DC, SL], F32)
            load_xT(k, S, xT, [(k.din["xp"][j * SL:(j + 1) * SL, :], 0, SL)])
            rmsnorm(k, S, xT, k.hT, k.gv["ffn1_norm"], SL)
            ffn(k, S, xT, k.hT, k.din["ffn1_wi"], k.din["ffn1_wo"], SL)
            rmsnorm(k, S, xT, k.hT, k.gv["mix_norm"], SL)
        lastpre = (j == NPRE - 1)
        for sub in range(2):
            with phase(k) as S:
                ssd_branch(k, S, sub * 512, 512, False, j == 0 and sub == 0,
                           mode=("state_tail" if lastpre else "state"), maskcol=(j if sub == 1 else None))
        if lastpre:
            with phase(k) as S:
                cm_tail_only(k, S)
    if stop == "prefix":
        dump(k, "sts", k.sts)
        return nc
    sl = NPRE
    if True:
        last = True
        first = (NPRE == 0)
        TT = TM
        srcs = [(k.din["xp"][sl * SL:(sl + 1) * SL, :], 0, SL), (k.din["xs"], SL, DEC)]
        with phase(k) as S:
            xT = sb(k, "xT", [128, DC, TT], F32)
            load_xT(k, S, xT, srcs)
            rmsnorm(k, S, xT, k.hT, k.gv["ffn1_norm"], TT)
            ffn(k, S, xT, k.hT, k.din["ffn1_wi"], k.din["ffn1_wo"], TT)
            rmsnorm(k, S, xT, k.hT, k.gv["mix_norm"], TT)
            S.add("sync", lambda e: e.dma_start(out=k.x1s[:, :, 0:TT], in_=xT[:]),
                  reads=[("xT", c, ti) for c in range(DC) for ti in allt(TT)], dkey="x1s")
        if stop == "ffn1":
            dump(k, "x1s", k.x1s)
            return nc
        with phase(k) as S:
            cm_branch(k, S, TT, last)
        with phase(k) as S:
            cm_branch_b(k, S, TT)
        if stop == "cm":
            dump(k, "Bs", k.Bs)
            dump(k, "lns", k.lns)
            return nc
        for sub in range(2):
            with phase(k) as S:
                ssd_branch(k, S, sub * 512, 512, last and sub == 1, first and sub == 0)
        if stop == "ssd":
            dump(k, "ynd", k.ynd)
            return nc
        with phase(k) as S:
            merge_phase(k, S, TT)
        if stop == "merge":
            dump(k, "mts", k.mts)
            return nc
        with phase(k) as S:
            xT = sb(k, "xT", [128, DC, TT], F32)
            attn_phase(k, S, TT, last, xT)
            S.add("sync", lambda e: e.dma_start(out=k.x1s[:, :, 0:TT], in_=xT[:]),
                  reads=[("xT", c, ti) for c in range(DC) for ti in allt(TT)], dkey="x3s")
        if stop == "attn":
            dump(k, "x3s", k.x1s)
            return nc
        with phase(k) as S:
            xT = sb(k, "xT", [128, DC, TT], F32)
            S.add("sync", lambda e: e.dma_start(out=xT[:], in_=k.x1s[:, :, 0:TT]),
                  writes=[("xT", c, ti) for c in range(DC) for ti in allt(TT)], dkey="x3l")
            rmsnorm(k, S, xT, k.hT, k.gv["ffn2_norm"], TT)
            ffn(k, S, xT, k.hT, k.din["ffn2_wi"], k.din["ffn2_wo"], TT)
            final_out(k, S, TT, last, xT, 0)
    return nc


NORMS = ["ffn1_norm", "mix_norm", "xa_norm", "mem_norm", "ffn2_norm", "final_norm"]
IN_SHAPES = {
    "xp": [SEQ, D], "xs": [DEC, D], "mem": [NMEM, D], "c_ssdconv": [3, XBC], "c_ssm": [NH * HP, NS], "c_cmconv": [30, D],
    "c_k": [NMEM, D], "c_v": [NMEM, D],
    "ffn1_wi": [D, 2 * DFF], "ffn1_wo": [DFF, D], "w_in": [D, INC], "ssd_w_out": [DIN, D], "cm_w_out": [D, D], "w_mix_out": [D, D],
    "xa_wq": [D, D], "xa_wk": [D, D], "xa_wv": [D, D], "xa_wo": [D, D], "ffn2_wi": [D, 2 * DFF], "ffn2_wo": [DFF, D],
    "k_ident": [128, 128], "k_onesD": [128, 128], "k_ones512": [128, 128], "k_tri": [128, 128], "k_ones64": [64, 128],
    "v_cm_w": [128, DC, CMW], "v_cm_b": [128, DC], "v_cm_g": [128, DC], "v_cm_lb": [128, DC], "v_sc_w": [128, 48, 4], "v_sc_b": [128, 48],
    "v_ssdn": [128, 32], "v_dexp": [128, 32], "v_dtb": [64, 1], "v_acol": [64, 8], "v_pm": [128, 4],
}
for _n in NORMS:
    IN_SHAPES["v_" + _n] = [128, DC]
OUT_SHAPES = {
    "y_p": [SL, D], "y_s": [DEC, D], "o_p_ssdconv": [3, XBC], "o_p_ssm": [NH * HP, NS], "o_p_cm": [30, D], "o_p_k": [NMEM, D], "o_p_v": [NMEM, D],
    "o_s_ssdconv": [3, XBC], "o_s_ssm": [NH * HP, NS], "o_s_cm": [30, D],
}


def col_layout(v):
    return np.ascontiguousarray(np.asarray(v, np.float32).reshape(-1, 128).T)


def shared_inputs(inp):
    m = {}
    for nm in ["ffn1_wi", "ffn1_wo", "w_in", "ssd_w_out", "cm_w_out", "w_mix_out", "xa_wq", "xa_wk", "xa_wv", "xa_wo", "ffn2_wi", "ffn2_wo"]:
        m[nm] = np.ascontiguousarray(inp[nm][0], dtype=np.float32)
    for nm in NORMS:
        v = inp[nm] if nm == "final_norm" else inp[nm][0]
        m["v_" + nm] = col_layout(v)
    m["k_ident"] = np.eye(128, dtype=np.float32)
    m["k_onesD"] = np.full((128, 128), 1.0 / D, np.float32)
    m["k_ones512"] = np.full((128, 128), 1.0 / 512, np.float32)
    m["k_tri"] = np.triu(np.ones((128, 128), np.float32))
    m["k_ones64"] = np.ones((64, 128), np.float32)
    m["v_cm_w"] = np.ascontiguousarray(inp["cm_dw_w"][0].reshape(CMW, DC, 128).transpose(2, 1, 0))
    m["v_cm_b"] = col_layout(inp["cm_dw_b"][0])
    m["v_cm_g"] = col_layout(inp["cm_ln_g"][0])
    m["v_cm_lb"] = col_layout(inp["cm_ln_b"][0])
    m["v_sc_w"] = np.ascontiguousarray(inp["ssd_conv_w"][0].reshape(4, 48, 128).transpose(2, 1, 0))
    m["v_sc_b"] = col_layout(inp["ssd_conv_b"][0])
    m["v_ssdn"] = col_layout(inp["ssd_norm"][0])
    m["v_dexp"] = col_layout(np.repeat(inp["ssd_d"][0], HP))
    m["v_dtb"] = np.ascontiguousarray(inp["ssd_dt_bias"][0].reshape(64, 1))
    m["v_acol"] = np.ascontiguousarray(np.repeat(inp["ssd_a_log"][0].reshape(64, 1), 8, axis=1))
    return m


def core_inputs(inp, shared, core):
    m = dict(shared)
    b, q = core // 4, core % 4
    xp = np.zeros((SEQ, D), np.float32)
    xp[(3 - q) * SL:, :] = inp["x_prompt"][b, :(q + 1) * SL, :]
    m["xp"] = xp
    pm = np.zeros((128, 4), np.float32)
    for j in range(3):
        pm[:, j] = 1.0 if j >= 3 - q else 0.0
    m["v_pm"] = pm
    m["mem"] = np.ascontiguousarray(inp["mem_prompt"][b])
    m["xs"] = np.ascontiguousarray(inp["x_sample"][core])
    m["c_ssdconv"] = np.ascontiguousarray(inp["cache_ssd_conv"][0, core])
    m["c_ssm"] = np.ascontiguousarray(inp["cache_ssm_state"][0, core].reshape(NH * HP, NS))
    m["c_cmconv"] = np.ascontiguousarray(inp["cache_cm_conv"][0, core])
    m["c_k"] = np.ascontiguousarray(inp["cache_mem_k"][0, core].reshape(NMEM, D))
    m["c_v"] = np.ascontiguousarray(inp["cache_mem_v"][0, core].reshape(NMEM, D))
    return m


_NC = None


def kernel(**inputs):
    global _NC
    inp = {kk: np.asarray(v) for kk, v in inputs.items()}
    if _NC is None:
        _NC = build()
    shared = shared_inputs(inp)
    in_maps = [core_inputs(inp, shared, c) for c in range(8)]
    res = run_bass_kernel_spmd(_NC, in_maps, core_ids=list(range(8)))
    r = res.results
    f = np.float32
    y_prompt = np.stack([np.concatenate([r[4 * b + q]["y_p"] for q in range(4)], axis=0) for b in range(2)]).astype(f)
    y_sample = np.stack([r[c]["y_s"] for c in range(8)]).astype(f)
    p_ssd_conv = np.stack([r[c]["o_p_ssdconv"] for c in (3, 7)])[None].astype(f)
    p_ssm = np.stack([r[c]["o_p_ssm"].reshape(NH, HP, NS) for c in (3, 7)])[None].astype(f)
    p_cm = np.stack([r[c]["o_p_cm"] for c in (3, 7)])[None].astype(f)
    p_k = np.stack([r[c]["o_p_k"].reshape(NMEM, 4, 512) for c in (3, 7)])[None].astype(f)
    p_v = np.stack([r[c]["o_p_v"].reshape(NMEM, 4, 512) for c in (3, 7)])[None].astype(f)
    s_ssd_conv = np.stack([r[c]["o_s_ssdconv"] for c in range(8)])[None].astype(f)
    s_ssm = np.stack([r[c]["o_s_ssm"].reshape(NH, HP, NS) for c in range(8)])[None].astype(f)
    s_cm = np.stack([r[c]["o_s_cm"] for c in range(8)])[None].astype(f)
    return (y_prompt, y_sample, p_ssd_conv, p_ssm, p_cm, p_k, p_v, s_ssd_conv, s_ssm, s_cm)
```

```python
import contextlib
import numpy as np
import concourse.bass as bass
import concourse.mybir as mybir
from concourse.bass_utils import run_bass_kernel_spmd

F32 = mybir.dt.float32
BF16 = mybir.dt.bfloat16
AF = mybir.ActivationFunctionType
ALU = mybir.AluOpType

D = 2048
DC = 16
SEQ = 4096
SL = 1024
NSL = 4
DEC = 16
DFF = 8192
DIN = 4096
XBC = 6144
NH = 64
HP = 64
NG = 8
NS = 128
NMEM = 256
CMW = 31
EPS = 1e-6
INC = 18496
O_Z, O_XBC, O_DT, O_CV, O_CG, O_GS, O_GC = 0, 4096, 10240, 10304, 12352, 14400, 16448

ENGS = ("tensor", "vector", "scalar", "gpsimd", "sync")


class Op:
    __slots__ = ("eng", "fn", "waits", "signal", "idx", "dkey", "dval")

    def __init__(self, eng, fn, dkey=None):
        self.eng, self.fn, self.dkey = eng, fn, dkey
        self.waits = []
        self.signal = False
        self.idx = None
        self.dval = None


import os
DBG = bool(os.environ.get('SCHED_DBG'))


class Sched:
    PID = 0

    def __init__(self, nc):
        self.nc = nc
        self.ops = {e: [] for e in ENGS}
        self.last_w = {}
        self.readers = {}
        self.dcount = {}
        self.out_dmas = []

    def add(self, eng, fn, reads=(), writes=(), dkey=None):
        op = Op(eng, fn, dkey)
        deps = []
        for b in reads:
            w = self.last_w.get(b)
            if w is not None:
                deps.append(w)
        for b in writes:
            w = self.last_w.get(b)
            if w is not None:
                deps.append(w)
            deps.extend(self.readers.get(b, ()))
        seen = set()
        for d in deps:
            if id(d) in seen:
                continue
            seen.add(id(d))
            if d.dkey is None and d.eng == "tensor" and eng == "tensor" and dkey is None:
                continue
            op.waits.append(d)
            d.signal = True
        for b in reads:
            self.readers.setdefault(b, []).append(op)
        for b in writes:
            self.last_w[b] = op
            self.readers[b] = []
        if dkey is not None:
            self.dcount[dkey] = self.dcount.get(dkey, 0) + 1
            op.dval = 16 * self.dcount[dkey]
        self.ops[eng].append(op)
        return op

    def emit(self):
        nc = self.nc
        Sched.PID += 1
        pid = Sched.PID
        esem = {e: nc.alloc_semaphore("se%d_%s" % (pid, e)) for e in ENGS}
        dsem = {k: nc.alloc_semaphore("sd%d_%d" % (pid, i)) for i, k in enumerate(self.dcount)}
        for e in ENGS:
            c = 0
            for op in self.ops[e]:
                if op.dkey is None and op.signal:
                    c += 1
                    op.idx = c
        out_dmas = self.out_dmas

        def run(e, eng):
            known = {}
            for op in self.ops[e]:
                need = {}
                for d in op.waits:
                    if d.dkey is not None:
                        key, val = ("d", d.dkey), d.dval
                    else:
                        key, val = ("e", d.eng), d.idx
                    if val > need.get(key, 0):
                        need[key] = val
                for key, val in need.items():
                    if known.get(key, 0) >= val:
                        continue
                    eng.wait_ge(dsem[key[1]] if key[0] == "d" else esem[key[1]], val)
                    known[key] = val
                    if DBG:
                        print("WAIT", e, key, val)
                ins = op.fn(eng)
                if DBG:
                    print("OP", e, "dkey", op.dkey, "sig", op.idx if op.signal else None)
                if op.dkey is not None:
                    ins.then_inc(dsem[op.dkey], 16)
                elif op.signal:
                    ins.then_inc(esem[e], 1)
            fin = {}
            for d in self.ops[e]:
                if d.dkey is not None:
                    fin[d.dkey] = max(fin.get(d.dkey, 0), d.dval)
            for kk, v in fin.items():
                if known.get(("d", kk), 0) < v:
                    eng.wait_ge(dsem[kk], v)

        with nc.Block() as block:
            for e in ENGS:
                if self.ops[e] or e == "sync":
                    getattr(block, e)(lambda eng, e=e: run(e, eng))


class K:
    pass


@contextlib.contextmanager
def phase(k):
    with k.nc.cleanup_on_exit():
        S = Sched(k.nc)
        k.S = S
        k.rr = {}
        k.cache = {}
        yield S
        S.emit()


def rot(k, name, n):
    i = k.rr.get(name, 0)
    k.rr[name] = i + 1
    return i % n


def sb(k, name, shape, dt):
    return k.nc.alloc_sbuf_tensor(name, list(shape), dt, allow_name_mangling=True)


def bank(k, i, n=512, p=128, dt=None):
    ap = k.ps[0:p, i * 512:i * 512 + 512]
    if dt is not None:
        return ap.bitcast(dt)
    return ap[:, 0:n]


def tiles_of(TT):
    t = []
    o = 0
    while o < TT:
        n = min(512, TT - o)
        t.append((o, n))
        o += n
    return t


def load_xT(k, S, xT, srcs):
    nc = k.nc
    if "xin" not in k.cache:
        k.cache["xin"] = [sb(k, "xin%d" % i, [128, D], F32) for i in range(2)]
    xin = k.cache["xin"]
    for (src, col0, rows) in srcs:
        for t0 in range(0, rows, 128):
            r = min(128, rows - t0)
            sl = rot(k, "xin", 2)
            xi = xin[sl]
            S.add("sync", lambda e, xi=xi, src=src, t0=t0, r=r: e.dma_start(out=xi[0:r, :], in_=src[t0:t0 + r, :]),
                  writes=[("xin", sl)], dkey=("xin", sl))
            for c0 in range(0, DC, 4):
                b = 4 + rot(k, "tpb", 4)
                for j in range(4):
                    c = c0 + j
                    S.add("tensor", lambda e, b=b, j=j, xi=xi, c=c, r=r: e.transpose(
                        bank(k, b)[:, j * 128:j * 128 + r], xi[0:r, c * 128:(c + 1) * 128], k.ident[0:r, 0:r]),
                        reads=[("xin", sl)], writes=[("ps", b)])
                eng = "scalar" if (c0 // 4) % 2 == 0 else "vector"
                col = col0 + t0

                def cp(e, b=b, c0=c0, col=col, r=r, eng=eng):
                    src_ = bank(k, b).rearrange("p (j t) -> p j t", j=4)[:, :, 0:r]
                    dst_ = xT[:, c0:c0 + 4, col:col + r]
                    if eng == "scalar":
                        return e.copy(dst_, src_)
                    return e.tensor_copy(dst_, src_)
                S.add(eng, cp, reads=[("ps", b)], writes=[("xT", c, col // 512) for c in range(c0, c0 + 4)] +
                      [("xT", c, (col + r - 1) // 512) for c in range(c0, c0 + 4)])


def rmsnorm(k, S, xT, hT, gcol, TT, xname="xT", hname="hT"):
    if "nsq" not in k.cache:
        k.cache["nsq"] = [sb(k, "nsq%d" % i, [128, 512], F32) for i in range(2)]
        k.cache["nrs"] = [sb(k, "nrs%d" % i, [128, 512], F32) for i in range(2)]
    sq, rstd = k.cache["nsq"], k.cache["nrs"]
    for ti, (o, n) in enumerate(tiles_of(TT)):
        b = 7
        for c in range(DC):
            s = rot(k, "nsq", 2)
            S.add("scalar", lambda e, s=s, c=c, o=o, n=n: e.activation(out=sq[s][:, 0:n], in_=xT[:, c, o:o + n], func=AF.Square),
                  reads=[(xname, c, ti)], writes=[("nsq", s)])
            S.add("tensor", lambda e, s=s, c=c, n=n, b=b: e.matmul(bank(k, b, n), k.onesD[:, :], sq[s][:, 0:n],
                                                                   start=(c == 0), stop=(c == DC - 1)),
                  reads=[("nsq", s)], writes=[("ps", b)])
        r = rot(k, "nrs", 2)
        S.add("scalar", lambda e, r=r, n=n, b=b: e.activation(out=rstd[r][:, 0:n], in_=bank(k, b, n), func=AF.Sqrt, bias=k.eps[:, 0:1]),
              reads=[("ps", b)], writes=[("nrs", r)])
        S.add("vector", lambda e, r=r, n=n: e.reciprocal(rstd[r][:, 0:n], rstd[r][:, 0:n]),
              reads=[("nrs", r)], writes=[("nrs", r)])
        for c in range(DC):
            S.add("vector", lambda e, r=r, c=c, o=o, n=n: e.scalar_tensor_tensor(
                hT[:, c, o:o + n], xT[:, c, o:o + n], gcol[:, c:c + 1], rstd[r][:, 0:n], ALU.mult, ALU.mult),
                reads=[(xname, c, ti), ("nrs", r)], writes=[(hname, c, ti)])


def ffn(k, S, xT, hT, wi, wo, TT):
    tl = tiles_of(TT)
    FG = 256
    NB = 4
    wa = [sb(k, "wa%d" % i, [128, DC, FG], BF16) for i in range(2)]
    wb = [sb(k, "wb%d" % i, [128, DC, FG], BF16) for i in range(2)]
    wos = [sb(k, "wo%d" % i, [128, NB, D], BF16) for i in range(2)]
    g = [sb(k, "g%d" % i, [128, NB, TT], BF16) for i in range(1)]
    sa = [sb(k, "sa%d" % i, [128, 512], F32) for i in range(2)]
    wiv = wi.rearrange("(kc p) n -> p kc n", p=128)
    wov = wo.rearrange("(fc p) n -> p fc n", p=128)
    nblk = DFF // (128 * NB)
    for blk in range(nblk):
        gs = rot(k, "g", 1)
        ws = rot(k, "wo", 2)
        S.add("gpsimd", lambda e, ws=ws, blk=blk: e.dma_start(out=wos[ws][:], in_=wov[:, blk * NB:(blk + 1) * NB, :]),
              writes=[("wo", ws)], dkey=("wo", ws))
        for half in range(NB * 128 // FG):
            s = rot(k, "wi", 2)
            f0 = blk * NB * 128 + half * FG
            S.add("gpsimd", lambda e, s=s, f0=f0: e.dma_start(out=wa[s][:], in_=wiv[:, :, f0:f0 + FG]),
                  writes=[("wa", s)], dkey=("wa", s))
            S.add("gpsimd", lambda e, s=s, f0=f0: e.dma_start(out=wb[s][:], in_=wiv[:, :, DFF + f0:DFF + f0 + FG]),
                  writes=[("wb", s)], dkey=("wb", s))
            for fcl in range(FG // 128):
                fc = half * (FG // 128) + fcl
                for ti, (o, n) in enumerate(tl):
                    ba = rot(k, "psA", 2)
                    bb = 2 + rot(k, "psB", 2)
                    for kc in range(DC):
                        S.add("tensor", lambda e, ba=ba, s=s, kc=kc, fcl=fcl, o=o, n=n: e.matmul(
                            bank(k, ba, n), wa[s][:, kc, fcl * 128:(fcl + 1) * 128], hT[:, kc, o:o + n],
                            start=(kc == 0), stop=(kc == DC - 1)),
                            reads=[("wa", s), ("hT", kc, ti)], writes=[("ps", ba)])
                    for kc in range(DC):
                        S.add("tensor", lambda e, bb=bb, s=s, kc=kc, fcl=fcl, o=o, n=n: e.matmul(
                            bank(k, bb, n), wb[s][:, kc, fcl * 128:(fcl + 1) * 128], hT[:, kc, o:o + n],
                            start=(kc == 0), stop=(kc == DC - 1)),
                            reads=[("wb", s), ("hT", kc, ti)], writes=[("ps", bb)])
                    ss = rot(k, "sa", 2)
                    S.add("scalar", lambda e, ss=ss, ba=ba, n=n: e.activation(out=sa[ss][:, 0:n], in_=bank(k, ba, n), func=AF.Silu),
                          reads=[("ps", ba)], writes=[("sa", ss)])
                    S.add("vector", lambda e, ss=ss, bb=bb, gs=gs, fc=fc, o=o, n=n: e.tensor_tensor(
                        g[gs][:, fc, o:o + n], sa[ss][:, 0:n], bank(k, bb, n), ALU.mult),
                        reads=[("sa", ss), ("ps", bb)], writes=[("g", gs, fc, ti)])
        for dc in range(DC):
            for ti, (o, n) in enumerate(tl):
                bo = 4 + rot(k, "psO", 3)
                for fc in range(NB):
                    S.add("tensor", lambda e, bo=bo, ws=ws, fc=fc, dc=dc, gs=gs, o=o, n=n: e.matmul(
                        bank(k, bo, n), wos[ws][:, fc, dc * 128:(dc + 1) * 128], g[gs][:, fc, o:o + n],
                        start=(fc == 0), stop=(fc == NB - 1)),
                        reads=[("wo", ws), ("g", gs, fc, ti)], writes=[("ps", bo)])
                S.add("vector", lambda e, bo=bo, dc=dc, o=o, n=n: e.scalar_tensor_tensor(
                    xT[:, dc, o:o + n], bank(k, bo, n), 0.5, xT[:, dc, o:o + n], ALU.mult, ALU.add),
                    reads=[("ps", bo), ("xT", dc, ti)], writes=[("xT", dc, ti)])


def allt(TT):
    return range(len(tiles_of(TT)))


def multilinear(k, S, jobs, nchunks, TT, epi, width=128, tag="ml", banks=(0, 1, 2, 3)):
    SW = 2 * width
    st_bufs = []
    for j, (inT, inname, KC, W, col0) in enumerate(jobs):
        ck = ("stage", tag, j, KC, SW)
        if ck not in k.cache:
            k.cache[ck] = [sb(k, "%s_w%d_%d" % (tag, j, i), [128, KC, SW], BF16) for i in range(2)]
        st_bufs.append(k.cache[ck])
    tl = tiles_of(TT)
    for st in range((nchunks + 1) // 2):
        nch = min(2, nchunks - 2 * st)
        slots = []
        for j, (inT, inname, KC, W, col0) in enumerate(jobs):
            sl = rot(k, "%s_s%d" % (tag, j), 2)
            slots.append(sl)
            Wv = W.rearrange("(kc p) n -> p kc n", p=128)
            c0 = col0 + st * SW
            S.add("gpsimd", lambda e, j=j, sl=sl, Wv=Wv, c0=c0, nch=nch: e.dma_start(
                out=st_bufs[j][sl][:, :, 0:nch * width], in_=Wv[:, :, c0:c0 + nch * width]),
                writes=[(tag, j, sl)], dkey=(tag, j, sl))
        for cl in range(nch):
            i = 2 * st + cl
            for ti, (o, n) in enumerate(tl):
                bks = []
                for j, (inT, inname, KC, W, col0) in enumerate(jobs):
                    b = banks[rot(k, tag + "_b", len(banks))]
                    sl = slots[j]
                    for kc in range(KC):
                        S.add("tensor", lambda e, b=b, j=j, sl=sl, kc=kc, cl=cl, inT=inT, o=o, n=n, KC=KC: e.matmul(
                            bank(k, b, n, p=width), st_bufs[j][sl][:, kc, cl * width:(cl + 1) * width], inT[:, kc, o:o + n],
                            start=(kc == 0), stop=(kc == KC - 1)),
                            reads=[(tag, j, sl), (inname, kc, ti)], writes=[("ps", b)])
                    bks.append(b)
                epi(i, ti, o, n, bks)


def transpose_to(k, S, dst_fn, src, rows, cols, dt, reads, writes, b, eng="scalar", col_off=0):
    def mm(e):
        if dt == BF16:
            out = bank(k, b, dt=BF16)[0:cols, col_off:col_off + rows]
            idn = k.identb
        else:
            out = bank(k, b)[0:cols, col_off:col_off + rows]
            idn = k.ident
        return e.transpose(out, src, idn[0:rows, 0:rows])
    S.add("tensor", mm, reads=reads, writes=[("ps", b)])


def evac(S, eng, dst, src, reads, writes):
    if eng == "scalar":
        S.add("scalar", lambda e: e.copy(dst, src), reads=reads, writes=writes)
    else:
        S.add("vector", lambda e: e.tensor_copy(dst, src), reads=reads, writes=writes)


def cm_branch(k, S, TT, last):
    nT = len(tiles_of(TT))
    UB = 30 + SL + (30 + DEC if last else 0)
    hT = k.hT
    v = sb(k, "cm_v", [128, DC, TT], F32)
    lnst = [sb(k, "cm_lnst%d" % i, [128, 512], BF16) for i in range(2)]
    ubuf = [sb(k, "cm_ub%d" % i, [128, UB], F32) for i in range(2)]
    sgt = [sb(k, "cm_sg%d" % i, [128, 512], F32) for i in range(2)]
    NPE = 26
    ubb = [sb(k, "cm_ubb%d" % i, [128, 30 + SL], BF16) for i in range(2)]
    dg = [sb(k, "cm_dg%d" % i, [128, NPE, 128], BF16) for i in range(2)]
    if last:
        cct = sb(k, "cm_cct", [32, D], F32)
        cmc = sb(k, "cm_cmc", [128, DC, 30], F32)
        stail = sb(k, "cm_stail", [128, DC, 30], F32)
        S.add("sync", lambda e: e.dma_start(out=cct[0:30, :], in_=k.din["c_cmconv"][:, :]), writes=["cct"], dkey="cct")
        for c0 in range(0, DC, 4):
            b = 4 + rot(k, "tpb", 4)
            for j in range(4):
                c = c0 + j
                transpose_to(k, S, None, cct[0:30, c * 128:(c + 1) * 128], 30, 128, F32, ["cct"], None, b, col_off=j * 128)
            S.add("scalar", lambda e, b=b, c0=c0: e.copy(cmc[:, c0:c0 + 4, :], bank(k, b).rearrange("p (j t) -> p j t", j=4)[:, :, 0:30]),
                  reads=[("ps", b)], writes=[("cmc", c) for c in range(c0, c0 + 4)])

    def off(o):
        return 30 + o if o < SL else 30 + SL + 30 + (o - SL)

    state = {}

    def epi(i, ti, o, n, bks):
        bv, bg = bks
        ub = i % 2
        if ti == 0:
            S.add("vector", lambda e, ub=ub, i=i: e.tensor_copy(ubuf[ub][:, 0:30], k.cm_tail[:, i, :]),
                  reads=[("cm_tail", i)], writes=[("ub", ub)])
            if last:
                S.add("vector", lambda e, ub=ub, i=i: e.tensor_copy(ubuf[ub][:, 30 + SL:60 + SL], cmc[:, i, :]),
                      reads=[("cmc", i)], writes=[("ub", ub)])
        ss = rot(k, "cm_sg", 2)
        S.add("scalar", lambda e, ss=ss, bg=bg, n=n: e.activation(out=sgt[ss][:, 0:n], in_=bank(k, bg, n), func=AF.Sigmoid),
              reads=[("ps", bg)], writes=[("cm_sg", ss)])
        S.add("vector", lambda e, ss=ss, bv=bv, ub=ub, o=o, n=n: e.tensor_tensor(
            ubuf[ub][:, off(o):off(o) + n], bank(k, bv, n), sgt[ss][:, 0:n], ALU.mult),
            reads=[("ps", bv), ("cm_sg", ss)], writes=[("ub", ub)])
        if ti == nT - 1:
            S.add("scalar", lambda e, ub=ub: e.copy(ubb[ub][:, 0:30 + SL], ubuf[ub][:, 0:30 + SL]),
                  reads=[("ub", ub)], writes=[("ubb", ub)])
            ds = rot(k, "cm_dg", 2)
            S.add("vector", lambda e, ds=ds, i=i: e.tensor_tensor(
                dg[ds][:, :, :], k.ident[:, :].unsqueeze(1).to_broadcast([128, NPE, 128]),
                k.cm_w[:, i, 0:NPE].unsqueeze(2).to_broadcast([128, NPE, 128]), ALU.mult),
                writes=[("cm_dg", ds)])
            for (o2, n2) in ((0, 512), (512, 512)):
                bcv = 6 + rot(k, "cm_cvb", 2)
                for kk in range(NPE):
                    S.add("tensor", lambda e, bcv=bcv, ds=ds, ub=ub, kk=kk, o2=o2, n2=n2: e.matmul(
                        bank(k, bcv, n2), dg[ds][:, kk, :], ubb[ub][:, o2 + kk:o2 + kk + n2], start=(kk == 0), stop=(kk == NPE - 1)),
                        reads=[("cm_dg", ds), ("ubb", ub)], writes=[("ps", bcv)])
                S.add("vector", lambda e, bcv=bcv, i=i, o2=o2, n2=n2: e.tensor_single_scalar(
                    v[:, i, o2:o2 + n2], bank(k, bcv, n2), k.cm_b[:, i:i + 1], ALU.add),
                    reads=[("ps", bcv)], writes=[("cm_v", i)])
            for kk in range(NPE, CMW):
                S.add("vector", lambda e, ub=ub, i=i, kk=kk: e.scalar_tensor_tensor(
                    v[:, i, 0:SL], ubuf[ub][:, kk:kk + SL], k.cm_w[:, i, kk:kk + 1], v[:, i, 0:SL], ALU.mult, ALU.add),
                    reads=[("ub", ub), ("cm_v", i)], writes=[("cm_v", i)])
            parts = []
            if last:
                parts.append((SL, 30 + SL, DEC))
            for (vo, uo, ln) in parts:
                S.add("vector", lambda e, ub=ub, i=i, vo=vo, uo=uo, ln=ln: e.tensor_scalar(
                    v[:, i, vo:vo + ln], ubuf[ub][:, uo:uo + ln], k.cm_w[:, i, 0:1], k.cm_b[:, i:i + 1], ALU.mult, ALU.add),
                    reads=[("ub", ub)], writes=[("cm_v", i)])
                for kk in range(1, CMW):
                    S.add("vector", lambda e, ub=ub, i=i, vo=vo, uo=uo, ln=ln, kk=kk: e.scalar_tensor_tensor(
                        v[:, i, vo:vo + ln], ubuf[ub][:, uo + kk:uo + kk + ln], k.cm_w[:, i, kk:kk + 1], v[:, i, vo:vo + ln],
                        ALU.mult, ALU.add),
                        reads=[("ub", ub), ("cm_v", i)], writes=[("cm_v", i)])
            S.add("scalar", lambda e, ub=ub, i=i: e.copy(k.cm_tail[:, i, :], ubuf[ub][:, SL:SL + 30]),
                  reads=[("ub", ub)], writes=[("cm_tail", i)])
            if last:
                S.add("scalar", lambda e, ub=ub, i=i: e.copy(stail[:, i, :], ubuf[ub][:, UB - 30:UB]),
                      reads=[("ub", ub)], writes=[("cm_stail", i)])

    multilinear(k, S, [(hT, "hT", DC, k.din["w_in"], O_CV), (hT, "hT", DC, k.din["w_in"], O_CG)], DC, TT, epi, tag="cmA")

    sq = [sb(k, "cm_sq%d" % i, [128, 512], F32) for i in range(2)]
    mean = sb(k, "cm_mean", [128, 512], F32)
    rstd = sb(k, "cm_rstd", [128, 512], F32)
    tmp = [sb(k, "cm_tmp%d" % i, [128, 512], F32) for i in range(2)]
    for ti, (o, n) in enumerate(tiles_of(TT)):
        b1, b2 = 4, 5
        for c in range(DC):
            S.add("tensor", lambda e, c=c, o=o, n=n: e.matmul(bank(k, b1, n), k.onesD[:, :], v[:, c, o:o + n], start=(c == 0), stop=(c == DC - 1)),
                  reads=[("cm_v", c)], writes=[("ps", b1)])
        for c in range(DC):
            s = rot(k, "cm_sq", 2)
            S.add("scalar", lambda e, s=s, c=c, o=o, n=n: e.activation(out=sq[s][:, 0:n], in_=v[:, c, o:o + n], func=AF.Square),
                  reads=[("cm_v", c)], writes=[("cm_sq", s)])
            S.add("tensor", lambda e, s=s, c=c, n=n: e.matmul(bank(k, b2, n), k.onesD[:, :], sq[s][:, 0:n], start=(c == 0), stop=(c == DC - 1)),
                  reads=[("cm_sq", s)], writes=[("ps", b2)])
        S.add("scalar", lambda e, n=n: e.copy(mean[:, 0:n], bank(k, b1, n)), reads=[("ps", b1)], writes=["cm_mean"])
        S.add("vector", lambda e, n=n: e.tensor_tensor(rstd[:, 0:n], mean[:, 0:n], mean[:, 0:n], ALU.mult), reads=["cm_mean"], writes=["cm_rstd"])
        S.add("vector", lambda e, n=n: e.tensor_tensor(rstd[:, 0:n], bank(k, b2, n), rstd[:, 0:n], ALU.subtract),
              reads=[("ps", b2), "cm_rstd"], writes=["cm_rstd"])
        S.add("scalar", lambda e, n=n: e.activation(out=rstd[:, 0:n], in_=rstd[:, 0:n], func=AF.Sqrt, bias=k.eps[:, 0:1]),
              reads=["cm_rstd"], writes=["cm_rstd"])
        S.add("vector", lambda e, n=n: e.reciprocal(rstd[:, 0:n], rstd[:, 0:n]), reads=["cm_rstd"], writes=["cm_rstd"])
        for c in range(DC):
            t = rot(k, "cm_tmp", 2)
            S.add("vector", lambda e, t=t, c=c, o=o, n=n: e.tensor_tensor(tmp[t][:, 0:n], v[:, c, o:o + n], mean[:, 0:n], ALU.subtract),
                  reads=[("cm_v", c), "cm_mean"], writes=[("cm_tmp", t)])
            S.add("vector", lambda e, t=t, c=c, n=n: e.scalar_tensor_tensor(tmp[t][:, 0:n], tmp[t][:, 0:n], k.cm_g[:, c:c + 1], rstd[:, 0:n], ALU.mult, ALU.mult),
                  reads=[("cm_tmp", t), "cm_rstd"], writes=[("cm_tmp", t)])
            ls = rot(k, "cm_lnst", 2)
            S.add("scalar", lambda e, t=t, c=c, ls=ls, n=n: e.activation(out=lnst[ls][:, 0:n], in_=tmp[t][:, 0:n], func=AF.Silu, bias=k.cm_lb[:, c:c + 1]),
                  reads=[("cm_tmp", t)], writes=[("cm_lnst", ls)])
            S.add("sync", lambda e, ls=ls, c=c, o=o, n=n: e.dma_start(out=k.lns[:, c, o:o + n], in_=lnst[ls][:, 0:n]),
                  reads=[("cm_lnst", ls)], dkey=("cm_lnst", ls))

    if last:
        cm_tails_out(k, S, stail)


def cm_tail_only(k, S):
    hv = k.hT[:, :, SL - 30:SL]
    sg = [sb(k, "ct_sg%d" % i, [128, 32], F32) for i in range(2)]

    def epi(i, ti, o, n, bks):
        bv, bg = bks
        ss = rot(k, "ct_sg", 2)
        S.add("scalar", lambda e, ss=ss, bg=bg: e.activation(out=sg[ss][:, 0:30], in_=bank(k, bg, 30), func=AF.Sigmoid),
              reads=[("ps", bg)], writes=[("ct_sg", ss)])
        S.add("vector", lambda e, ss=ss, bv=bv, i=i: e.tensor_tensor(k.cm_tail[:, i, :], bank(k, bv, 30), sg[ss][:, 0:30], ALU.mult),
              reads=[("ps", bv), ("ct_sg", ss)], writes=[("cm_tail", i)])
    multilinear(k, S, [(hv, "hT", DC, k.din["w_in"], O_CV), (hv, "hT", DC, k.din["w_in"], O_CG)], DC, 30, epi, tag="cmA")


def cm_branch_b(k, S, TT):
    hT = k.hT
    lnT = sb(k, "cm_ln", [128, DC, TT], BF16)
    sgt = [sb(k, "cm_sg%d" % i, [128, 512], F32) for i in range(2)]
    S.add("sync", lambda e: e.dma_start(out=lnT[:], in_=k.lns[:, :, 0:TT]), writes=[("lnT", c, ti) for c in range(DC) for ti in allt(TT)], dkey="lnl")
    bst = [sb(k, "cm_bst%d" % i, [128, 512], F32) for i in range(2)]

    def epi2(i, ti, o, n, bks):
        bc, bg = bks
        ss = rot(k, "cm_sg", 2)
        S.add("scalar", lambda e, ss=ss, bg=bg, n=n: e.activation(out=sgt[ss][:, 0:n], in_=bank(k, bg, n), func=AF.Sigmoid),
              reads=[("ps", bg)], writes=[("cm_sg", ss)])
        bs = rot(k, "cm_bst", 2)
        S.add("vector", lambda e, ss=ss, bc=bc, bs=bs, n=n: e.tensor_tensor(bst[bs][:, 0:n], bank(k, bc, n), sgt[ss][:, 0:n], ALU.mult),
              reads=[("ps", bc), ("cm_sg", ss)], writes=[("cm_bst", bs)])
        S.add("sync", lambda e, bs=bs, i=i, o=o, n=n: e.dma_start(out=k.Bs[:, i, o:o + n], in_=bst[bs][:, 0:n]),
              reads=[("cm_bst", bs)], dkey=("cm_bst", bs))

    multilinear(k, S, [(lnT, "lnT", DC, k.din["cm_w_out"], 0), (hT, "hT", DC, k.din["w_in"], O_GC)], DC, TT, epi2, tag="cmA")


def cm_tails_out(k, S, stail):
    if True:
        for (src, sname, dst) in ((k.cm_tail, "cm_tail", k.dout["o_p_cm"]), (stail, "cm_stail", k.dout["o_s_cm"])):
            if "cm_ot" not in k.cache:
                k.cache["cm_ot"] = sb(k, "cm_ot", [32, D], F32)
            ot = k.cache["cm_ot"]
            for c0 in range(0, DC, 4):
                b = 4 + rot(k, "tpb", 4)
                for j in range(4):
                    c = c0 + j
                    S.add("tensor", lambda e, b=b, j=j, c=c, src=src: e.transpose(
                        bank(k, b)[0:30, j * 128:(j + 1) * 128], src[:, c, :], k.ident[:, :]),
                        reads=[(sname, c)], writes=[("ps", b)])
                S.add("scalar", lambda e, b=b, c0=c0, ot=ot: e.copy(ot[0:30, c0 * 128:(c0 + 4) * 128], bank(k, b)[0:30, :]),
                      reads=[("ps", b)], writes=["cm_ot"])
            S.add("sync", lambda e, ot=ot, dst=dst: e.dma_start(out=dst[:, :], in_=ot[0:30, :]), reads=["cm_ot"], dkey=("cm_ot", sname))


def ssd_branch(k, S, T0, NM, last, first, mode="full", maskcol=None):
    full = (mode == "full")
    TT = NM + (DEC if last else 0)
    hT = k.hT[:, :, T0:T0 + TT]
    tl = tiles_of(TT)
    nT = len(tl)
    chunks = [(c * 128, 128) for c in range(NM // 128)] + ([(NM, DEC)] if last else [])
    NCH = len(chunks)
    XB = 3 + NM + (3 + DEC if last else 0)
    win = k.din["w_in"]
    MISC = [2, 3, 6, 7] if full else [2, 3, 4, 5, 6, 7]

    def mb():
        return MISC[rot(k, "misc", len(MISC))]

    def xoff(o):
        return 3 + o if o < NM else 3 + NM + 3 + (o - NM)

    ST = sb(k, "ssd_ST", [128, NH, HP], F32)
    STb = sb(k, "ssd_STb", [128, NH, HP], BF16)
    if first:
        S.add("vector", lambda e: e.memset(ST[:], 0.0), writes=[("ST", u) for u in range(16)])
    else:
        S.add("sync", lambda e: e.dma_start(out=ST[:], in_=k.sts[:, :].rearrange("p (h q) -> p h q", h=NH)),
              writes=[("ST", u) for u in range(16)], dkey="stload")
    if full:
        S.add("scalar", lambda e: e.copy(STb[:], ST[:]), reads=[("ST", u) for u in range(16)], writes=[("STb", u) for u in range(16)])

    dtT = sb(k, "ssd_dtT", [64, TT], F32)
    acsT = sb(k, "ssd_acsT", [64, TT], F32)
    dteT = sb(k, "ssd_dteT", [64, TT], F32)
    dA = sb(k, "ssd_dA", [64, TT], F32)
    tk = sb(k, "ssd_tk", [128, NCH, 3, 64], F32)
    Edec = sb(k, "ssd_Edec", [128, NCH, 64], F32)
    diagm = sb(k, "ssd_diag", [64, NCH, 64], F32)
    et64 = sb(k, "ssd_et64", [64, 512], F32)

    def epi_dt(i, ti, o, n, bks):
        b = bks[0]
        S.add("scalar", lambda e, b=b, n=n: e.activation(out=et64[:, 0:n], in_=bank(k, b, n, p=64), func=AF.Exp, bias=k.dtb[:, 0:1]),
              reads=[("ps", b)], writes=["et64"])
        S.add("scalar", lambda e, o=o, n=n: e.activation(out=dtT[:, o:o + n], in_=et64[:, 0:n], func=AF.Ln, bias=1.0),
              reads=["et64"], writes=[("dtT", ti)])
        S.add("vector", lambda e, o=o, n=n: e.tensor_single_scalar(dA[:, o:o + n], dtT[:, o:o + n], k.acol[:, 0:1], ALU.mult),
              reads=[("dtT", ti)], writes=[("dA", ti)])

    multilinear(k, S, [(hT, "hT", DC, win, O_DT)], 1, TT, epi_dt, width=64, tag="dt", banks=(0, 1))
    for ch, (t0, ln) in enumerate(chunks):
        ti = t0 // 512
        S.add("vector", lambda e, t0=t0, ln=ln: e.tensor_tensor_scan(acsT[:, t0:t0 + ln], k.ones64[:, 0:ln], dA[:, t0:t0 + ln], 0.0, ALU.mult, ALU.add),
              reads=[("dA", ti)], writes=[("acsT", ch)])
        S.add("scalar", lambda e, t0=t0, ln=ln: e.activation(out=dteT[:, t0:t0 + ln], in_=acsT[:, t0:t0 + ln], func=AF.Exp,
                                                            bias=acsT[:, t0 + ln - 1:t0 + ln], scale=-1.0),
              reads=[("acsT", ch)], writes=[("dteT", ch)])
        S.add("vector", lambda e, t0=t0, ln=ln: e.tensor_tensor(dteT[:, t0:t0 + ln], dteT[:, t0:t0 + ln], dtT[:, t0:t0 + ln], ALU.mult),
              reads=[("dteT", ch), ("dtT", ti)], writes=[("dteT", ch)])
        b = mb()
        for j, (src, nm) in enumerate(((dtT, ("dtT", ti)), (dteT, ("dteT", ch)), (acsT, ("acsT", ch)))):
            S.add("tensor", lambda e, b=b, j=j, src=src, t0=t0, ln=ln: e.transpose(
                bank(k, b)[0:ln, j * 64:(j + 1) * 64], src[:, t0:t0 + ln], k.ident[0:64, 0:64]),
                reads=[nm], writes=[("ps", b)])
        S.add("scalar", lambda e, b=b, ch=ch, ln=ln: e.copy(tk[0:ln, ch, :, :], bank(k, b)[0:ln, 0:192].rearrange("p (j h) -> p j h", j=3)),
              reads=[("ps", b)], writes=[("tk", ch)])
        S.add("vector", lambda e, ch=ch, t0=t0, ln=ln: e.tensor_single_scalar(diagm[:, ch, :], k.ident[0:64, 0:64], acsT[:, t0 + ln - 1:t0 + ln], ALU.mult),
              reads=[("acsT", ch)], writes=[("diagm", ch)])
    if full:
        S.add("sync", lambda e: e.dma_start(out=k.acsd[:, 0:TT], in_=acsT[:, :]), reads=[("acsT", ch) for ch in range(NCH)], writes=["acsd"], dkey="acsd")
    dflat = diagm[:, :, :].rearrange("p c h -> p (c h)")
    eflat = Edec[:, :, :].rearrange("p c h -> p (c h)")
    for o in range(0, NCH * 64, 512):
        n = min(512, NCH * 64 - o)
        b = mb()
        S.add("tensor", lambda e, b=b, o=o, n=n: e.matmul(bank(k, b, n), k.ones64[:, :], dflat[:, o:o + n], start=True, stop=True),
              reads=[("diagm", ch) for ch in range(NCH)], writes=[("ps", b)])
        S.add("scalar", lambda e, b=b, o=o, n=n: e.activation(out=eflat[:, o:o + n], in_=bank(k, b, n), func=AF.Exp),
              reads=[("ps", b)], writes=["Edec"])

    xbuf = [sb(k, "ssd_xb%d" % i, [128, XB], F32) for i in range(2)]
    cacc = [sb(k, "ssd_ca%d" % i, [128, XB], F32) for i in range(2)]
    two = range(2)
    BT2 = [sb(k, "ssd_BT%d" % i, [128, TT], BF16) for i in two]
    CT2 = [sb(k, "ssd_CT%d" % i, [128, TT], BF16) for i in (two if full else range(1))] * (1 if full else 2)
    Btok2 = [sb(k, "ssd_Btok%d" % i, [128, NCH, 128], BF16) for i in two]
    GmT2 = [sb(k, "ssd_GmT%d" % i, [128, NCH, 128], BF16) for i in (two if full else range(1))] * (1 if full else 2)
    ygs = [sb(k, "ssd_yg%d" % i, [128, 4, TT], BF16) for i in (two if full else range(1))] * (1 if full else 2)
    xs2 = [sb(k, "ssd_xs%d" % i, [128, 2, TT], F32) for i in two]
    zs2 = [sb(k, "ssd_zs%d" % i, [128, 2, TT], BF16) for i in (two if full else range(1))] * (1 if full else 2)
    xdt2 = [sb(k, "ssd_xdt%d" % i, [128, NCH, 256], BF16) for i in (two if full else range(1))] * (1 if full else 2)
    xw2 = [sb(k, "ssd_xw%d" % i, [128, NCH, 256], BF16) for i in two]
    MT2 = [sb(k, "ssd_MT%d" % i, [128, 4, NCH, 128], BF16) for i in (two if full else range(1))] * (1 if full else 2)
    CdT2 = [sb(k, "ssd_CdT%d" % i, [128, 4, TT], BF16) for i in (two if full else range(1))] * (1 if full else 2)
    ar = [sb(k, "ssd_ar%d" % i, [128, TT], F32) for i in range(3 if full else 0)]
    er = [sb(k, "ssd_er%d" % i, [128, TT], F32) for i in range(2 if full else 0)]
    sg = [sb(k, "ssd_sg%d" % i, [128, 128], F32) for i in range(2)]
    et = [sb(k, "ssd_et%d" % i, [128, 128], BF16) for i in range(2)]
    sqn = [sb(k, "ssd_sq%d" % i, [128, 512], F32) for i in range(2)]
    rsn = sb(k, "ssd_rs", [128, 512], F32)
    if last:
        sct = sb(k, "ssd_sct", [8, 1536], F32)
        scc = sb(k, "ssd_scc", [128, 48, 3], F32)
        sst = sb(k, "ssd_sst", [128, 48, 3], F32)
        stst = [sb(k, "ssd_stst%d" % i, [128, 2, 128], F32) for i in range(2)]
        cst = [sb(k, "ssd_cst%d" % i, [128, 2, 128], F32) for i in range(2)]
        for c0 in range(0, 48, 4):
            if c0 % 12 == 0:
                S.add("sync", lambda e, c0=c0: e.dma_start(out=sct[0:3, :], in_=k.din["c_ssdconv"][:, c0 * 128:c0 * 128 + 1536]), writes=["sct"], dkey="sct")
            b = mb()
            for j in range(4):
                c = c0 + j
                S.add("tensor", lambda e, b=b, j=j, c=c: e.transpose(bank(k, b)[:, j * 128:j * 128 + 3], sct[0:3, (c % 12) * 128:(c % 12 + 1) * 128], k.ident[0:3, 0:3]),
                      reads=["sct"], writes=[("ps", b)])
            S.add("scalar", lambda e, b=b, c0=c0: e.copy(scc[:, c0:c0 + 4, :], bank(k, b).rearrange("p (j t) -> p j t", j=4)[:, :, 0:3]),
                  reads=[("ps", b)], writes=[("scc", c) for c in range(c0, c0 + 4)])

    def conv_chunk(xb, cidx, outs, tail_only=False):
        if tail_only:
            S.add("scalar", lambda e, xb=xb, cidx=cidx: e.copy(k.ssd_tail[:, cidx, :], xbuf[xb][:, NM:NM + 3]),
                  reads=[("xb", xb)], writes=[("ssd_tail", cidx)])
            return
        S.add("vector", lambda e, xb=xb, cidx=cidx: e.tensor_copy(xbuf[xb][:, 0:3], k.ssd_tail[:, cidx, :]),
              reads=[("ssd_tail", cidx)], writes=[("xb", xb)])
        if last:
            S.add("vector", lambda e, xb=xb, cidx=cidx: e.tensor_copy(xbuf[xb][:, 3 + NM:6 + NM], scc[:, cidx, :]),
                  reads=[("scc", cidx)], writes=[("xb", xb)])
        W = XB - 3
        S.add("vector", lambda e, xb=xb, cidx=cidx: e.tensor_scalar(
            cacc[xb][:, 0:W], xbuf[xb][:, 0:W], k.sc_w[:, cidx, 0:1], k.sc_b[:, cidx:cidx + 1], ALU.mult, ALU.add),
            reads=[("xb", xb)], writes=[("ca", xb)])
        for kk in range(1, 4):
            S.add("vector", lambda e, xb=xb, cidx=cidx, kk=kk: e.scalar_tensor_tensor(
                cacc[xb][:, 0:W], xbuf[xb][:, kk:kk + W], k.sc_w[:, cidx, kk:kk + 1], cacc[xb][:, 0:W], ALU.mult, ALU.add),
                reads=[("xb", xb), ("ca", xb)], writes=[("ca", xb)])
        S.add("scalar", lambda e, xb=xb, cidx=cidx: e.copy(k.ssd_tail[:, cidx, :], xbuf[xb][:, NM:NM + 3]),
              reads=[("xb", xb)], writes=[("ssd_tail", cidx)])
        if last:
            S.add("scalar", lambda e, xb=xb, cidx=cidx: e.copy(sst[:, cidx, :], xbuf[xb][:, XB - 3:XB]),
                  reads=[("xb", xb)], writes=[("sst", cidx)])

    def conv_silu(xb, dst_fn, wkeys):
        parts = [(0, 0, NM)] + ([(NM, NM + 3, DEC)] if last else [])
        for (vo, ao, ln) in parts:
            S.add("scalar", lambda e, xb=xb, vo=vo, ao=ao, ln=ln: e.activation(out=dst_fn(vo, ln), in_=cacc[xb][:, ao:ao + ln], func=AF.Silu),
                  reads=[("ca", xb)], writes=wkeys)

    def GA(g):
        gp = g % 2
        for which in range(2):
            if which == 1 and mode == "state":
                continue
            dstT = (BT2, CT2)[which][gp]
            nm = (("BT", gp), ("CT", gp))[which]
            cidx = 32 + 8 * which + g
            xb = rot(k, "xb", 2)

            def epi_bc(i, ti, o, n, bks, xb=xb):
                b = bks[0]
                S.add("scalar", lambda e, b=b, xb=xb, o=o, n=n: e.copy(xbuf[xb][:, xoff(o):xoff(o) + n], bank(k, b, n)),
                      reads=[("ps", b)], writes=[("xb", xb)])
            multilinear(k, S, [(hT, "hT", DC, win, O_XBC + cidx * 128)], 1, TT, epi_bc, tag="sp", banks=(0, 1))
            if which == 1 and not full:
                conv_chunk(xb, cidx, None, tail_only=True)
                continue
            conv_chunk(xb, cidx, None)
            conv_silu(xb, lambda vo, ln, dstT=dstT: dstT[:, vo:vo + ln], [nm])
        BT, CT = BT2[gp], CT2[gp]
        for ch, (t0, ln) in enumerate(chunks):
            b = mb()
            S.add("tensor", lambda e, b=b, t0=t0, ln=ln: e.transpose(bank(k, b, dt=BF16)[0:ln, 0:128], BT[:, t0:t0 + ln], k.identb[:, :]),
                  reads=[("BT", gp)], writes=[("ps", b)])
            S.add("scalar", lambda e, b=b, ch=ch, ln=ln: e.copy(Btok2[gp][0:ln, ch, :], bank(k, b, dt=BF16)[0:ln, 0:128]),
                  reads=[("ps", b)], writes=[("Btok", gp, ch)])
            if not full:
                continue
            b2 = mb()
            S.add("tensor", lambda e, b2=b2, t0=t0, ln=ln: e.matmul(bank(k, b2)[0:ln, 0:ln], BT[:, t0:t0 + ln], CT[:, t0:t0 + ln], start=True, stop=True),
                  reads=[("BT", gp), ("CT", gp)], writes=[("ps", b2)])
            S.add("vector", lambda e, b2=b2, ch=ch, ln=ln: e.tensor_tensor(GmT2[gp][0:ln, ch, 0:ln], bank(k, b2)[0:ln, 0:ln], k.tri[0:ln, 0:ln], ALU.mult),
                  reads=[("ps", b2)], writes=[("GmT", gp, ch)])

    def UA(u):
        g, half = u // 2, u % 2
        up = u % 2
        cc0 = 4 * g + 2 * half
        h0 = 8 * g + 4 * half
        xs, zs, xdt, xw = xs2[up], zs2[up], xdt2[up], xw2[up]

        def epi_z(i, ti, o, n, bks):
            b = bks[0]
            S.add("scalar", lambda e, b=b, i=i, o=o, n=n: e.activation(out=zs[:, i, o:o + n], in_=bank(k, b, n), func=AF.Silu),
                  reads=[("ps", b)], writes=[("zs", up, i, ti)])
        if full:
            multilinear(k, S, [(hT, "hT", DC, win, O_Z + cc0 * 128)], 2, TT, epi_z, tag="sp", banks=(0, 1))
        xbs = {}

        def epi_x(i, ti, o, n, bks):
            b = bks[0]
            if ti == 0:
                xbs[i] = rot(k, "xb", 2)
            xb = xbs[i]
            S.add("scalar", lambda e, b=b, xb=xb, o=o, n=n: e.copy(xbuf[xb][:, xoff(o):xoff(o) + n], bank(k, b, n)),
                  reads=[("ps", b)], writes=[("xb", xb)])
            if ti == nT - 1:
                conv_chunk(xb, cc0 + i, None)
                conv_silu(xb, lambda vo, ln, i=i: xs[:, i, vo:vo + ln], [("xs", up, i)])
        multilinear(k, S, [(hT, "hT", DC, win, O_XBC + cc0 * 128)], 2, TT, epi_x, tag="sp", banks=(0, 1))

    def UA2(u):
        g, half = u // 2, u % 2
        up = u % 2
        h0 = 8 * g + 4 * half
        xs, xdt, xw = xs2[up], xdt2[up], xw2[up]
        for ch, (t0, ln) in enumerate(chunks):
            b = mb()
            for cl in range(2):
                S.add("tensor", lambda e, b=b, cl=cl, t0=t0, ln=ln: e.transpose(
                    bank(k, b)[0:ln, cl * 128:(cl + 1) * 128], xs[:, cl, t0:t0 + ln], k.ident[:, :]),
                    reads=[("xs", up, cl)], writes=[("ps", b)])
            for (dst, nm, j) in (((xdt, "xdt", 0), (xw, "xw", 1)) if full else ((xw, "xw", 1),)):
                S.add("vector", lambda e, b=b, dst=dst, j=j, ch=ch, ln=ln: e.tensor_tensor(
                    dst[0:ln, ch, :].rearrange("p (h q) -> p h q", h=4),
                    bank(k, b)[0:ln, 0:256].rearrange("p (h q) -> p h q", h=4),
                    tk[0:ln, ch, j, h0:h0 + 4].unsqueeze(2).to_broadcast([ln, 4, 64]), ALU.mult),
                    reads=[("ps", b), ("tk", ch)], writes=[(nm, up, ch)])

    def UB(u):
        if not full:
            return
        g, half = u // 2, u % 2
        up, gp = u % 2, g % 2
        h0 = 8 * g + 4 * half
        MT, CdT, CT, GmT = MT2[up], CdT2[up], CT2[gp], GmT2[gp]
        for hl in range(4):
            h = h0 + hl
            a = rot(k, "ar", 3)
            S.add("sync", lambda e, a=a, h=h: e.dma_start(out=ar[a][:, 0:TT], in_=k.acsd[h:h + 1, 0:TT].to_broadcast([128, TT])),
                  reads=["acsd"], writes=[("ar", a)], dkey=("ar", a))
            r = rot(k, "er", 2)
            S.add("scalar", lambda e, r=r, a=a: e.activation(out=er[r][:, 0:TT], in_=ar[a][:, 0:TT], func=AF.Exp),
                  reads=[("ar", a)], writes=[("er", r)])
            S.add("vector", lambda e, r=r, hl=hl: e.tensor_tensor(CdT[:, hl, 0:TT], CT[:, 0:TT], er[r][:, 0:TT], ALU.mult),
                  reads=[("er", r), ("CT", gp)], writes=[("CdT", up, hl, ti) for ti in range(nT)])
            for ch, (t0, ln) in enumerate(chunks):
                s_ = rot(k, "sg", 2)
                S.add("vector", lambda e, s_=s_, a=a, t0=t0, ln=ln, ch=ch, h=h: e.tensor_scalar(
                    sg[s_][0:ln, 0:ln], ar[a][0:ln, t0:t0 + ln], tk[0:ln, ch, 2, h:h + 1], 0.0, ALU.subtract, ALU.min),
                    reads=[("ar", a), ("tk", ch)], writes=[("sg", s_)])
                S.add("scalar", lambda e, s_=s_, ln=ln: e.activation(out=et[s_][0:ln, 0:ln], in_=sg[s_][0:ln, 0:ln], func=AF.Exp),
                      reads=[("sg", s_)], writes=[("et", s_)])
                S.add("vector", lambda e, s_=s_, hl=hl, ch=ch, ln=ln: e.tensor_tensor(MT[0:ln, hl, ch, 0:ln], et[s_][0:ln, 0:ln], GmT[0:ln, ch, 0:ln], ALU.mult),
                      reads=[("et", s_), ("GmT", gp, ch)], writes=[("MT", up, hl, ch)])

    def UC(u):
        g, half = u // 2, u % 2
        up, gp = u % 2, g % 2
        cc0 = 4 * g + 2 * half
        h0 = 8 * g + 4 * half
        xs, zs, xdt, xw, MT, CdT = xs2[up], zs2[up], xdt2[up], xw2[up], MT2[up], CdT2[up]
        Btok = Btok2[gp]
        yg = ygs[gp]

        def save_state(dst):
            s_ = rot(k, "stst", 2)
            b = mb()
            for pr in range(2):
                S.add("tensor", lambda e, b=b, pr=pr: e.transpose(
                    bank(k, b)[:, pr * 128:(pr + 1) * 128], ST[:, h0 + 2 * pr:h0 + 2 * pr + 2, :].rearrange("p h q -> p (h q)"), k.ident[:, :]),
                    reads=[("ST", u)], writes=[("ps", b)])
            S.add("scalar", lambda e, b=b, s_=s_: e.copy(stst[s_][:, :, :], bank(k, b)[:, 0:256].rearrange("p (j n) -> p j n", j=2)),
                  reads=[("ps", b)], writes=[("stst", s_)])
            S.add("sync", lambda e, s_=s_, dst=dst: e.dma_start(
                out=dst.rearrange("(pr q) n -> q pr n", q=128)[:, h0 // 2:h0 // 2 + 2, :], in_=stst[s_][:, :, :]),
                reads=[("stst", s_)], dkey=("stst", s_))

        for ch, (t0, ln) in enumerate(chunks):
            if last and ch == NCH - 1:
                save_state(k.dout["o_p_ssm"])
                s_ = rot(k, "cst", 2)
                S.add("sync", lambda e, s_=s_: e.dma_start(
                    out=cst[s_][:, :, :], in_=k.din["c_ssm"].rearrange("(pr q) n -> q pr n", q=128)[:, h0 // 2:h0 // 2 + 2, :]),
                    writes=[("cst", s_)], dkey=("cst", s_))
                b = mb()
                for pr in range(2):
                    S.add("tensor", lambda e, b=b, pr=pr, s_=s_: e.transpose(bank(k, b)[:, pr * 128:(pr + 1) * 128], cst[s_][:, pr, :], k.ident[:, :]),
                          reads=[("cst", s_)], writes=[("ps", b)])
                S.add("vector", lambda e, b=b: e.tensor_copy(ST[:, h0:h0 + 4, :].rearrange("p h q -> p (h q)"), bank(k, b)[:, 0:256]),
                      reads=[("ps", b)], writes=[("ST", u)])
                S.add("scalar", lambda e: e.copy(STb[:, h0:h0 + 4, :], ST[:, h0:h0 + 4, :]), reads=[("ST", u)], writes=[("STb", u)])
            bS = mb()
            S.add("tensor", lambda e, bS=bS, ch=ch, ln=ln: e.matmul(bank(k, bS, 256), Btok[0:ln, ch, :], xw[0:ln, ch, :], start=True, stop=True),
                  reads=[("Btok", gp, ch), ("xw", up, ch)], writes=[("ps", bS)])
            bY = 4 + rot(k, "psY", 2)
            for hl in (range(4) if full else ()):
                def outap(bY=bY, hl=hl, ln=ln):
                    return bank(k, bY)[(hl % 2) * 64:(hl % 2) * 64 + 64, (hl // 2) * 128:(hl // 2) * 128 + ln]
                S.add("tensor", lambda e, outap=outap, hl=hl, ch=ch, ln=ln: e.matmul(
                    outap(), xdt[0:ln, ch, hl * 64:(hl + 1) * 64], MT[0:ln, hl, ch, 0:ln], start=True, stop=False),
                    reads=[("xdt", up, ch), ("MT", up, hl, ch)], writes=[("ps", bY)])
                S.add("tensor", lambda e, outap=outap, hl=hl, t0=t0, ln=ln: e.matmul(
                    outap(), STb[:, h0 + hl, :], CdT[:, hl, t0:t0 + ln], start=False, stop=True),
                    reads=[("STb", u), ("CdT", up, hl, t0 // 512)], writes=[("ps", bY)])
            for cl in (range(2) if full else ()):
                S.add("vector", lambda e, bY=bY, cl=cl, t0=t0, ln=ln: e.scalar_tensor_tensor(
                    yg[:, 2 * half + cl, t0:t0 + ln], xs[:, cl, t0:t0 + ln], k.dexp[:, cc0 + cl:cc0 + cl + 1],
                    bank(k, bY)[:, cl * 128:cl * 128 + ln], ALU.mult, ALU.add),
                    reads=[("ps", bY), ("xs", up, cl)], writes=[("yg", gp, 2 * half + cl, ch)])
                S.add("vector", lambda e, cl=cl, t0=t0, ln=ln: e.tensor_tensor(
                    yg[:, 2 * half + cl, t0:t0 + ln], yg[:, 2 * half + cl, t0:t0 + ln], zs[:, cl, t0:t0 + ln], ALU.mult),
                    reads=[("yg", gp, 2 * half + cl, ch), ("zs", up, cl, t0 // 512)], writes=[("yg", gp, 2 * half + cl, ch)])
            stv = ST[:, h0:h0 + 4, :]
            S.add("vector", lambda e, stv=stv, ch=ch: e.tensor_tensor(stv, stv, Edec[:, ch, h0:h0 + 4].unsqueeze(2).to_broadcast([128, 4, 64]), ALU.mult),
                  reads=[("ST", u), "Edec"], writes=[("ST", u)])
            S.add("vector", lambda e, stv=stv, bS=bS: e.tensor_tensor(stv, stv, bank(k, bS, 256).rearrange("p (h q) -> p h q", h=4), ALU.add),
                  reads=[("ST", u), ("ps", bS)], writes=[("ST", u)])
            if full:
                S.add("scalar", lambda e, stv=stv: e.copy(STb[:, h0:h0 + 4, :], stv), reads=[("ST", u)], writes=[("STb", u)])
        if last:
            save_state(k.dout["o_s_ssm"])

    def GN(g):
        if not full:
            return
        gp = g % 2
        yg = ygs[gp]
        rk = [("yg", gp, c4, ch) for c4 in range(4) for ch in range(NCH)]
        for ti, (o, n) in enumerate(tl):
            b = mb()
            for c4 in range(4):
                s_ = rot(k, "sqn", 2)
                S.add("scalar", lambda e, s_=s_, c4=c4, o=o, n=n: e.activation(out=sqn[s_][:, 0:n], in_=yg[:, c4, o:o + n], func=AF.Square),
                      reads=rk, writes=[("sqn", s_)])
                S.add("tensor", lambda e, b=b, s_=s_, c4=c4, n=n: e.matmul(bank(k, b, n), k.ones512[:, :], sqn[s_][:, 0:n], start=(c4 == 0), stop=(c4 == 3)),
                      reads=[("sqn", s_)], writes=[("ps", b)])
            S.add("scalar", lambda e, b=b, n=n: e.activation(out=rsn[:, 0:n], in_=bank(k, b, n), func=AF.Sqrt, bias=k.eps[:, 0:1]),
                  reads=[("ps", b)], writes=["rsn"])
            S.add("vector", lambda e, n=n: e.reciprocal(rsn[:, 0:n], rsn[:, 0:n]), reads=["rsn"], writes=["rsn"])
            for c4 in range(4):
                S.add("vector", lambda e, c4=c4, o=o, n=n: e.scalar_tensor_tensor(
                    yg[:, c4, o:o + n], yg[:, c4, o:o + n], k.ssdn[:, 4 * g + c4:4 * g + c4 + 1], rsn[:, 0:n], ALU.mult, ALU.mult),
                    reads=rk + ["rsn"], writes=[("ygn", gp, c4, ti)])
        S.add("sync", lambda e: e.dma_start(out=k.ynd[:, 4 * g:4 * g + 4, T0:T0 + TT], in_=yg[:, :, :]),
              reads=[("ygn", gp, c4, ti) for c4 in range(4) for ti in range(nT)] + rk, dkey=("ynb", gp))

    NU = 2 * NG
    GA(0)
    UA(0)
    UA2(0)
    UB(0)
    for u in range(NU):
        if u + 1 < NU:
            if (u + 1) % 2 == 0:
                GA((u + 1) // 2)
            UA(u + 1)
        UC(u)
        if u + 1 < NU:
            UA2(u + 1)
            UB(u + 1)
        if u % 2 == 0 and u >= 2:
            GN((u - 2) // 2)
    GN(NG - 1)

    if maskcol is not None:
        S.add("vector", lambda e: e.tensor_single_scalar(ST[:], ST[:], k.pm[:, maskcol:maskcol + 1], ALU.mult),
              reads=[("ST", u) for u in range(16)], writes=[("ST", u) for u in range(16)])
    if not last:
        S.add("sync", lambda e: e.dma_start(out=k.sts[:, :].rearrange("p (h q) -> p h q", h=NH), in_=ST[:]),
              reads=[("ST", u) for u in range(16)], dkey="stsave")
    else:
        for (src, sname, dst) in ((k.ssd_tail, "ssd_tail", k.dout["o_p_ssdconv"]), (sst, "sst", k.dout["o_s_ssdconv"])):
            if "ssd_ot" not in k.cache:
                k.cache["ssd_ot"] = sb(k, "ssd_ot", [8, 1536], F32)
            ot = k.cache["ssd_ot"]
            for c0 in range(0, 48, 4):
                b = mb()
                for j in range(4):
                    c = c0 + j
                    S.add("tensor", lambda e, b=b, j=j, c=c, src=src: e.transpose(bank(k, b)[0:3, j * 128:(j + 1) * 128], src[:, c, :], k.ident[:, :]),
                          reads=[(sname, c)], writes=[("ps", b)])
                S.add("scalar", lambda e, b=b, c0=c0, ot=ot: e.copy(ot[0:3, (c0 % 12) * 128:(c0 % 12 + 4) * 128], bank(k, b)[0:3, :]),
                      reads=[("ps", b)], writes=["ssd_ot"])
                if c0 % 12 == 8:
                    S.add("sync", lambda e, ot=ot, dst=dst, c0=c0: e.dma_start(out=dst[:, (c0 - 8) * 128:(c0 - 8) * 128 + 1536], in_=ot[0:3, :]),
                          reads=["ssd_ot"], dkey="ssd_ot")


def merge_phase(k, S, TT):
    hT = k.hT
    yn = sb(k, "mg_yn", [128, 32, TT], BF16)
    for q in range(4):
        S.add("sync", lambda e, q=q: e.dma_start(out=yn[:, 8 * q:8 * q + 8, :], in_=k.ynd[:, 8 * q:8 * q + 8, 0:TT]),
              writes=[("yn", kc, ti) for kc in range(8 * q, 8 * q + 8) for ti in allt(TT)], dkey=("ynl", q))
    sgt = [sb(k, "mg_sg%d" % i, [128, 512], F32) for i in range(2)]
    bt = [sb(k, "mg_bt%d" % i, [128, 512], F32) for i in range(2)]
    mt = [sb(k, "mg_mt%d" % i, [128, 512], BF16) for i in range(2)]

    def epi(i, ti, o, n, bks):
        bo, bg = bks
        ss = rot(k, "mg_sg", 2)
        S.add("sync", lambda e, ss=ss, i=i, o=o, n=n: e.dma_start(out=bt[ss][:, 0:n], in_=k.Bs[:, i, o:o + n]),
              writes=[("mg_bt", ss)], dkey=("mg_bt", ss))
        S.add("scalar", lambda e, ss=ss, bg=bg, n=n: e.activation(out=sgt[ss][:, 0:n], in_=bank(k, bg, n), func=AF.Sigmoid),
              reads=[("ps", bg)], writes=[("mg_sg", ss)])
        S.add("vector", lambda e, ss=ss, bo=bo, n=n: e.tensor_tensor(sgt[ss][:, 0:n], bank(k, bo, n), sgt[ss][:, 0:n], ALU.mult),
              reads=[("ps", bo), ("mg_sg", ss)], writes=[("mg_sg", ss)])
        ms = rot(k, "mg_mt", 2)
        S.add("vector", lambda e, ss=ss, ms=ms, n=n: e.tensor_tensor(mt[ms][:, 0:n], sgt[ss][:, 0:n], bt[ss][:, 0:n], ALU.add),
              reads=[("mg_sg", ss), ("mg_bt", ss)], writes=[("mg_mt", ms)])
        S.add("sync", lambda e, ms=ms, i=i, o=o, n=n: e.dma_start(out=k.mts[:, i, o:o + n], in_=mt[ms][:, 0:n]),
              reads=[("mg_mt", ms)], dkey=("mg_mt", ms))

    multilinear(k, S, [(yn, "yn", 32, k.din["ssd_w_out"], 0), (hT, "hT", DC, k.din["w_in"], O_GS)], DC, TT, epi, tag="mg")


def attn_phase(k, S, TT, last, xT):
    hT = k.hT
    tl = tiles_of(TT)
    aT = sb(k, "at_aT", [128, DC, TT], BF16)
    S.add("sync", lambda e: e.dma_start(out=xT[:], in_=k.x1s[:, :, 0:TT]), writes=[("xT", c, ti) for c in range(DC) for ti in allt(TT)], dkey="x1l")
    S.add("sync", lambda e: e.dma_start(out=aT[:], in_=k.mts[:, :, 0:TT]), writes=[("aT", c, ti) for c in range(DC) for ti in allt(TT)], dkey="mtl")

    def epi_add(i, ti, o, n, bks):
        b = bks[0]
        S.add("vector", lambda e, b=b, i=i, o=o, n=n: e.tensor_tensor(xT[:, i, o:o + n], xT[:, i, o:o + n], bank(k, b, n), ALU.add),
              reads=[("ps", b), ("xT", i, ti)], writes=[("xT", i, ti)])
    multilinear(k, S, [(aT, "aT", DC, k.din["w_mix_out"], 0)], DC, TT, epi_add, tag="lin")
    rmsnorm(k, S, xT, hT, k.gv["xa_norm"], TT)

    def epi_q(i, ti, o, n, bks):
        b = bks[0]
        S.add("scalar", lambda e, b=b, i=i, o=o, n=n: e.copy(aT[:, i, o:o + n], bank(k, b, n)),
              reads=[("ps", b)], writes=[("aT", i, ti)])
    multilinear(k, S, [(hT, "hT", DC, k.din["xa_wq"], 0)], DC, TT, epi_q, tag="lin")

    kT1 = sb(k, "at_kT", [128, DC, NMEM], BF16)
    V1 = sb(k, "at_V", [128, 2, D], BF16)
    kT = [kT1, kT1]
    V = [V1, V1]
    S.add("sync", lambda e: e.dma_start(out=kT1[:], in_=k.kTs[0][:, :, :]), writes=["kT"], dkey="kTl")
    S.add("sync", lambda e: e.dma_start(out=V1[:], in_=k.Vs[0][:, :, :]), writes=["V"], dkey="Vl")
    PT = sb(k, "at_PT", [128, 2, 4, 512], BF16)
    pe = [sb(k, "at_pe%d" % i, [128, 2, NMEM], BF16) for i in range(2)]
    mx = [sb(k, "at_mx%d" % i, [128, 4], F32) for i in range(2)]
    rs = [sb(k, "at_rs%d" % i, [128, 4], F32) for i in range(2)]
    scale = float(512 ** -0.5)
    for ti, (o, n) in enumerate(tl):
        kv = 1 if (last and o >= SL) else 0
        if kv == 1:
            S.add("sync", lambda e: e.dma_start(out=kT1[:], in_=k.kTs[1][:, :, :]), writes=["kT"], dkey="kTl")
            S.add("sync", lambda e: e.dma_start(out=V1[:], in_=k.Vs[1][:, :, :]), writes=["V"], dkey="Vl")
        for s0 in range(0, n, 128):
            sn = min(128, n - s0)
            st_ = rot(k, "at_st", 2)
            for hp in range(2):
                b = 4 + rot(k, "at_sb", 2)
                for h2 in range(2):
                    h = 2 * hp + h2
                    for dc in range(4):
                        c = 4 * h + dc
                        S.add("tensor", lambda e, b=b, h2=h2, c=c, o=o, s0=s0, sn=sn, kv=kv, dc=dc: e.matmul(
                            bank(k, b)[0:sn, h2 * 256:(h2 + 1) * 256], aT[:, c, o + s0:o + s0 + sn], kT[kv][:, c, :],
                            start=(dc == 0), stop=(dc == 3)),
                            reads=[("aT", c, ti), "kT"], writes=[("ps", b)])
                S.add("vector", lambda e, b=b, st_=st_, hp=hp, sn=sn: e.tensor_reduce(
                    mx[st_][0:sn, 2 * hp:2 * hp + 2], bank(k, b)[0:sn, :].rearrange("p (h m) -> p h m", h=2), mybir.AxisListType.X, ALU.max),
                    reads=[("ps", b)], writes=[("at_mx", st_, hp)])
                S.add("vector", lambda e, st_=st_, hp=hp, sn=sn: e.tensor_single_scalar(
                    mx[st_][0:sn, 2 * hp:2 * hp + 2], mx[st_][0:sn, 2 * hp:2 * hp + 2], -scale, ALU.mult),
                    reads=[("at_mx", st_, hp)], writes=[("at_mx", st_, hp)])
                ps_ = rot(k, "at_pe", 2)
                for h2 in range(2):
                    S.add("scalar", lambda e, b=b, ps_=ps_, h2=h2, st_=st_, hp=hp, sn=sn: e.activation(
                        out=pe[ps_][0:sn, h2, :], in_=bank(k, b)[0:sn, h2 * 256:(h2 + 1) * 256], func=AF.Exp,
                        bias=mx[st_][0:sn, 2 * hp + h2:2 * hp + h2 + 1], scale=scale,
                        accum_out=rs[st_][0:sn, 2 * hp + h2:2 * hp + h2 + 1]),
                        reads=[("ps", b), ("at_mx", st_, hp)], writes=[("at_pe", ps_), ("at_rs", st_, hp)])
                S.add("vector", lambda e, st_=st_, hp=hp, sn=sn: e.reciprocal(rs[st_][0:sn, 2 * hp:2 * hp + 2], rs[st_][0:sn, 2 * hp:2 * hp + 2]),
                      reads=[("at_rs", st_, hp)], writes=[("at_rs", st_, hp)])
                for h2 in range(2):
                    S.add("vector", lambda e, ps_=ps_, h2=h2, st_=st_, hp=hp, sn=sn: e.tensor_single_scalar(
                        pe[ps_][0:sn, h2, :], pe[ps_][0:sn, h2, :], rs[st_][0:sn, 2 * hp + h2:2 * hp + h2 + 1], ALU.mult),
                        reads=[("at_pe", ps_), ("at_rs", st_, hp)], writes=[("at_pe", ps_)])
                bt_ = 6 + rot(k, "misc", 2)
                for h2 in range(2):
                    for mc in range(2):
                        S.add("tensor", lambda e, bt_=bt_, ps_=ps_, h2=h2, mc=mc, sn=sn: e.transpose(
                            bank(k, bt_, dt=BF16)[:, (h2 * 2 + mc) * 128:(h2 * 2 + mc) * 128 + sn], pe[ps_][0:sn, h2, mc * 128:(mc + 1) * 128],
                            k.identb[0:sn, 0:sn]),
                            reads=[("at_pe", ps_)], writes=[("ps", bt_)])
                for h2 in range(2):
                    S.add("scalar", lambda e, bt_=bt_, h2=h2, hp=hp, s0=s0, sn=sn: e.copy(
                        PT[:, :, 2 * hp + h2, s0:s0 + sn], bank(k, bt_, dt=BF16)[:, h2 * 256:(h2 + 1) * 256].rearrange("p (mc t) -> p mc t", mc=2)[:, :, 0:sn]),
                        reads=[("ps", bt_)], writes=[("PT", 2 * hp + h2)])
        for c in range(DC):
            h = c // 4
            b = rot(k, "at_ob", 4)
            for mc in range(2):
                S.add("tensor", lambda e, b=b, c=c, mc=mc, h=h, n=n, kv=kv: e.matmul(
                    bank(k, b, n), V[kv][:, mc, c * 128:(c + 1) * 128], PT[:, mc, h, 0:n], start=(mc == 0), stop=(mc == 1)),
                    reads=["V", ("PT", h)], writes=[("ps", b)])
            S.add("scalar", lambda e, b=b, c=c, o=o, n=n: e.copy(hT[:, c, o:o + n], bank(k, b, n)),
                  reads=[("ps", b)], writes=[("hT", c, ti)])
    multilinear(k, S, [(hT, "hT", DC, k.din["xa_wo"], 0)], DC, TT, epi_add, tag="lin")


def final_out(k, S, TT, last, xT, sl):
    sq = k.cache["nsq"]
    rstd = k.cache["nrs"][0]
    yt = [sb(k, "fn_yt%d" % i, [128, 128], F32) for i in range(2)]
    yo = [sb(k, "fn_yo%d" % i, [128, D], F32) for i in range(1)]
    for ti, (o, n) in enumerate(tiles_of(TT)):
        b = 7
        for c in range(DC):
            s = rot(k, "nsq", 2)
            S.add("scalar", lambda e, s=s, c=c, o=o, n=n: e.activation(out=sq[s][:, 0:n], in_=xT[:, c, o:o + n], func=AF.Square),
                  reads=[("xT", c, ti)], writes=[("nsq", s)])
            S.add("tensor", lambda e, s=s, c=c, n=n: e.matmul(bank(k, b, n), k.onesD[:, :], sq[s][:, 0:n], start=(c == 0), stop=(c == DC - 1)),
                  reads=[("nsq", s)], writes=[("ps", b)])
        S.add("scalar", lambda e, n=n: e.activation(out=rstd[:, 0:n], in_=bank(k, b, n), func=AF.Sqrt, bias=k.eps[:, 0:1]),
              reads=[("ps", b)], writes=[("nrs", 0)])
        S.add("vector", lambda e, n=n: e.reciprocal(rstd[:, 0:n], rstd[:, 0:n]), reads=[("nrs", 0)], writes=[("nrs", 0)])
        for s0 in range(0, n, 128):
            sn = min(128, n - s0)
            ys = rot(k, "fn_yo", 1)
            for c0 in range(0, DC, 4):
                bt_ = 4 + rot(k, "fn_tb", 2)
                for j in range(4):
                    c = c0 + j
                    t = rot(k, "fn_yt", 2)
                    S.add("vector", lambda e, t=t, c=c, o=o, s0=s0, sn=sn: e.scalar_tensor_tensor(
                        yt[t][:, 0:sn], xT[:, c, o + s0:o + s0 + sn], k.gv["final_norm"][:, c:c + 1], rstd[:, s0:s0 + sn], ALU.mult, ALU.mult),
                        reads=[("xT", c, ti), ("nrs", 0)], writes=[("fn_yt", t)])
                    S.add("tensor", lambda e, bt_=bt_, j=j, t=t, sn=sn: e.transpose(bank(k, bt_)[0:sn, j * 128:(j + 1) * 128], yt[t][:, 0:sn], k.ident[:, :]),
                          reads=[("fn_yt", t)], writes=[("ps", bt_)])
                S.add("scalar", lambda e, bt_=bt_, ys=ys, c0=c0, sn=sn: e.copy(yo[ys][0:sn, c0 * 128:(c0 + 4) * 128], bank(k, bt_)[0:sn, :]),
                      reads=[("ps", bt_)], writes=[("fn_yo", ys)])
            tok = o + s0
            if tok < SL:
                dst = k.dout["y_p"][sl * SL + tok:sl * SL + tok + sn, :]
            else:
                dst = k.dout["y_s"][tok - SL:tok - SL + sn, :]
            S.add("sync", lambda e, ys=ys, dst=dst, sn=sn: e.dma_start(out=dst, in_=yo[ys][0:sn, :]), reads=[("fn_yo", ys)], dkey=("fn_yo", ys))


def kv_phase(k, S):
    mT = sb(k, "kv_mT", [128, DC, NMEM], F32)
    mh = sb(k, "kv_mh", [128, DC, NMEM], BF16)
    load_xT(k, S, mT, [(k.din["mem"], 0, NMEM)])
    rmsnorm(k, S, mT, mh, k.gv["mem_norm"], NMEM, xname="xT", hname="mh")
    import os
    KVS = int(os.environ.get("KV_STOP", "9"))
    if KVS <= 1:
        return
    wst = [sb(k, "kv_w%d" % i, [128, DC, 512], BF16) for i in range(2)]
    of = [sb(k, "kv_of%d" % i, [128, 512], F32) for i in range(2)]
    ktok = [sb(k, "kv_kt%d" % i, [128, 2, D], BF16) for i in range(2)]
    kTb = sb(k, "kv_kT", [128, DC, NMEM], BF16)
    cin = [sb(k, "kv_cin%d" % i, [128, D], F32) for i in range(2)]
    for which, (wname, oname) in enumerate((("xa_wk", "o_p_k"), ("xa_wv", "o_p_v"))):
        Wv = k.din[wname].rearrange("(kc p) n -> p kc n", p=128)
        for nt in range(4):
            ws = rot(k, "kv_w", 2)
            for hf in range(2):
                S.add("gpsimd", lambda e, ws=ws, Wv=Wv, nt=nt, hf=hf: e.dma_start(
                    out=wst[ws][:, :, hf * 256:(hf + 1) * 256], in_=Wv[:, :, nt * 512 + hf * 256:nt * 512 + (hf + 1) * 256]),
                    writes=[("kv_w", ws)], dkey=("kv_w", ws, hf))
            for mc in range(2):
                b = rot(k, "kv_b", 4)
                for kc in range(DC):
                    S.add("tensor", lambda e, b=b, ws=ws, kc=kc, mc=mc: e.matmul(bank(k, b), mh[:, kc, mc * 128:(mc + 1) * 128], wst[ws][:, kc, :],
                                                                          start=(kc == 0), stop=(kc == DC - 1)),
                          reads=[("kv_w", ws), ("mh", kc, 0)], writes=[("ps", b)])
                os_ = rot(k, "kv_of", 2)
                S.add("vector", lambda e, b=b, os_=os_: e.tensor_copy(of[os_][:, :], bank(k, b)), reads=[("ps", b)], writes=[("kv_of", os_)])
                S.add("vector", lambda e, b=b, which=which, mc=mc, nt=nt: e.tensor_copy(ktok[which][:, mc, nt * 512:(nt + 1) * 512], bank(k, b)),
                      reads=[("ps", b)], writes=[("kv_kt", which)])
                S.add("sync", lambda e, os_=os_, oname=oname, mc=mc, nt=nt: e.dma_start(
                    out=k.dout[oname][mc * 128:(mc + 1) * 128, nt * 512:(nt + 1) * 512], in_=of[os_][:, :]),
                    reads=[("kv_of", os_)], dkey=("kv_of", os_))

    if KVS <= 2:
        return

    def make_kT(src, sname, dst_kT, dst_V, vsrc, vname):
        for mc in range(2):
            for c0 in range(0, DC, 8):
                b = 4 + rot(k, "tpb", 4)
                for j in range(8):
                    c = c0 + j
                    S.add("tensor", lambda e, b=b, j=j, c=c, mc=mc, src=src: e.transpose(
                        bank(k, b, dt=BF16)[:, j * 128:(j + 1) * 128], src[:, mc, c * 128:(c + 1) * 128], k.identb[:, :]),
                        reads=[sname], writes=[("ps", b)])
                S.add("scalar", lambda e, b=b, c0=c0, mc=mc: e.copy(
                    kTb[:, c0:c0 + 8, mc * 128:(mc + 1) * 128], bank(k, b, dt=BF16).rearrange("p (j t) -> p j t", j=8)),
                    reads=[("ps", b)], writes=["kTb"])
        S.add("sync", lambda e: e.dma_start(out=dst_kT[:, :, :], in_=kTb[:]), reads=["kTb"], dkey=("kTst", sname if isinstance(sname, str) else str(sname)))
        S.add("sync", lambda e: e.dma_start(out=dst_V[:, :, :], in_=vsrc[:]), reads=[vname], dkey=("Vst", str(vname)))

    make_kT(ktok[0], ("kv_kt", 0), k.kTs[0], k.Vs[0], ktok[1], ("kv_kt", 1))
    if KVS <= 3:
        return
    ck = [sb(k, "kv_ck%d" % i, [128, 2, D], BF16) for i in range(2)]
    for which, nm in enumerate(("c_k", "c_v")):
        for mc in range(2):
            cs = rot(k, "kv_cin", 2)
            S.add("sync", lambda e, cs=cs, nm=nm, mc=mc: e.dma_start(out=cin[cs][:, :], in_=k.din[nm][mc * 128:(mc + 1) * 128, :]),
                  writes=[("kv_cin", cs)], dkey=("kv_cin", cs))
            S.add("vector", lambda e, cs=cs, which=which, mc=mc: e.tensor_copy(ck[which][:, mc, :], cin[cs][:, :]),
                  reads=[("kv_cin", cs)], writes=[("kv_ck", which)])
    make_kT(ck[0], ("kv_ck", 0), k.kTs[1], k.Vs[1], ck[1], ("kv_ck", 1))


def dump(k, name, src):
    d = k.nc.dram_tensor("dbg_" + name, list(src.shape), src.dtype, kind="ExternalOutput").ap()
    with phase(k) as S:
        S.add("sync", lambda e: e.dma_start(out=d, in_=src), dkey="dump")


def build(nsl=NSL, dbg=None, stop=None):
    nc = bass.Bass("TRN2", target_bir_lowering=False)
    k = K()
    k.nc = nc
    k.din, k.dout = {}, {}

    def din(name, shape):
        k.din[name] = nc.dram_tensor(name, list(shape), F32, kind="ExternalInput").ap()

    def dout(name, shape):
        k.dout[name] = nc.dram_tensor(name, list(shape), F32, kind="ExternalOutput").ap()

    for nm, shp in IN_SHAPES.items():
        din(nm, shp)
    for nm, shp in OUT_SHAPES.items():
        dout(nm, shp)
    TM = SL + DEC
    k.x1s = nc.dram_tensor("x1s", [128, DC, TM], F32).ap()
    k.Bs = nc.dram_tensor("Bs", [128, DC, TM], F32).ap()
    k.mts = nc.dram_tensor("mts", [128, DC, TM], BF16).ap()
    k.ynd = nc.dram_tensor("ynd", [128, 32, TM], BF16).ap()
    k.lns = nc.dram_tensor("lns", [128, DC, TM], BF16).ap()
    k.acsd = nc.dram_tensor("acsd", [64, TM], F32).ap()
    k.sts = nc.dram_tensor("sts", [128, NH * HP], F32).ap()
    k.kTs = [nc.dram_tensor("kTs%d" % i, [128, DC, NMEM], BF16).ap() for i in range(2)]
    k.Vs = [nc.dram_tensor("Vs%d" % i, [128, 2, D], BF16).ap() for i in range(2)]

    k.ps = nc.alloc_psum_tensor("ps", [128, 8 * 512], F32).ap()
    cshapes = {"ident": [128, 128], "onesD": [128, 128], "ones512": [128, 128], "tri": [128, 128], "ones64": [64, 128]}
    for nm, shp in cshapes.items():
        setattr(k, nm, sb(k, "c_" + nm, shp, F32))
    k.identb = sb(k, "c_identb", [128, 128], BF16)
    k.eps = sb(k, "c_eps", [128, 1], F32)
    k.gv = {nm: sb(k, "g_" + nm, [128, DC], F32) for nm in NORMS}
    k.cm_w = sb(k, "c_cm_w", [128, DC, CMW], F32)
    k.cm_b = sb(k, "c_cm_b", [128, DC], F32)
    k.cm_g = sb(k, "c_cm_g", [128, DC], F32)
    k.cm_lb = sb(k, "c_cm_lb", [128, DC], F32)
    k.sc_w = sb(k, "c_sc_w", [128, 48, 4], F32)
    k.sc_b = sb(k, "c_sc_b", [128, 48], F32)
    k.ssdn = sb(k, "c_ssdn", [128, 32], F32)
    k.dexp = sb(k, "c_dexp", [128, 32], F32)
    k.dtb = sb(k, "c_dtb", [64, 1], F32)
    k.acol = sb(k, "c_acol", [64, 8], F32)
    k.pm = sb(k, "c_pm", [128, 4], F32)
    k.cm_tail = sb(k, "p_cm_tail", [128, DC, 30], F32)
    k.ssd_tail = sb(k, "p_ssd_tail", [128, 48, 3], F32)
    k.hT = sb(k, "p_hT", [128, DC, TM], BF16)

    with phase(k) as S:
        i = 0
        for nm in list(cshapes) + ["cm_w", "cm_b", "cm_g", "cm_lb", "sc_w", "sc_b", "ssdn", "dexp", "dtb", "acol", "pm"]:
            t = getattr(k, nm)
            src = k.din["v_" + nm] if nm not in cshapes else k.din["k_" + nm]
            S.add("sync", lambda e, t=t, src=src: e.dma_start(out=t[:], in_=src), writes=[nm], dkey="c%d" % i)
            i += 1
        for nm in NORMS:
            S.add("sync", lambda e, nm=nm: e.dma_start(out=k.gv[nm][:], in_=k.din["v_" + nm][:, :]), writes=["g_" + nm], dkey="c%d" % i)
            i += 1
        S.add("vector", lambda e: e.memset(k.eps[:], EPS), writes=["eps"])
        S.add("vector", lambda e: e.tensor_copy(k.identb[:], k.ident[:]), reads=["ident"], writes=["identb"])
        S.add("vector", lambda e: e.memset(k.cm_tail[:], 0.0), writes=["cmt"])
        S.add("vector", lambda e: e.memset(k.ssd_tail[:], 0.0), writes=["sst"])
        S.add("scalar", lambda e: e.activation(out=k.acol[:], in_=k.acol[:], func=AF.Exp), reads=["acol"], writes=["acol"])
        S.add("vector", lambda e: e.tensor_single_scalar(k.acol[:], k.acol[:], -1.0, ALU.mult), reads=["acol"], writes=["acol"])
    if stop == "consts":
        return nc
    with phase(k) as S:
        kv_phase(k, S)
    if stop == "kv":
        return nc

    NPRE = nsl - 1
    for j in range(NPRE):
        with phase(k) as S:
            xT = sb(k, "xT", [128, DC, SL], F32)
            load_xT(k, S, xT, [(k.din["xp"][j * SL:(j + 1) * SL, :], 0, SL)])
            rmsnorm(k, S, xT, k.hT, k.gv["ffn1_norm"], SL)
            ffn(k, S, xT, k.hT, k.din["ffn1_wi"], k.din["ffn1_wo"], SL)
            rmsnorm(k, S, xT, k.hT, k.gv["mix_norm"], SL)
        lastpre = (j == NPRE - 1)
        for sub in range(2):
            with phase(k) as S:
                ssd_branch(k, S, sub * 512, 512, False, j == 0 and sub == 0,
                           mode=("state_tail" if lastpre else "state"), maskcol=(j if sub == 1 else None))
        if lastpre:
            with phase(k) as S:
                cm_tail_only(k, S)
    if stop == "prefix":
        dump(k, "sts", k.sts)
        return nc
    sl = NPRE
    if True:
        last = True
        first = (NPRE == 0)
        TT = TM
        srcs = [(k.din["xp"][sl * SL:(sl + 1) * SL, :], 0, SL), (k.din["xs"], SL, DEC)]
        with phase(k) as S:
            xT = sb(k, "xT", [128, DC, TT], F32)
            load_xT(k, S, xT, srcs)
            rmsnorm(k, S, xT, k.hT, k.gv["ffn1_norm"], TT)
            ffn(k, S, xT, k.hT, k.din["ffn1_wi"], k.din["ffn1_wo"], TT)
            rmsnorm(k, S, xT, k.hT, k.gv["mix_norm"], TT)
            S.add("sync", lambda e: e.dma_start(out=k.x1s[:, :, 0:TT], in_=xT[:]),
                  reads=[("xT", c, ti) for c in range(DC) for ti in allt(TT)], dkey="x1s")
        if stop == "ffn1":
            dump(k, "x1s", k.x1s)
            return nc
        with phase(k) as S:
            cm_branch(k, S, TT, last)
        with phase(k) as S:
            cm_branch_b(k, S, TT)
        if stop == "cm":
            dump(k, "Bs", k.Bs)
            dump(k, "lns", k.lns)
            return nc
        for sub in range(2):
            with phase(k) as S:
                ssd_branch(k, S, sub * 512, 512, last and sub == 1, first and sub == 0)
        if stop == "ssd":
            dump(k, "ynd", k.ynd)
            return nc
        with phase(k) as S:
            merge_phase(k, S, TT)
        if stop == "merge":
            dump(k, "mts", k.mts)
            return nc
        with phase(k) as S:
            xT = sb(k, "xT", [128, DC, TT], F32)
            attn_phase(k, S, TT, last, xT)
            S.add("sync", lambda e: e.dma_start(out=k.x1s[:, :, 0:TT], in_=xT[:]),
                  reads=[("xT", c, ti) for c in range(DC) for ti in allt(TT)], dkey="x3s")
        if stop == "attn":
            dump(k, "x3s", k.x1s)
            return nc
        with phase(k) as S:
            xT = sb(k, "xT", [128, DC, TT], F32)
            S.add("sync", lambda e: e.dma_start(out=xT[:], in_=k.x1s[:, :, 0:TT]),
                  writes=[("xT", c, ti) for c in range(DC) for ti in allt(TT)], dkey="x3l")
            rmsnorm(k, S, xT, k.hT, k.gv["ffn2_norm"], TT)
            ffn(k, S, xT, k.hT, k.din["ffn2_wi"], k.din["ffn2_wo"], TT)
            final_out(k, S, TT, last, xT, 0)
    return nc


NORMS = ["ffn1_norm", "mix_norm", "xa_norm", "mem_norm", "ffn2_norm", "final_norm"]
IN_SHAPES = {
    "xp": [SEQ, D], "xs": [DEC, D], "mem": [NMEM, D], "c_ssdconv": [3, XBC], "c_ssm": [NH * HP, NS], "c_cmconv": [30, D],
    "c_k": [NMEM, D], "c_v": [NMEM, D],
    "ffn1_wi": [D, 2 * DFF], "ffn1_wo": [DFF, D], "w_in": [D, INC], "ssd_w_out": [DIN, D], "cm_w_out": [D, D], "w_mix_out": [D, D],
    "xa_wq": [D, D], "xa_wk": [D, D], "xa_wv": [D, D], "xa_wo": [D, D], "ffn2_wi": [D, 2 * DFF], "ffn2_wo": [DFF, D],
    "k_ident": [128, 128], "k_onesD": [128, 128], "k_ones512": [128, 128], "k_tri": [128, 128], "k_ones64": [64, 128],
    "v_cm_w": [128, DC, CMW], "v_cm_b": [128, DC], "v_cm_g": [128, DC], "v_cm_lb": [128, DC], "v_sc_w": [128, 48, 4], "v_sc_b": [128, 48],
    "v_ssdn": [128, 32], "v_dexp": [128, 32], "v_dtb": [64, 1], "v_acol": [64, 8], "v_pm": [128, 4],
}
for _n in NORMS:
    IN_SHAPES["v_" + _n] = [128, DC]
OUT_SHAPES = {
    "y_p": [SL, D], "y_s": [DEC, D], "o_p_ssdconv": [3, XBC], "o_p_ssm": [NH * HP, NS], "o_p_cm": [30, D], "o_p_k": [NMEM, D], "o_p_v": [NMEM, D],
    "o_s_ssdconv": [3, XBC], "o_s_ssm": [NH * HP, NS], "o_s_cm": [30, D],
}


def col_layout(v):
    return np.ascontiguousarray(np.asarray(v, np.float32).reshape(-1, 128).T)


def shared_inputs(inp):
    m = {}
    for nm in ["ffn1_wi", "ffn1_wo", "w_in", "ssd_w_out", "cm_w_out", "w_mix_out", "xa_wq", "xa_wk", "xa_wv", "xa_wo", "ffn2_wi", "ffn2_wo"]:
        m[nm] = np.ascontiguousarray(inp[nm][0], dtype=np.float32)
    for nm in NORMS:
        v = inp[nm] if nm == "final_norm" else inp[nm][0]
        m["v_" + nm] = col_layout(v)
    m["k_ident"] = np.eye(128, dtype=np.float32)
    m["k_onesD"] = np.full((128, 128), 1.0 / D, np.float32)
    m["k_ones512"] = np.full((128, 128), 1.0 / 512, np.float32)
    m["k_tri"] = np.triu(np.ones((128, 128), np.float32))
    m["k_ones64"] = np.ones((64, 128), np.float32)
    m["v_cm_w"] = np.ascontiguousarray(inp["cm_dw_w"][0].reshape(CMW, DC, 128).transpose(2, 1, 0))
    m["v_cm_b"] = col_layout(inp["cm_dw_b"][0])
    m["v_cm_g"] = col_layout(inp["cm_ln_g"][0])
    m["v_cm_lb"] = col_layout(inp["cm_ln_b"][0])
    m["v_sc_w"] = np.ascontiguousarray(inp["ssd_conv_w"][0].reshape(4, 48, 128).transpose(2, 1, 0))
    m["v_sc_b"] = col_layout(inp["ssd_conv_b"][0])
    m["v_ssdn"] = col_layout(inp["ssd_norm"][0])
    m["v_dexp"] = col_layout(np.repeat(inp["ssd_d"][0], HP))
    m["v_dtb"] = np.ascontiguousarray(inp["ssd_dt_bias"][0].reshape(64, 1))
    m["v_acol"] = np.ascontiguousarray(np.repeat(inp["ssd_a_log"][0].reshape(64, 1), 8, axis=1))
    return m


def core_inputs(inp, shared, core):
    m = dict(shared)
    b, q = core // 4, core % 4
    xp = np.zeros((SEQ, D), np.float32)
    xp[(3 - q) * SL:, :] = inp["x_prompt"][b, :(q + 1) * SL, :]
    m["xp"] = xp
    pm = np.zeros((128, 4), np.float32)
    for j in range(3):
        pm[:, j] = 1.0 if j >= 3 - q else 0.0
    m["v_pm"] = pm
    m["mem"] = np.ascontiguousarray(inp["mem_prompt"][b])
    m["xs"] = np.ascontiguousarray(inp["x_sample"][core])
    m["c_ssdconv"] = np.ascontiguousarray(inp["cache_ssd_conv"][0, core])
    m["c_ssm"] = np.ascontiguousarray(inp["cache_ssm_state"][0, core].reshape(NH * HP, NS))
    m["c_cmconv"] = np.ascontiguousarray(inp["cache_cm_conv"][0, core])
    m["c_k"] = np.ascontiguousarray(inp["cache_mem_k"][0, core].reshape(NMEM, D))
    m["c_v"] = np.ascontiguousarray(inp["cache_mem_v"][0, core].reshape(NMEM, D))
    return m


_NC = None


def kernel(**inputs):
    global _NC
    inp = {kk: np.asarray(v) for kk, v in inputs.items()}
    if _NC is None:
        _NC = build()
    shared = shared_inputs(inp)
    in_maps = [core_inputs(inp, shared, c) for c in range(8)]
    res = run_bass_kernel_spmd(_NC, in_maps, core_ids=list(range(8)))
    r = res.results
    f = np.float32
    y_prompt = np.stack([np.concatenate([r[4 * b + q]["y_p"] for q in range(4)], axis=0) for b in range(2)]).astype(f)
    y_sample = np.stack([r[c]["y_s"] for c in range(8)]).astype(f)
    p_ssd_conv = np.stack([r[c]["o_p_ssdconv"] for c in (3, 7)])[None].astype(f)
    p_ssm = np.stack([r[c]["o_p_ssm"].reshape(NH, HP, NS) for c in (3, 7)])[None].astype(f)
    p_cm = np.stack([r[c]["o_p_cm"] for c in (3, 7)])[None].astype(f)
    p_k = np.stack([r[c]["o_p_k"].reshape(NMEM, 4, 512) for c in (3, 7)])[None].astype(f)
    p_v = np.stack([r[c]["o_p_v"].reshape(NMEM, 4, 512) for c in (3, 7)])[None].astype(f)
    s_ssd_conv = np.stack([r[c]["o_s_ssdconv"] for c in range(8)])[None].astype(f)
    s_ssm = np.stack([r[c]["o_s_ssm"].reshape(NH, HP, NS) for c in range(8)])[None].astype(f)
    s_cm = np.stack([r[c]["o_s_cm"] for c in range(8)])[None].astype(f)
    return (y_prompt, y_sample, p_ssd_conv, p_ssm, p_cm, p_k, p_v, s_ssd_conv, s_ssm, s_cm)
```
